# Optimizing a Trainium2 kernel written in Bass

```python
import math
import jax, jax.numpy as jnp
from jax import lax
import numpy as np

D_MODEL = 1024
BATCH = 4
SEQ = 4096
DEPTH = 4

SSM_WIDTH = D_MODEL // 2
SSM_GROUP = 16
SSM_GROUPS = SSM_WIDTH // SSM_GROUP
SSM_STATE = 64
HEAD_DIM = 64
NSA_WIDTH = D_MODEL // 2
N_HEADS = NSA_WIDTH // HEAD_DIM
N_KV = 2
Q_PER_KV = N_HEADS // N_KV
KV_WIDTH = N_KV * HEAD_DIM
CMP_LEN = 32
CMP_STRIDE = 16
CMP_HIDDEN = 128
SEL_BLOCK = 64
SEL_TOPK = 16
SEL_FORCED_SCORE = 1e4
WINDOW = 512
Q_BLOCK = 128
D_FF = 2816
ROPE_THETA = 10000.0
NORM_EPS = 1e-6
MASK_VALUE = -1e30
COL_SIZES = (SSM_WIDTH, NSA_WIDTH, KV_WIDTH, KV_WIDTH, KV_WIDTH, KV_WIDTH, KV_WIDTH, KV_WIDTH, 3 * N_HEADS, D_MODEL, D_MODEL)
IN_COLS = sum(COL_SIZES)
SPLIT_POINTS = tuple(int(v) for v in np.cumsum(COL_SIZES)[:-1])

kernel_name = 'hybrid_s5_nsa_macaron'


def rms_norm(x, gain):
    xf = x.astype(jnp.float32)
    y = xf * lax.rsqrt(jnp.mean(xf * xf, axis=-1, keepdims=True) + NORM_EPS)
    return (y * gain.astype(jnp.float32)).astype(x.dtype)


def rope(x, pos):
    half = x.shape[-1] // 2
    inv_freq = 1.0 / (ROPE_THETA ** (jnp.arange(half, dtype=jnp.float32) / half))
    ang = pos.astype(jnp.float32)[:, None] * inv_freq[None, :]
    cos = jnp.cos(ang)[:, None, :].astype(x.dtype)
    sin = jnp.sin(ang)[:, None, :].astype(x.dtype)
    x1, x2 = x[..., :half], x[..., half:]
    return jnp.concatenate([x1 * cos - x2 * sin, x2 * cos + x1 * sin], axis=-1)


def swiglu(h, wi, wo):
    gate, up = jnp.split(h @ wi, 2, axis=-1)
    return (jax.nn.silu(gate) * up) @ wo


def masked_softmax(scores, mask):
    s = jnp.where(mask, scores.astype(jnp.float32), MASK_VALUE)
    p = jax.nn.softmax(s, axis=-1)
    return p * jnp.any(mask, axis=-1, keepdims=True)


def complex_affine_combine(e1, e2):
    a1r, a1i, b1r, b1i = e1
    a2r, a2i, b2r, b2i = e2
    return (a1r * a2r - a1i * a2i,
            a1r * a2i + a1i * a2r,
            a2r * b1r - a2i * b1i + b2r,
            a2r * b1i + a2i * b1r + b2i)


def s5_branch(u, lam_re, lam_im, log_dt, b_re, b_im, c_re, c_im, d_skip, w_glu):
    bsz, seq, _ = u.shape
    u_g = u.reshape(bsz, seq, SSM_GROUPS, SSM_GROUP).astype(jnp.float32)
    dt = jnp.exp(log_dt.astype(jnp.float32))[:, None]
    lr = lam_re.astype(jnp.float32)
    li = lam_im.astype(jnp.float32)
    mag = jnp.exp(lr * dt)
    ang = li * dt
    abar_re, abar_im = mag * jnp.cos(ang), mag * jnp.sin(ang)
    inv_abs2 = 1.0 / (lr * lr + li * li)
    num_re, num_im = abar_re - 1.0, abar_im
    f_re = (num_re * lr + num_im * li) * inv_abs2
    f_im = (num_im * lr - num_re * li) * inv_abs2
    br, bi = b_re.astype(jnp.float32), b_im.astype(jnp.float32)
    bbar_re = f_re[..., None] * br - f_im[..., None] * bi
    bbar_im = f_re[..., None] * bi + f_im[..., None] * br
    bu_re = jnp.einsum('bsgc,gnc->bsgn', u_g, bbar_re)
    bu_im = jnp.einsum('bsgc,gnc->bsgn', u_g, bbar_im)
    a_re = jnp.broadcast_to(abar_re, (1, seq, SSM_GROUPS, SSM_STATE))
    a_im = jnp.broadcast_to(abar_im, (1, seq, SSM_GROUPS, SSM_STATE))
    _, _, st_re, st_im = lax.associative_scan(complex_affine_combine, (a_re, a_im, bu_re, bu_im), axis=1)
    y = (jnp.einsum('bsgn,gcn->bsgc', st_re, c_re.astype(jnp.float32))
         - jnp.einsum('bsgn,gcn->bsgc', st_im, c_im.astype(jnp.float32))
         + d_skip.astype(jnp.float32) * u_g)
    y = jax.nn.gelu(y.reshape(bsz, seq, SSM_WIDTH)).astype(u.dtype)
    val, gate = jnp.split(y @ w_glu, 2, axis=-1)
    return val * jax.nn.sigmoid(gate)


def compress(kv, pe, w1, w2):
    bsz, seq = kv.shape[0], kv.shape[1]
    n_cmp = (seq - CMP_LEN) // CMP_STRIDE + 1
    idx = np.arange(n_cmp)[:, None] * CMP_STRIDE + np.arange(CMP_LEN)[None, :]
    blocks = kv[:, idx] + pe[:, None, :]
    blocks = jnp.moveaxis(blocks, 2, 3).reshape(bsz, n_cmp, N_KV, CMP_LEN * HEAD_DIM)
    return jax.nn.gelu(blocks @ w1) @ w2


def selection_overlap(n_cmp, n_sel):
    cs = np.arange(n_cmp)[:, None] * CMP_STRIDE
    ss = np.arange(n_sel)[None, :] * SEL_BLOCK
    inter = np.clip(np.minimum(cs + CMP_LEN, ss + SEL_BLOCK) - np.maximum(cs, ss), 0, None)
    return jnp.asarray(inter / CMP_LEN, dtype=jnp.float32)


def nsa_branch(q_raw, k_cmp, v_cmp, k_sel, v_sel, k_win, v_win, gate_logits,
               q_norm, k_norm, cmp_pe, cmp_w1, cmp_w2):
    bsz, seq = q_raw.shape[0], q_raw.shape[1]
    pos = jnp.arange(seq, dtype=jnp.int32)
    heads = lambda a: a.reshape(bsz, seq, N_KV, HEAD_DIM)
    q = rope(rms_norm(q_raw.reshape(bsz, seq, N_HEADS, HEAD_DIM), q_norm), pos)
    q = q.reshape(bsz, seq, N_KV, Q_PER_KV, HEAD_DIM)
    n_cmp = (seq - CMP_LEN) // CMP_STRIDE + 1
    cmp_end = jnp.arange(n_cmp, dtype=jnp.int32) * CMP_STRIDE + (CMP_LEN - 1)
    k_c = compress(heads(k_cmp), cmp_pe[0], cmp_w1[0], cmp_w2[0])
    v_c = compress(heads(v_cmp), cmp_pe[1], cmp_w1[1], cmp_w2[1])
    k_c = rope(rms_norm(k_c, k_norm[0]), cmp_end)
    n_sel = seq // SEL_BLOCK
    top_k = min(SEL_TOPK, n_sel)
    overlap = selection_overlap(n_cmp, n_sel)
    to_sel = lambda a: jnp.moveaxis(a.reshape(bsz, n_sel, SEL_BLOCK, N_KV, HEAD_DIM), 3, 1)
    ks_blocks = to_sel(rope(rms_norm(heads(k_sel), k_norm[1]), pos))
    vs_blocks = to_sel(heads(v_sel))
    pad = lambda a: jnp.pad(a, ((0, 0), (WINDOW, 0), (0, 0), (0, 0)))
    kw_pad = pad(rope(rms_norm(heads(k_win), k_norm[2]), pos))
    vw_pad = pad(heads(v_win))
    gates = jax.nn.sigmoid(gate_logits.astype(jnp.float32)).reshape(bsz, seq, N_KV, Q_PER_KV, 3).astype(q.dtype)
    scale = HEAD_DIM ** -0.5
    gather_blocks = jax.vmap(jax.vmap(lambda blocks, ix: blocks[ix]))
    n_qb = seq // Q_BLOCK
    to_qb = lambda a: jnp.moveaxis(a.reshape(bsz, n_qb, Q_BLOCK, *a.shape[2:]), 1, 0)
    sel_j = jnp.arange(n_sel, dtype=jnp.int32)

    def block_fn(args):
        qb, gb, blk = args
        qs = blk * Q_BLOCK
        t = qs + jnp.arange(Q_BLOCK, dtype=jnp.int32)
        s_c = jnp.einsum('bqgrd,bngd->bgrqn', qb, k_c) * scale
        p_c = masked_softmax(s_c, cmp_end[None, :] <= t[:, None])
        o_c = jnp.einsum('bgrqn,bngd->bqgrd', p_c.astype(v_c.dtype), v_c)
        imp = jnp.einsum('bgrqn,nj->bgqj', p_c, overlap)
        cur = t // SEL_BLOCK
        valid = sel_j[None, :] <= cur[:, None]
        forced = (sel_j[None, :] == 0) | (sel_j[None, :] == cur[:, None]) | (sel_j[None, :] == cur[:, None] - 1)
        prio = jnp.where(valid, jnp.where(forced, SEL_FORCED_SCORE, imp), -1.0)
        _, idx = lax.top_k(prio, top_k)
        n_tok = top_k * SEL_BLOCK
        k_g = gather_blocks(ks_blocks, idx).reshape(bsz, N_KV, Q_BLOCK, n_tok, HEAD_DIM)
        v_g = gather_blocks(vs_blocks, idx).reshape(bsz, N_KV, Q_BLOCK, n_tok, HEAD_DIM)
        tok = (idx[..., None] * SEL_BLOCK + jnp.arange(SEL_BLOCK, dtype=jnp.int32)).reshape(bsz, N_KV, Q_BLOCK, n_tok)
        s_s = jnp.einsum('bqgrd,bgqld->bgrql', qb, k_g) * scale
        p_s = masked_softmax(s_s, (tok <= t[None, None, :, None])[:, :, None])
        o_s = jnp.einsum('bgrql,bgqld->bqgrd', p_s.astype(v_g.dtype), v_g)
        k_wb = lax.dynamic_slice_in_dim(kw_pad, qs, WINDOW + Q_BLOCK, axis=1)
        v_wb = lax.dynamic_slice_in_dim(vw_pad, qs, WINDOW + Q_BLOCK, axis=1)
        kpos = qs - WINDOW + jnp.arange(WINDOW + Q_BLOCK, dtype=jnp.int32)
        diff = t[:, None] - kpos[None, :]
        mask_w = (kpos[None, :] >= 0) & (diff >= 0) & (diff < WINDOW)
        s_w = jnp.einsum('bqgrd,bkgd->bgrqk', qb, k_wb) * scale
        p_w = masked_softmax(s_w, mask_w)
        o_w = jnp.einsum('bgrqk,bkgd->bqgrd', p_w.astype(v_wb.dtype), v_wb)
        return gb[..., 0:1] * o_c + gb[..., 1:2] * o_s + gb[..., 2:3] * o_w

    out = lax.map(block_fn, (to_qb(q), to_qb(gates), jnp.arange(n_qb, dtype=jnp.int32)))
    return jnp.moveaxis(out, 0, 1).reshape(bsz, seq, NSA_WIDTH)


def setup_inputs(seed: int = 0) -> dict:
    key = jax.random.key(seed)
    ks = jax.random.split(key, 32)
    nrm = lambda k, shape, s: jax.random.normal(k, shape, jnp.float32) * s
    L, D, G, N, C = DEPTH, D_MODEL, SSM_GROUPS, SSM_STATE, SSM_GROUP
    return {
        'x': nrm(ks[0], (BATCH, SEQ, D), 1.0),
        'norm_ffn1': 1.0 + nrm(ks[1], (L, D), 0.02),
        'ffn1_wi': nrm(ks[2], (L, D, 2 * D_FF), D ** -0.5),
        'ffn1_wo': nrm(ks[3], (L, D_FF, D), D_FF ** -0.5),
        'norm_mix': 1.0 + nrm(ks[4], (L, D), 0.02),
        'w_in': nrm(ks[5], (L, D, IN_COLS), D ** -0.5),
        'ssm_lambda_re': -0.5 + nrm(ks[6], (L, G, N), 0.01),
        'ssm_lambda_im': math.pi * jnp.arange(N, dtype=jnp.float32) + nrm(ks[7], (L, G, N), 0.01),
        'ssm_log_dt': jax.random.uniform(ks[8], (L, G), jnp.float32, math.log(1e-3), math.log(1e-1)),
        'ssm_b_re': nrm(ks[9], (L, G, N, C), (2 * C) ** -0.5),
        'ssm_b_im': nrm(ks[10], (L, G, N, C), (2 * C) ** -0.5),
        'ssm_c_re': nrm(ks[11], (L, G, C, N), N ** -0.5),
        'ssm_c_im': nrm(ks[12], (L, G, C, N), N ** -0.5),
        'ssm_d': nrm(ks[13], (L, G, C), 1.0),
        'ssm_w_glu': nrm(ks[14], (L, SSM_WIDTH, 2 * D), SSM_WIDTH ** -0.5),
        'q_norm': 1.0 + nrm(ks[15], (L, HEAD_DIM), 0.02),
        'k_norm': 1.0 + nrm(ks[16], (L, 3, HEAD_DIM), 0.02),
        'cmp_pe': nrm(ks[17], (L, 2, CMP_LEN, HEAD_DIM), 0.1),
        'cmp_w1': nrm(ks[18], (L, 2, CMP_LEN * HEAD_DIM, CMP_HIDDEN), (CMP_LEN * HEAD_DIM) ** -0.5),
        'cmp_w2': nrm(ks[19], (L, 2, CMP_HIDDEN, HEAD_DIM), CMP_HIDDEN ** -0.5),
        'nsa_w_up': nrm(ks[20], (L, NSA_WIDTH, D), NSA_WIDTH ** -0.5),
        'w_out': nrm(ks[21], (L, D, D), D ** -0.5),
        'norm_ffn2': 1.0 + nrm(ks[22], (L, D), 0.02),
        'ffn2_wi': nrm(ks[23], (L, D, 2 * D_FF), D ** -0.5),
        'ffn2_wo': nrm(ks[24], (L, D_FF, D), D_FF ** -0.5),
    }


def reference(x, norm_ffn1, ffn1_wi, ffn1_wo, norm_mix, w_in,
              ssm_lambda_re, ssm_lambda_im, ssm_log_dt, ssm_b_re, ssm_b_im,
              ssm_c_re, ssm_c_im, ssm_d, ssm_w_glu,
              q_norm, k_norm, cmp_pe, cmp_w1, cmp_w2, nsa_w_up, w_out,
              norm_ffn2, ffn2_wi, ffn2_wo):
    for l in range(DEPTH):
        x = x + 0.5 * swiglu(rms_norm(x, norm_ffn1[l]), ffn1_wi[l], ffn1_wo[l])
        h = rms_norm(x, norm_mix[l])
        (u, q_raw, k_cmp, v_cmp, k_sel, v_sel, k_win, v_win,
         nsa_gate, gate_a, gate_b) = jnp.split(h @ w_in[l], SPLIT_POINTS, axis=-1)
        y_a = s5_branch(u, ssm_lambda_re[l], ssm_lambda_im[l], ssm_log_dt[l],
                        ssm_b_re[l], ssm_b_im[l], ssm_c_re[l], ssm_c_im[l],
                        ssm_d[l], ssm_w_glu[l])
        y_b = nsa_branch(q_raw, k_cmp, v_cmp, k_sel, v_sel, k_win, v_win, nsa_gate,
                         q_norm[l], k_norm[l], cmp_pe[l], cmp_w1[l], cmp_w2[l]) @ nsa_w_up[l]
        merged = jax.nn.sigmoid(gate_a) * y_a + jax.nn.sigmoid(gate_b) * y_b
        x = x + merged @ w_out[l]
        x = x + 0.5 * swiglu(rms_norm(x, norm_ffn2[l]), ffn2_wi[l], ffn2_wo[l])
    return x
```

```python
import numpy as np
from contextlib import ExitStack
import concourse.bass as bass
import concourse.mybir as mybir
from concourse.bass_utils import run_bass_kernel_spmd

F32 = mybir.dt.float32
BF16 = mybir.dt.bfloat16
AF = mybir.ActivationFunctionType
ALU = mybir.AluOpType
AX = mybir.AxisListType

D_MODEL = 1024
SEQ = 4096
BATCH = 4
DEPTH = 4
D_FF = 2816
NFT = D_MODEL // 128
NFF = D_FF // 128
TT = 512
ST = 1024
EPS = 1e-6

ENGS = ("pe", "act", "dve", "pool", "sp")


class Prog:
    EPOCH = 20000
    NDMA = 24

    def __init__(self, nc, es):
        self.nc = nc
        self.es = es
        self.streams = {e: [] for e in ENGS}
        self.cnt = {e: 0 for e in ENGS}
        self.cur = {}
        self.nsem = 0
        self.pe_sems = []
        for e in ENGS:
            self.cur[e] = self._newsem(e)
        self.pe_sems.append(self.cur["pe"])
        self.waited = {e: {} for e in ENGS}
        self.buf = {}
        self.xacc = {}
        self.dsem = [self._newsem("dma%d" % i) for i in range(self.NDMA)]
        self.dcnt = [0] * self.NDMA
        self.dnext = 0
        self.ninst = 0

    def _newsem(self, tag):
        self.nsem += 1
        return self.es.enter_context(self.nc.semaphore("s_%s_%d" % (tag, self.nsem)))

    def _wait(self, eng, ev):
        if ev is None:
            return
        sem, val = ev
        if eng == "pe" and any(sem is s for s in self.pe_sems):
            return
        w = self.waited[eng]
        if w.get(id(sem), 0) >= val:
            return
        w[id(sem)] = val
        self.streams[eng].append(("wait", sem, val))

    def _deps(self, eng, reads, writes):
        for b in reads:
            st = self.buf.get(b)
            if st is not None:
                self._wait(eng, st["w"])
        for b in writes:
            st = self.buf.get(b)
            if st is not None:
                self._wait(eng, st["w"])
                for ev in st["r"].values():
                    self._wait(eng, ev)

    def _mark(self, key, ev, reads, writes):
        for b in reads:
            st = self.buf.setdefault(b, {"w": None, "r": {}})
            st["r"][key] = ev
        for b in writes:
            self.buf[b] = {"w": ev, "r": {}}

    def _excl(self, eng, names):
        out = []
        for b in names:
            if isinstance(b, str) and (b.startswith("bank") or b == "pbank"):
                st = self.xacc.setdefault(b, {})
                for e2, ev in st.items():
                    if e2 != eng:
                        self._wait(eng, ev)
                out.append(st)
        return out

    def op(self, eng, fn, reads=(), writes=()):
        self._deps(eng, reads, writes)
        xs = self._excl(eng, list(reads) + list(writes))
        if self.cnt[eng] >= self.EPOCH:
            self.cur[eng] = self._newsem(eng)
            self.cnt[eng] = 0
            if eng == "pe":
                self.pe_sems.append(self.cur[eng])
        self.cnt[eng] += 1
        sem = self.cur[eng]
        ev = (sem, self.cnt[eng])
        self.streams[eng].append(("op", fn, sem))
        self._mark(eng, ev, reads, writes)
        for st in xs:
            st[eng] = ev
        self.ninst += 1
        return ev

    def dma(self, q, out, in_, reads=(), writes=()):
        self._deps(q, reads, writes)
        i = self.dnext
        self.dnext = (self.dnext + 1) % self.NDMA
        sem = self.dsem[i]
        self._wait(q, (sem, 16 * self.dcnt[i]))
        self.dcnt[i] += 1
        ev = (sem, 16 * self.dcnt[i])
        self.streams[q].append(("dma", out, in_, sem))
        self._mark(("dma", i, self.dcnt[i]), ev, reads, writes)
        self.ninst += 1
        return ev

    def barrier(self):
        evs = [(self.cur[e], self.cnt[e]) for e in ENGS if self.cnt[e] > 0]
        evs += [(self.dsem[i], 16 * self.dcnt[i]) for i in range(self.NDMA) if self.dcnt[i] > 0]
        for e in ENGS:
            for ev in evs:
                self._wait(e, ev)
        self.buf = {}
        self.xacc = {}

    def finish(self, final_events):
        for ev in final_events:
            self._wait("sp", ev)
        nc = self.nc
        block = self.es.enter_context(nc.Block())

        def replay(engobj, stream):
            for it in stream:
                if it[0] == "wait":
                    engobj.wait_ge(it[1], it[2])
                elif it[0] == "op":
                    it[1](engobj).then_inc(it[2], 1)
                else:
                    engobj.dma_start(out=it[1], in_=it[2]).then_inc(it[3], 16)

        @block.tensor
        def _(e):
            replay(e, self.streams["pe"])

        @block.scalar
        def _(e):
            replay(e, self.streams["act"])

        @block.vector
        def _(e):
            replay(e, self.streams["dve"])

        @block.gpsimd
        def _(e):
            replay(e, self.streams["pool"])

        @block.sync
        def _(e):
            replay(e, self.streams["sp"])


NCOL = 3864
DBG = {"ncores": 8, "nqb": 32}
QKV0, QKVW = 512, 1304
GA0, GB0 = 1816, 2840
NEG = -30000.0
OFFC = 248
OFFV = 62
F32N = 13312
BFN = 49152
TWO_PI = float(2 * np.pi)
EVEC = [float(7 - k) for k in range(15)] + [float(k) for k in range(9)] + [float(8 * 2 ** s) for s in range(9)]
NE = len(EVEC)


def host_constants():
    c = {}
    pos = np.arange(SEQ, dtype=np.float32)
    inv_freq = (1.0 / (10000.0 ** (np.arange(32, dtype=np.float32) / 32))).astype(np.float32)
    ang = pos[:, None] * inv_freq[None, :]
    c["cs_tok"] = np.concatenate([np.cos(ang), np.sin(ang)], axis=1).astype(np.float32)
    cend = (np.arange(256) * 16 + 31).astype(np.float32)
    angc = cend[:, None] * inv_freq[None, :]
    c["cs_cmp"] = np.concatenate([np.cos(angc), np.sin(angc)], axis=1).astype(np.float32)
    sel = np.zeros((128, 8, 8, 128), np.float32)
    for g8 in range(8):
        for i in range(8):
            for cin in range(16):
                sel[g8 * 16 + cin, g8, i, i * 16 + cin] = 1.0
    c["sel"] = sel.reshape(128, 64, 128)
    c["selT"] = np.ascontiguousarray(sel.transpose(3, 1, 2, 0)).reshape(128, 64, 128)
    ii = np.arange(128) // 16
    c["tmask"] = (ii[None, :] >= ii[:, None]).astype(np.float32)
    c["identf"] = np.eye(128, dtype=np.float32)
    J = np.zeros((128, 128), np.float32)
    for k in range(128):
        J[k, (k + 64) % 128] = 1.0
    c["jmat"] = J
    sg = np.ones((128, 1), np.float32); sg[64:] = -1.0
    c["sgn"] = sg
    c["evec"] = np.broadcast_to(np.asarray(EVEC, np.float32)[None, :], (128, NE)).copy()
    tq = np.arange(128)
    m = np.arange(512)
    c["maskc"] = np.where(16 * (m[None, :] - OFFC) + 31 <= tq[:, None], 0.0, NEG).astype(np.float32)
    curp = (tq >= 64).astype(np.int64)
    mm = np.arange(128)
    jp = np.broadcast_to(mm[None, :] - OFFV, (128, 128))
    A = (jp < (curp[:, None] - 1)).astype(np.float32)
    Bt = np.zeros((128, 128), np.float32)
    forced = (jp == curp[:, None]) | (jp == curp[:, None] - 1)
    invalid = jp > curp[:, None]
    Bt[forced] = (1e4 + (jp + 70))[forced]
    Bt[invalid] = (-1.0 - (jp + 70))[invalid]
    c["atab"] = A
    c["btab"] = Bt
    kk = np.arange(128)
    c["tri"] = np.where(kk[:, None] <= tq[None, :], 0.0, NEG).astype(np.float32)
    c["anti"] = np.where(kk[:, None] > tq[None, :], 0.0, NEG).astype(np.float32)
    key = np.arange(SEQ)
    c["erows"] = (key[None, :] // 64 == np.arange(64)[:, None]).astype(np.float32)
    return c


CONST_SHAPES = {"cs_tok": [SEQ, 64], "cs_cmp": [256, 64], "sel": [128, 64, 128], "selT": [128, 64, 128],
                "tmask": [128, 128], "identf": [128, 128], "jmat": [128, 128], "sgn": [128, 1],
                "evec": [128, NE], "maskc": [128, 512], "atab": [128, 128], "btab": [128, 128],
                "tri": [128, 128], "anti": [128, 128], "erows": [64, SEQ]}


def layout_params(inp):
    L = DEPTH
    f = lambda a: np.ascontiguousarray(np.asarray(a, np.float32))
    o = {}
    trn = lambda a: f(np.asarray(a, np.float32).reshape(L, NFT, 128).transpose(2, 0, 1))
    o["norm_ffn1T"] = trn(inp["norm_ffn1"]); o["norm_ffn2T"] = trn(inp["norm_ffn2"]); o["norm_mixT"] = trn(inp["norm_mix"])
    for k in ("ffn1_wi", "ffn1_wo", "ffn2_wi", "ffn2_wo", "w_in", "ssm_w_glu", "nsa_w_up", "w_out", "cmp_w1", "cmp_w2"):
        o[k] = f(inp[k])
    dup = lambda a: np.concatenate([a, a], axis=0)
    o["lamre"] = f(dup(np.asarray(inp["ssm_lambda_re"]).transpose(2, 0, 1)))
    o["lamim"] = f(dup(np.asarray(inp["ssm_lambda_im"]).transpose(2, 0, 1)))
    o["logdt"] = f(np.broadcast_to(np.asarray(inp["ssm_log_dt"])[None], (128, L, 32)))
    o["bre"] = f(dup(np.asarray(inp["ssm_b_re"]).transpose(2, 0, 1, 3)))
    o["bim"] = f(dup(np.asarray(inp["ssm_b_im"]).transpose(2, 0, 1, 3)))
    o["cre"] = f(dup(np.asarray(inp["ssm_c_re"]).transpose(3, 0, 1, 2)))
    o["cim"] = f(dup(np.asarray(inp["ssm_c_im"]).transpose(3, 0, 1, 2)))
    dsk = np.asarray(inp["ssm_d"])
    o["dvec"] = f(np.tile(dsk.transpose(2, 0, 1), (8, 1, 1)))
    o["qgain"] = f(np.broadcast_to(np.asarray(inp["q_norm"])[None], (128, L, 64)))
    o["kgain"] = f(np.broadcast_to(np.asarray(inp["k_norm"])[None], (128, L, 3, 64)))
    o["peT"] = f(np.asarray(inp["cmp_pe"]).transpose(3, 0, 1, 2))
    return o


PARAM_SHAPES = {"norm_ffn1T": [128, DEPTH, NFT], "norm_ffn2T": [128, DEPTH, NFT], "norm_mixT": [128, DEPTH, NFT],
                "ffn1_wi": [DEPTH, D_MODEL, 2 * D_FF], "ffn1_wo": [DEPTH, D_FF, D_MODEL],
                "ffn2_wi": [DEPTH, D_MODEL, 2 * D_FF], "ffn2_wo": [DEPTH, D_FF, D_MODEL],
                "w_in": [DEPTH, D_MODEL, NCOL], "ssm_w_glu": [DEPTH, 512, 2048], "nsa_w_up": [DEPTH, 512, 1024],
                "w_out": [DEPTH, D_MODEL, D_MODEL], "cmp_w1": [DEPTH, 2, 2048, 128], "cmp_w2": [DEPTH, 2, 128, 64],
                "lamre": [128, DEPTH, 32], "lamim": [128, DEPTH, 32], "logdt": [128, DEPTH, 32],
                "bre": [128, DEPTH, 32, 16], "bim": [128, DEPTH, 32, 16], "cre": [128, DEPTH, 32, 16], "cim": [128, DEPTH, 32, 16],
                "dvec": [128, DEPTH, 32], "qgain": [128, DEPTH, 64], "kgain": [128, DEPTH, 3, 64], "peT": [64, DEPTH, 2, 32]}


class Arena:
    def __init__(self, ap, n, tag):
        self.ap, self.n, self.tag, self.off = ap, n, tag, 0

    def reset(self):
        self.off = 0

    def take(self, shape):
        n = 1
        for v in shape[1:]:
            n *= v
        assert self.off + n <= self.n, (self.tag, self.off, n, self.n)
        v = self.ap[:, self.off:self.off + n]
        self.off += n
        if len(shape) == 3:
            v = v.rearrange("p (a b) -> p a b", a=shape[1])
        elif len(shape) == 4:
            v = v.rearrange("p (a b c) -> p a b c", a=shape[1], b=shape[2])
        return v


def build_program(depth=DEPTH, stages=("ffn1", "mix", "ffn2"), debug=False, mix_parts=("proj", "s5", "nsa", "out")):
    nc = bass.Bass("TRN2", target_bir_lowering=False)
    es = ExitStack()
    D = {}
    for k, shp in list(PARAM_SHAPES.items()) + list(CONST_SHAPES.items()):
        D[k] = nc.dram_tensor(k, list(shp), F32, kind="ExternalInput").ap()
    xT_in = nc.dram_tensor("xT", [D_MODEL, SEQ], F32, kind="ExternalInput").ap()
    outT = nc.dram_tensor("outT", [D_MODEL, SEQ], F32, kind="ExternalOutput").ap()
    skind = "ExternalOutput" if debug else "Internal"
    xres = nc.dram_tensor("xres", [D_MODEL, SEQ], F32, kind=skind).ap()
    uT_d = nc.dram_tensor("uT_d", [512, SEQ], BF16, kind=skind).ap()
    yT_d = nc.dram_tensor("yT_d", [512, SEQ], BF16, kind=skind).ap()
    ocT_d = nc.dram_tensor("ocT_d", [512, SEQ], BF16, kind=skind).ap()
    qkv_d = nc.dram_tensor("qkv_d", [SEQ, QKVW], F32, kind=skind).ap()

    with es:
        P = Prog(nc, es)
        sbt = lambda name, shape, dt: es.enter_context(nc.sbuf_tensor(name, list(shape), dt))
        FA = Arena(sbt("f32arena", [128, F32N], F32), F32N, "f32")
        BA = Arena(sbt("bf16arena", [128, BFN], BF16), BFN, "bf16")
        itmp = sbt("itmp", [128, 8 * NE], mybir.dt.int32)
        gains = {k: sbt("g_" + k, [128, DEPTH, NFT], F32) for k in ("norm_ffn1T", "norm_ffn2T", "norm_mixT")}
        ones_bf = sbt("ones_bf", [128, 128], BF16)
        ident_bf = sbt("ident_bf", [128, 128], BF16)
        identf = sbt("identf_sb", [128, 128], F32)
        banks = [es.enter_context(nc.psum_tensor("bank%d" % i, [128, 512], F32)) for i in range(7)]
        pbank = es.enter_context(nc.psum_tensor("pbank", [128, 1024], BF16))
        bn = ["bank%d" % i for i in range(7)]

        def MM(out, lhsT, rhs, st, sp, r, w, sgc=False):
            if sgc:
                P.op("pe", lambda e: e.matmul(out, lhsT, rhs, start=st, stop=sp, skip_group_check=True), r, w)
            else:
                P.op("pe", lambda e: e.matmul(out, lhsT, rhs, start=st, stop=sp), r, w)

        def TR(out, in_, r, w):
            P.op("pe", lambda e: e.transpose(out, in_, ident_bf[:]), list(r) + ["ident_bf"], w)

        def ACT(out, in_, func, r, w, **kw):
            P.op("act", lambda e: e.activation(out=out, in_=in_, func=func, **kw), r, w)

        def VTT(eng, out, a, b, op, r, w):
            P.op(eng, lambda e: e.tensor_tensor(out, a, b, op), r, w)

        def TS(eng, out, a, s1, s2, op0, op1, r, w):
            if op1 is None:
                P.op(eng, lambda e: e.tensor_scalar(out, a, s1, s2, op0=op0), r, w)
            else:
                P.op(eng, lambda e: e.tensor_scalar(out, a, s1, s2, op0=op0, op1=op1), r, w)

        def STT(eng, out, in0, scalar, in1, op0, op1, r, w):
            eng = "dve"
            P.op(eng, lambda e: e.scalar_tensor_tensor(out=out, in0=in0, scalar=scalar, in1=in1, op0=op0, op1=op1), r, w)

        def CP(eng, out, in_, r, w):
            if eng == "act":
                ACT(out, in_, AF.Copy, r, w)
            else:
                P.op(eng, lambda e: e.tensor_copy(out, in_), r, w)

        def MEMSET(eng, out, val, w):
            P.op(eng, lambda e: e.memset(out, val), (), w)

        def RECIP(out, in_, r, w):
            P.op("dve", lambda e: e.reciprocal(out, in_), r, w)

        def DMA(q, out, in_, r, w):
            return P.dma(q, out, in_, r, w)

        xview = lambda t: t.rearrange("(ft p) s -> p ft s", p=128)
        dtile = lambda name, ft, tok0: ("dram", name, ft, tok0)

        for k in gains:
            DMA("sp", gains[k][:], D[k][:, :, :], (), ["g_" + k])
        MEMSET("dve", ones_bf[:], 1.0, ["ones"])
        DMA("sp", identf[:], D["identf"][:, :], (), ["identf"])
        DMA("pool", ident_bf[:], D["identf"][:, :], (), ["ident_bf"])

        def norm_tile(pfx, xb, xn, gain_ap, gname, hdst, hname, sq, rstd, ps_n, psn_name):
            ACT(sq[:], xb[:], AF.Square, [xn], [pfx + "sq"])
            for ft in range(NFT):
                MM(ps_n, ones_bf[:], sq[:, ft, :], ft == 0, ft == NFT - 1, [pfx + "sq", "ones"], [psn_name])
            TS("dve", rstd[:], ps_n, 1.0 / D_MODEL, EPS, ALU.mult, ALU.add, [psn_name], [pfx + "rstd"])
            ACT(rstd[:], rstd[:], AF.Sqrt, [pfx + "rstd"], [pfx + "rstd"])
            RECIP(rstd[:], rstd[:], [pfx + "rstd"], [pfx + "rstd"])
            for ft in range(NFT):
                STT("dve" if ft % 2 == 0 else "pool", hdst[:, ft, :], xb[:, ft, :], gain_ap[:, ft:ft + 1], rstd[:],
                    ALU.mult, ALU.mult, [xn, pfx + "rstd", gname], [hname + "_%d" % ft])

        def ffn(l, src, sname, dst, dname, gkey, wi, wo):
            P.barrier(); FA.reset(); BA.reset()
            gain = gains[gkey]
            xt = [FA.take([128, NFT, TT]) for _ in range(2)]
            rstd = FA.take([128, TT])
            sg = [FA.take([128, TT]) for _ in range(2)]
            sq = BA.take([128, NFT, TT])
            hT = BA.take([128, NFT, ST])
            actT = BA.take([128, NFF, ST])
            wi_sb = [BA.take([128, NFT, 256]) for _ in range(3)]
            wo_sb = [BA.take([128, NFF, 128]) for _ in range(2)]
            ps_n, ps_g, ps_u, ps_o = banks[0][:], [banks[1][:], banks[2][:]], [banks[3][:], banks[4][:]], [banks[5][:], banks[6][:]]
            nst, ntt = SEQ // ST, ST // TT
            wiv = wi[l].rearrange("(kt p) c -> p kt c", p=128)
            wov = wo[l].rearrange("(ft p) c -> p ft c", p=128)
            for s in range(nst):
                for t in range(ntt):
                    tok0 = s * ST + t * TT
                    xb, xn = xt[t % 2], "f_xt%d" % (t % 2)
                    DMA("sp", xb[:], xview(src)[:, :, tok0:tok0 + TT], [dtile(sname, o, tok0) for o in range(NFT)], [xn])
                    norm_tile("f_", xb, xn, gain[:, l, :], "g_" + gkey, hT[:, :, t * TT:(t + 1) * TT], "f_hT%d" % t,
                              sq, rstd, ps_n, bn[0])
                for f in range(NFF):
                    wb, wn = wi_sb[f % 3], "f_wi%d" % (f % 3)
                    DMA("pool", wb[:, :, 0:128], wiv[:, :, f * 128:(f + 1) * 128], (), [wn + "g"])
                    DMA("pool", wb[:, :, 128:256], wiv[:, :, D_FF + f * 128:D_FF + (f + 1) * 128], (), [wn + "u"])
                    for t in range(ntt):
                        pg, pgn, pu, pun = ps_g[t % 2], bn[1 + t % 2], ps_u[t % 2], bn[3 + t % 2]
                        hr = ["f_hT%d_%d" % (t, ft) for ft in range(NFT)]
                        for kt in range(NFT):
                            MM(pg, wb[:, kt, 0:128], hT[:, kt, t * TT:(t + 1) * TT], kt == 0, kt == NFT - 1, hr + [wn + "g"], [pgn])
                        for kt in range(NFT):
                            MM(pu, wb[:, kt, 128:256], hT[:, kt, t * TT:(t + 1) * TT], kt == 0, kt == NFT - 1, hr + [wn + "u"], [pun])
                        sgb, sgn = sg[t % 2], "f_sg%d" % (t % 2)
                        ACT(sgb[:], pg, AF.Silu, [pgn], [sgn])
                        TT_ = "dve"
                        VTT(TT_, actT[:, f, t * TT:(t + 1) * TT], sgb[:], pu, ALU.mult, [sgn, pun], ["f_act%d_%d" % (f, t)])
                for ot in range(NFT):
                    wb, wn = wo_sb[ot % 2], "f_wo%d" % (ot % 2)
                    DMA("pool", wb[:], wov[:, :, ot * 128:(ot + 1) * 128], (), [wn])
                    for t in range(ntt):
                        tok0 = s * ST + t * TT
                        po, pon = ps_o[t % 2], bn[5 + t % 2]
                        for f in range(NFF):
                            MM(po, wb[:, f, :], actT[:, f, t * TT:(t + 1) * TT], f == 0, f == NFF - 1,
                               ["f_act%d_%d" % (f, t), wn], [pon])
                        rb, rn = sg[t % 2], "f_sg%d" % (t % 2)
                        DMA("sp", rb[:], src[ot * 128:(ot + 1) * 128, tok0:tok0 + TT], [dtile(sname, ot, tok0)], [rn])
                        STT("dve", rb[:], po, 0.5, rb[:], ALU.mult, ALU.add, [pon, rn], [rn])
                        DMA("sp", dst[ot * 128:(ot + 1) * 128, tok0:tok0 + TT], rb[:], [rn], [dtile(dname, ot, tok0)])

        def mixer_proj(l, src, sname):
            P.barrier(); FA.reset(); BA.reset()
            gain = gains["norm_mixT"]
            xt = [FA.take([128, NFT, TT]) for _ in range(2)]
            rstd = FA.take([128, TT])
            qrow = [FA.take([128, QKVW]) for _ in range(2)]
            sq = BA.take([128, NFT, TT])
            hT = BA.take([128, NFT, TT])
            wq = BA.take([128, NFT, GA0])
            ub = [BA.take([128, TT]) for _ in range(2)]
            wv = D["w_in"][l].rearrange("(kt p) c -> p kt c", p=128)
            for kt in range(NFT):
                DMA("pool", wq[:, kt, :], wv[:, kt, 0:GA0], (), ["m_wq%d" % kt])
            wqr = ["m_wq%d" % kt for kt in range(NFT)]
            for tt in range(SEQ // TT):
                tok0 = tt * TT
                xb, xn = xt[tt % 2], "m_xt%d" % (tt % 2)
                DMA("sp", xb[:], xview(src)[:, :, tok0:tok0 + TT], [dtile(sname, o, tok0) for o in range(NFT)], [xn])
                norm_tile("m_", xb, xn, gain[:, l, :], "g_norm_mixT", hT, "m_hT", sq, rstd, banks[0][:], bn[0])
                hr = ["m_hT_%d" % ft for ft in range(NFT)]
                for c in range(4):
                    pb, pn = banks[1 + c % 2][:], bn[1 + c % 2]
                    for kt in range(NFT):
                        MM(pb, wq[:, kt, c * 128:(c + 1) * 128], hT[:, kt, :], kt == 0, kt == NFT - 1, hr + wqr, [pn])
                    u_, un = ub[c % 2], "m_ub%d" % (c % 2)
                    CP("act", u_[:], pb, [pn], [un])
                    DMA("sp", uT_d[c * 128:(c + 1) * 128, tok0:tok0 + TT], u_[:], [un], [("dram", "uT", c, tt)])
                for sub in range(4):
                    qr, qn = qrow[sub % 2], "m_qrow%d" % (sub % 2)
                    for bi, (c0, c1) in enumerate(((512, 1024), (1024, 1536), (1536, GA0))):
                        pb, pn = banks[3 + bi][:], bn[3 + bi]
                        for kt in range(NFT):
                            MM(pb[:, 0:c1 - c0], hT[:, kt, sub * 128:(sub + 1) * 128], wq[:, kt, c0:c1], kt == 0, kt == NFT - 1,
                               hr + wqr, [pn])
                        CP("dve" if bi != 1 else "act", qr[:, c0 - 512:c1 - 512], pb[:, 0:c1 - c0], [pn], [qn + "_%d" % bi])
                    r0 = tok0 + sub * 128
                    DMA("sp", qkv_d[r0:r0 + 128, :], qr[:], [qn + "_%d" % bi for bi in range(3)], [("dram", "qkv", r0 // 128)])

        def range_reduce(x, tmp, n, r):
            iv = itmp[:, 0:n]
            TS("dve", tmp, x, 1.0 / TWO_PI, 64.5, ALU.mult, ALU.add, [r], ["s_rtmp"])
            CP("dve", iv, tmp, ["s_rtmp"], ["s_itmp"])
            CP("dve", tmp, iv, ["s_itmp"], ["s_rtmp"])
            TS("dve", tmp, tmp, -64.0, -TWO_PI, ALU.add, ALU.mult, ["s_rtmp"], ["s_rtmp"])
            VTT("dve", x, x, tmp, ALU.add, [r, "s_rtmp"], [r])
            TS("dve", tmp, x, float(np.pi), -TWO_PI, ALU.is_gt, ALU.mult, [r], ["s_rtmp"])
            VTT("dve", x, x, tmp, ALU.add, [r, "s_rtmp"], [r])
            TS("dve", tmp, x, float(-np.pi), TWO_PI, ALU.is_lt, ALU.mult, [r], ["s_rtmp"])
            VTT("dve", x, x, tmp, ALU.add, [r, "s_rtmp"], [r])

        def s5(l):
            P.barrier(); FA.reset(); BA.reset()
            G8 = 8
            lr = FA.take([128, G8]); li = FA.take([128, G8]); ldt = FA.take([128, G8])
            dvc = FA.take([128, G8]); sgn = FA.take([128, 1]); evec = FA.take([128, NE])
            braw = [FA.take([128, G8, 16]) for _ in range(2)]
            craw = [FA.take([128, G8, 16]) for _ in range(2)]
            bbar = [FA.take([128, G8, 16]) for _ in range(2)]
            t8 = [FA.take([128, G8]) for _ in range(6)]
            marg = FA.take([128, G8, NE]); ang = FA.take([128, G8, NE]); ang2 = FA.take([128, G8, NE])
            rtmp = FA.take([128, G8, NE])
            are = FA.take([128, G8, NE]); aim = FA.take([128, G8, NE])
            S = FA.take([128, G8, 15, 16]); R = FA.take([128, G8, 9, 16])
            T1 = FA.take([128, G8, 15, 16]); T2 = FA.take([128, G8, 15, 16])
            xpad = [FA.take([128, 1024]) for _ in range(2)]
            tmaskf = FA.take([128, 128]); jmat = FA.take([128, 128])
            ttmp = FA.take([128, 128]); mtmp = FA.take([128, 128]); ms = [FA.take([128, 128]) for _ in range(2)]
            sel = BA.take([128, 64, 128]); selT = BA.take([128, 64, 128])
            utile = BA.take([128, SEQ]); ytile = BA.take([128, SEQ])
            usb = BA.take([128, 512]); xprev = BA.take([128, 512])
            yact = [BA.take([128, 512]) for _ in range(8)]
            Tg = BA.take([128, 128]); Gg = BA.take([128, 128]); Hg = BA.take([128, 128])
            DMA("pool", sel[:], D["sel"][:, :, :], (), ["s_sel"])
            DMA("pool", selT[:], D["selT"][:, :, :], (), ["s_selT"])
            DMA("sp", tmaskf[:], D["tmask"][:, :], (), ["s_tmask"])
            DMA("sp", jmat[:], D["jmat"][:, :], (), ["s_jmat"])
            DMA("sp", sgn[:], D["sgn"][:, :], (), ["s_sgn"])
            DMA("sp", evec[:], D["evec"][:, :], (), ["s_evec"])
            MEMSET("pool", xpad[0][:, 0:512], 0.0, ["s_xp0z"])
            MEMSET("pool", xpad[1][:, 0:512], 0.0, ["s_xp1z"])
            for bt in range(4):
                g0 = bt * 8
                gs = slice(g0, g0 + 8)
                DMA("sp", lr[:], D["lamre"][:, l, gs], (), ["s_lr"])
                DMA("sp", li[:], D["lamim"][:, l, gs], (), ["s_li"])
                DMA("sp", ldt[:], D["logdt"][:, l, gs], (), ["s_ldt"])
                DMA("sp", dvc[:], D["dvec"][:, l, gs], (), ["s_dvc"])
                DMA("sp", braw[0][:], D["bre"][:, l, gs, :], (), ["s_bre"])
                DMA("sp", braw[1][:], D["bim"][:, l, gs, :], (), ["s_bim"])
                DMA("sp", craw[0][:], D["cre"][:, l, gs, :], (), ["s_cre"])
                DMA("sp", craw[1][:], D["cim"][:, l, gs, :], (), ["s_cim"])
                DMA("sp", utile[:], uT_d[bt * 128:(bt + 1) * 128, :], [("dram", "uT", bt, tt) for tt in range(8)], ["s_utile"])
                dt_, lrdt, lidt, inv, fre, fim = t8
                ACT(dt_[:], ldt[:], AF.Exp, ["s_ldt"], ["s_dt"])
                VTT("dve", lrdt[:], lr[:], dt_[:], ALU.mult, ["s_lr", "s_dt"], ["s_lrdt"])
                VTT("dve", lidt[:], li[:], dt_[:], ALU.mult, ["s_li", "s_dt"], ["s_lidt"])
                bc_e = lambda t: t[:].unsqueeze(2).to_broadcast([128, G8, NE])
                ev_b = evec[:].unsqueeze(1).to_broadcast([128, G8, NE])
                VTT("dve", marg[:], bc_e(lrdt), ev_b, ALU.mult, ["s_lrdt", "s_evec"], ["s_marg"])
                VTT("dve", ang[:], bc_e(lidt), ev_b, ALU.mult, ["s_lidt", "s_evec"], ["s_ang"])
                TS("dve", ang2[:], ang[:], float(np.pi / 2), None, ALU.add, None, ["s_ang"], ["s_ang2"])
                ACT(marg[:], marg[:], AF.Exp, ["s_marg"], ["s_marg"])
                fl = lambda t: t[:].rearrange("p g e -> p (g e)")
                range_reduce(fl(ang), fl(rtmp), G8 * NE, "s_ang")
                range_reduce(fl(ang2), fl(rtmp), G8 * NE, "s_ang2")
                ACT(ang[:], ang[:], AF.Sin, ["s_ang"], ["s_ang"])
                ACT(ang2[:], ang2[:], AF.Sin, ["s_ang2"], ["s_ang2"])
                VTT("dve", are[:], marg[:], ang2[:], ALU.mult, ["s_marg", "s_ang2"], ["s_are"])
                VTT("dve", aim[:], marg[:], ang[:], ALU.mult, ["s_marg", "s_ang"], ["s_aim"])
                a1r, a1i = are[:, :, 6], aim[:, :, 6]
                u0, u1 = T1[:, :, 0, 0], T1[:, :, 0, 1]
                VTT("dve", inv[:], lr[:], lr[:], ALU.mult, ["s_lr"], ["s_inv"])
                VTT("dve", u0, li[:], li[:], ALU.mult, ["s_li", "s_T1h", "s_T2h"], ["s_T1"])
                VTT("dve", inv[:], inv[:], u0, ALU.add, ["s_inv", "s_T1"], ["s_inv"])
                RECIP(inv[:], inv[:], ["s_inv"], ["s_inv"])
                TS("dve", u0, a1r, -1.0, None, ALU.add, None, ["s_are"], ["s_T1"])
                VTT("dve", fre[:], u0, lr[:], ALU.mult, ["s_T1", "s_lr"], ["s_fre"])
                VTT("dve", u1, a1i, li[:], ALU.mult, ["s_aim", "s_li"], ["s_T1b"])
                VTT("dve", fre[:], fre[:], u1, ALU.add, ["s_fre", "s_T1b"], ["s_fre"])
                VTT("dve", fre[:], fre[:], inv[:], ALU.mult, ["s_fre", "s_inv"], ["s_fre"])
                VTT("dve", fim[:], a1i, lr[:], ALU.mult, ["s_aim", "s_lr"], ["s_fim"])
                VTT("dve", u1, u0, li[:], ALU.mult, ["s_T1", "s_li"], ["s_T1b"])
                VTT("dve", fim[:], fim[:], u1, ALU.subtract, ["s_fim", "s_T1b"], ["s_fim"])
                VTT("dve", fim[:], fim[:], inv[:], ALU.mult, ["s_fim", "s_inv"], ["s_fim"])
                bc_c = lambda t: t[:].unsqueeze(2).to_broadcast([128, G8, 16])
                w0, w1_ = T2[:, :, 0, :], T2[:, :, 1, :]
                VTT("dve", w0, bc_c(fre), braw[0][:], ALU.mult, ["s_fre", "s_bre"], ["s_T2"])
                VTT("dve", w1_, bc_c(fim), braw[1][:], ALU.mult, ["s_fim", "s_bim"], ["s_T2b"])
                VTT("dve", bbar[0][:], w0, w1_, ALU.subtract, ["s_T2", "s_T2b"], ["s_bbr"])
                VTT("dve", w0, bc_c(fre), braw[1][:], ALU.mult, ["s_fre", "s_bim"], ["s_T2"])
                VTT("dve", w1_, bc_c(fim), braw[0][:], ALU.mult, ["s_fim", "s_bre"], ["s_T2b"])
                VTT("dve", bbar[1][:], w0, w1_, ALU.add, ["s_T2", "s_T2b"], ["s_bbi"])
                lo, hi = slice(0, 64), slice(64, 128)
                Ab = lambda t, h, k0, k1, n: t[h, :, k0:k1].unsqueeze(3).to_broadcast([64, G8, k1 - k0, 16])
                Zb = lambda t, h, n: t[h, :, :].unsqueeze(2).to_broadcast([64, G8, n, 16])
                XD = ["s_bbr", "s_bbi", "s_fre", "s_fim", "s_T1", "s_T1b", "s_T2", "s_T2b"]
                VTT("dve", T1[lo], Ab(are, lo, 0, 15, 15), Zb(bbar[0], lo, 15), ALU.mult, ["s_are"] + XD, ["s_T1"])
                VTT("pool", T2[lo], Ab(aim, lo, 0, 15, 15), Zb(bbar[1], lo, 15), ALU.mult, ["s_aim"] + XD, ["s_T2"])
                VTT("dve", S[lo], T1[lo], T2[lo], ALU.subtract, ["s_T1", "s_T2"], ["s_Slo"])
                VTT("dve", T1[hi], Ab(are, hi, 0, 15, 15), Zb(bbar[1], hi, 15), ALU.mult, ["s_are"] + XD, ["s_T1h"])
                VTT("pool", T2[hi], Ab(aim, hi, 0, 15, 15), Zb(bbar[0], hi, 15), ALU.mult, ["s_aim"] + XD, ["s_T2h"])
                VTT("dve", S[hi], T1[hi], T2[hi], ALU.add, ["s_T1h", "s_T2h"], ["s_Shi"])
                T1r, T2r = T1[:, :, 0:9, :], T2[:, :, 0:9, :]
                VTT("dve", T1r[lo], Ab(are, lo, 15, 24, 9), Zb(craw[0], lo, 9), ALU.mult, ["s_are", "s_cre", "s_Slo"], ["s_T1"])
                VTT("pool", T2r[lo], Ab(aim, lo, 15, 24, 9), Zb(craw[1], lo, 9), ALU.mult, ["s_aim", "s_cim", "s_Slo"], ["s_T2"])
                VTT("dve", R[lo], T1r[lo], T2r[lo], ALU.subtract, ["s_T1", "s_T2"], ["s_Rlo"])
                VTT("dve", T1r[hi], Ab(are, hi, 15, 24, 9), Zb(craw[1], hi, 9), ALU.mult, ["s_are", "s_cim", "s_Shi"], ["s_T1h"])
                VTT("pool", T2r[hi], Ab(aim, hi, 15, 24, 9), Zb(craw[0], hi, 9), ALU.mult, ["s_aim", "s_cre", "s_Shi"], ["s_T2h"])
                STT("dve", R[hi], T1r[hi], -1.0, T2r[hi], ALU.mult, ALU.subtract, ["s_T1h", "s_T2h"], ["s_Rhi"])
                TS("dve", aim[:, :, 24:33], aim[:, :, 24:33], sgn[:, 0:1], None, ALU.mult, None, ["s_aim", "s_sgn"], ["s_aims"])
                Sr, Rr = ["s_Slo", "s_Shi"], ["s_Rlo", "s_Rhi"]
                for gg in range(8):
                    g = g0 + gg
                    Pm = S[:, gg, 7:15, :].rearrange("p i c -> p (i c)")
                    Gs = S[:, gg, 0:8, :].rearrange("p i c -> p (i c)")
                    Qm = R[:, gg, 0:8, :].rearrange("p j c -> p (j c)")
                    Hm = R[:, gg, 1:9, :].rearrange("p j c -> p (j c)")
                    pt, ptn = banks[0][:, 0:128], bn[0]
                    MM(pt, Pm, Qm, True, True, Sr + Rr, [ptn])
                    VTT("dve", ttmp[:], pt, tmaskf[:], ALU.mult, [ptn, "s_tmask"], ["s_ttmp"])
                    STT("dve", Tg[:], identf[:], dvc[:, gg:gg + 1], ttmp[:], ALU.mult, ALU.add, ["identf", "s_dvc", "s_ttmp"], ["s_Tg"])
                    pg_, pgn = banks[0][:, 128:256], bn[0]
                    MM(pg_, Gs, identf[:], True, True, Sr + ["identf"], [pgn])
                    CP("act", Gg[:], pg_, [pgn], ["s_Gg"])
                    CP("act", Hg[:], Hm, Rr, ["s_Hg"])
                    pu, pun = banks[1][:], bn[1]
                    for i in range(8):
                        MM(pu, sel[:, gg * 8 + i, :], utile[:, i::8], i == 0, i == 7, ["s_sel", "s_utile"], [pun])
                    CP("act", usb[:], pu, [pun], ["s_usb"])
                    pv, pvn = banks[2][:], bn[2]
                    MM(pv, Gg[:], usb[:], True, True, ["s_Gg", "s_usb"], [pvn])
                    CP("dve", xpad[0][:, 512:1024], pv, [pvn], ["s_xp0"])
                    cur = 0
                    for s_ in range(9):
                        sh = 2 ** s_
                        m_, mn = ms[s_ % 2], "s_ms%d" % (s_ % 2)
                        TS("pool", mtmp[:], jmat[:], aim[:, gg, 24 + s_:25 + s_], None, ALU.mult, None, ["s_jmat", "s_aims", "s_aim"], ["s_mtmp"])
                        STT("pool", m_[:], identf[:], are[:, gg, 24 + s_:25 + s_], mtmp[:], ALU.mult, ALU.add,
                            ["identf", "s_are", "s_mtmp"], [mn])
                        pz, pzn = banks[3 + s_ % 2][:], bn[3 + s_ % 2]
                        xc, xcn = xpad[cur], "s_xp%d" % cur
                        xn_, xnn = xpad[1 - cur], "s_xp%d" % (1 - cur)
                        MM(pz, m_[:], xc[:, 512 - sh:1024 - sh], True, True, [mn, xcn, "s_xp%dz" % cur], [pzn])
                        VTT("dve", xn_[:, 512:1024], pz, xc[:, 512:1024], ALU.add, [pzn, xcn], [xnn])
                        cur = 1 - cur
                    CP("act", xprev[:], xpad[cur][:, 511:1023], ["s_xp%d" % cur, "s_xp%dz" % cur], ["s_xprev"])
                    py, pyn = banks[5][:], bn[5]
                    MM(py, Tg[:], usb[:], True, False, ["s_Tg", "s_usb"], [pyn])
                    MM(py, Hg[:], xprev[:], False, True, ["s_Hg", "s_xprev"], [pyn])
                    ACT(yact[gg][:], py, AF.Gelu_apprx_tanh, [pyn], ["s_yact%d" % gg])
                for j in range(8):
                    pj, pjn = banks[1 + j % 2][:], bn[1 + j % 2]
                    for gg in range(8):
                        MM(pj, selT[:, gg * 8 + j, :], yact[gg][:], gg == 0, gg == 7, ["s_selT", "s_yact%d" % gg], [pjn])
                    CP("dve" if j % 2 == 0 else "act", ytile[:, j::8], pj, [pjn], ["s_ytile%d" % j])
                DMA("sp", yT_d[bt * 128:(bt + 1) * 128, :], ytile[:], ["s_ytile%d" % j for j in range(8)], [("dram", "yT", bt)])

        def normrope(pfx, src, gain, cs, dst, T, H, tmp, r):
            sqv, xn, ssq, t1, t2 = tmp
            n = T * H
            v4 = lambda t, w: t[:, 0:n * w].rearrange("p (t h d) -> p t h d", t=T, h=H)
            sq4, xn4 = v4(sqv, 64), v4(xn, 64)
            VTT("dve", sq4, src, src, ALU.mult, r, [pfx + "sq"])
            P.op("dve", lambda e: e.tensor_reduce(out=ssq[:, 0:n], in_=sqv[:, 0:n * 64].rearrange("p (a d) -> p a d", d=64),
                                                  axis=AX.X, op=ALU.add), [pfx + "sq"], [pfx + "ssq"])
            TS("dve", ssq[:, 0:n], ssq[:, 0:n], 1.0 / 64, EPS, ALU.mult, ALU.add, [pfx + "ssq"], [pfx + "ssq"])
            ACT(ssq[:, 0:n], ssq[:, 0:n], AF.Sqrt, [pfx + "ssq"], [pfx + "ssq"])
            RECIP(ssq[:, 0:n], ssq[:, 0:n], [pfx + "ssq"], [pfx + "ssq"])
            rs4 = ssq[:, 0:n].rearrange("p (t h) -> p t h", t=T).unsqueeze(3).to_broadcast([128, T, H, 64])
            VTT("dve", xn4, src, rs4, ALU.mult, list(r) + [pfx + "ssq"], [pfx + "xn"])
            g4 = gain.unsqueeze(2).to_broadcast([128, T, H, 64])
            VTT("pool", xn4, xn4, g4, ALU.mult, [pfx + "xn", pfx + "gain"], [pfx + "xn"])
            c4 = cs[:, 0:32].unsqueeze(1).unsqueeze(1).to_broadcast([128, T, H, 32])
            s4 = cs[:, 32:64].unsqueeze(1).unsqueeze(1).to_broadcast([128, T, H, 32])
            x1, x2 = xn4[:, :, :, 0:32], xn4[:, :, :, 32:64]
            a4, b4 = v4(t1, 32), v4(t2, 32)
            VTT("dve", a4, x1, c4, ALU.mult, [pfx + "xn", pfx + "cs"], [pfx + "t1"])
            VTT("pool", b4, x2, s4, ALU.mult, [pfx + "xn", pfx + "cs"], [pfx + "t2"])
            VTT("dve", dst[:, :, :, 0:32], a4, b4, ALU.subtract, [pfx + "t1", pfx + "t2"], [pfx + "dstA"])
            VTT("dve", a4, x2, c4, ALU.mult, [pfx + "xn", pfx + "cs", pfx + "dstA"], [pfx + "t1"])
            VTT("pool", b4, x1, s4, ALU.mult, [pfx + "xn", pfx + "cs", pfx + "dstA"], [pfx + "t2"])
            VTT("dve", dst[:, :, :, 32:64], a4, b4, ALU.add, [pfx + "t1", pfx + "t2"], [pfx + "dstB"])

        def nsa(l):
            P.barrier(); FA.reset(); BA.reset()
            NB = SEQ // 128
            kselT = [BA.take([128, SEQ]) for _ in range(2)]
            kwinT = [BA.take([128, SEQ]) for _ in range(2)]
            vsel = BA.take([128, 2, NB, 65]); vwin = BA.take([128, 2, NB, 65])
            kcT = [BA.take([128, 256]) for _ in range(2)]
            vc = BA.take([128, 2, 2, 64])
            tri = BA.take([128, 128]); anti = BA.take([128, 128])
            tri4 = BA.take([128, 4, 128]); anti4 = BA.take([128, 4, 128])
            w2sb = BA.take([128, 2, 64])
            bf_mark = BA.off
            qg = FA.take([128, 1, 64]); kg = FA.take([128, 3, 64])
            maskc = FA.take([128, 512]); atab = FA.take([128, 128]); btab = FA.take([128, 128])
            bias = FA.take([128, 2])
            cs = [FA.take([128, 64]) for _ in range(2)]
            tmp = (FA.take([128, 768]), FA.take([128, 768]), FA.take([128, 16]), FA.take([128, 384]), FA.take([128, 384]))
            f_mark = FA.off
            for g in range(2):
                DMA("pool", kselT[g][64:128, :], D["erows"][:, :], (), ["n_erows%d" % g])
            MEMSET("pool", vsel[:, :, :, 64:65], 1.0, ["n_vsel1"])
            MEMSET("pool", vwin[:, :, :, 64:65], 1.0, ["n_vwin1"])
            DMA("pool", tri[:], D["tri"][:, :], (), ["n_tri"])
            DMA("pool", anti[:], D["anti"][:, :], (), ["n_anti"])
            CP("dve", tri4[:], tri[:].unsqueeze(1).to_broadcast([128, 4, 128]), ["n_tri"], ["n_tri4"])
            CP("dve", anti4[:], anti[:].unsqueeze(1).to_broadcast([128, 4, 128]), ["n_anti"], ["n_anti4"])
            DMA("sp", qg[:, 0, :], D["qgain"][:, l, :], (), ["n_qg"])
            DMA("sp", kg[:], D["kgain"][:, l, :, :], (), ["n_kg"])
            TS("dve", qg[:], qg[:], 0.125, None, ALU.mult, None, ["n_qg"], ["n_qg"])
            DMA("sp", maskc[:], D["maskc"][:, :], (), ["n_maskc"])
            DMA("sp", atab[:], D["atab"][:, :], (), ["n_atab"])
            DMA("sp", btab[:], D["btab"][:, :], (), ["n_btab"])
            DMA("pool", w2sb[:], D["cmp_w2"][l].rearrange("t h d -> h t d"), (), ["n_w2"])
            kcmpT = BA.take([128, SEQ]); vcmpT = BA.take([128, SEQ])
            w1sb = [BA.take([128, 32, 128]) for _ in range(2)]
            peT = BA.take([64, 2, 32])
            kb = BA.take([128, 2, 2, 64])
            cb = BA.take([128, 2, 128])
            hidT = BA.take([128, 256])
            kcn = BA.take([128, 1, 2, 64])
            kvrow = [FA.take([128, 768]) for _ in range(2)]
            kcraw = FA.take([128, 2, 2, 64])
            for ty in range(2):
                w1v = D["cmp_w1"][l, ty].rearrange("(l d) h -> d l h", d=64)
                DMA("pool", w1sb[ty][0:64], w1v, (), ["n_w1_%d_lo" % ty])
                DMA("pool", w1sb[ty][64:128], w1v, (), ["n_w1_%d_hi" % ty])
            DMA("pool", peT[0:64], D["peT"][:, l, :, :], (), ["n_peT"])
            for ty in range(2):
                pb, pn = banks[0][:, ty:ty + 1], bn[0]
                for ll in range(32):
                    MM(pb, w1sb[ty][0:64, ll, :], peT[0:64, ty, ll:ll + 1], ll == 0, ll == 31, ["n_w1_%d_lo" % ty, "n_peT"], [pn])
                CP("dve", bias[:, ty:ty + 1], pb, [pn], ["n_bias%d" % ty])
            pq = pbank[:, 0:512]
            for tb in range(NB):
                kr, krn = kvrow[tb % 2], "n_kvrow%d" % (tb % 2)
                c_, cn = cs[tb % 2], "n_cs%d" % (tb % 2)
                DMA("sp", kr[:], qkv_d[tb * 128:(tb + 1) * 128, 512:1280], [("dram", "qkv", tb)], [krn])
                DMA("sp", c_[:], D["cs_tok"][tb * 128:(tb + 1) * 128, :], (), [cn])
                src = kr[:, 256:768].rearrange("p (t x h d) -> p t x h d", t=2, x=2, h=2)[:, :, 0]
                P.buf["n_k_gain"] = P.buf.get("n_kg", {"w": None, "r": {}})
                P.buf["n_k_cs"] = P.buf.get(cn, {"w": None, "r": {}})
                normrope("n_k_", src, kg[:, 1:3, :], c_, kb[:], 2, 2, tmp, [krn])
                for ty in range(2):
                    for h in range(2):
                        po = pq[0:64, (ty * 2 + h) * 128:(ty * 2 + h + 1) * 128]
                        TR(po, kb[:, ty, h, :], ["n_k_dstA", "n_k_dstB"], ["pbank"])
                        dstt = (kselT if ty == 0 else kwinT)[h]
                        CP("act" if h == 0 else "dve", dstt[0:64, tb * 128:(tb + 1) * 128], po, ["pbank"],
                           ["n_kT%d_%d_%d" % (ty, h, tb)])
                vsrc = kr[:, 256:768].rearrange("p (t x h d) -> p t x h d", t=2, x=2, h=2)
                CP("pool", vsel[:, :, tb, 0:64], vsrc[:, 0, 1], [krn], ["n_vsel_%d" % tb])
                CP("pool", vwin[:, :, tb, 0:64], vsrc[:, 1, 1], [krn], ["n_vwin_%d" % tb])
                CP("pool", cb[:], kr[:, 0:256].rearrange("p (t c) -> p t c", t=2), [krn], ["n_cb"])
                for ty in range(2):
                    po = pbank[:, 512 + ty * 128:512 + (ty + 1) * 128]
                    TR(po, cb[:, ty, :], ["n_cb"], ["pbank"])
                    CP("act" if ty == 0 else "dve", (kcmpT if ty == 0 else vcmpT)[:, tb * 128:(tb + 1) * 128], po,
                       ["pbank"], ["n_cT%d_%d" % (ty, tb)])
            MEMSET("dve", hidT[:, 255:256], 0.0, ["n_hid255"])
            for ty in range(2):
                xT_ = kcmpT if ty == 0 else vcmpT
                xr = ["n_cT%d_%d" % (ty, tb) for tb in range(NB)]
                for g in range(2):
                    ph, phn = banks[1][:, 0:255], bn[1]
                    hs = slice(64 * g, 64 * g + 64)
                    for ll in range(32):
                        MM(ph, w1sb[ty][hs, ll, :], xT_[hs, ll:ll + 16 * 254 + 1:16], ll == 0, ll == 31,
                           xr + ["n_w1_%d_%s" % (ty, "lo" if g == 0 else "hi")], [phn])
                    ACT(hidT[:, 0:255], ph, AF.Gelu_apprx_tanh, [phn, "n_bias%d" % ty], ["n_hidT"], bias=bias[:, ty:ty + 1])
                    for c in range(2):
                        po, pon = banks[2][:, c * 64:(c + 1) * 64], bn[2]
                        MM(po, hidT[:, c * 128:(c + 1) * 128], w2sb[:, ty, :], True, True, ["n_hidT", "n_hid255", "n_w2"], [pon])
                        if ty == 1:
                            CP("act", vc[:, g, c, :], po, [pon], ["n_vc%d_%d" % (g, c)])
                        else:
                            CP("act", kcraw[:, c, g, :], po, [pon], ["n_kcraw%d_%d" % (c, g)])
            for c in range(2):
                c_, cn = cs[c], "n_cs%d" % c
                DMA("sp", c_[:], D["cs_cmp"][c * 128:(c + 1) * 128, :], (), [cn])
                P.buf["n_c_gain"] = P.buf.get("n_kg", {"w": None, "r": {}})
                P.buf["n_c_cs"] = P.buf.get(cn, {"w": None, "r": {}})
                normrope("n_c_", kcraw[:, c:c + 1, :, :], kg[:, 0:1, :], c_, kcn[:], 1, 2, tmp,
                         ["n_kcraw%d_%d" % (c, g) for g in range(2)])
                for g in range(2):
                    po = pq[0:64, g * 128:(g + 1) * 128]
                    TR(po, kcn[:, 0, g, :], ["n_c_dstA", "n_c_dstB"], ["pbank"])
                    CP("act", kcT[g][0:64, c * 128:(c + 1) * 128], po, ["pbank"], ["n_kcT%d_%d" % (g, c)])
            P.barrier()
            BA.off = bf_mark; FA.off = f_mark
            qaug = [BA.take([128, 512]) for _ in range(2)]
            PT = [BA.take([128, 512]) for _ in range(3)]
            qn = BA.take([128, 1, 8, 64])
            pbf = BA.take([128, 256]); pTsb = BA.take([128, 2, 128])
            nm = BA.take([128, 128]); otm = BA.take([128, 8, 64]); ocsb = BA.take([128, 4, 128])
            qrow = [FA.take([128, 512]) for _ in range(2)]
            grow = FA.take([128, 24]); gsig = FA.take([128, 24])
            scsb = FA.take([128, 256]); pun_ = FA.take([128, 256]); psumh = FA.take([128, 256])
            den = FA.take([128, 4]); rden = FA.take([128, 4])
            imp = FA.take([128, 64]); prio = FA.take([128, 64]); wk = FA.take([128, 64]); m1 = FA.take([128, 64]); m2 = FA.take([128, 64])
            mx = [FA.take([128, 8]) for _ in range(2)]
            coef = FA.take([128, 3, 4]); of32 = FA.take([128, 64])
            MEMSET("dve", nm[:, 0:64], 0.0, ["n_nm0"])
            kT_all = lambda ty, g: ["n_kT%d_%d_%d" % (ty, g, tb) for tb in range(NB)]
            for qb in range(min(NB, DBG["nqb"])):
                qr, qrn = qrow[qb % 2], "a_qrow%d" % (qb % 2)
                c_, cn = cs[qb % 2], "n_cs%d" % (qb % 2)
                DMA("sp", qr[:], qkv_d[qb * 128:(qb + 1) * 128, 0:512], [("dram", "qkv", qb)], [qrn])
                DMA("sp", grow[:], qkv_d[qb * 128:(qb + 1) * 128, 1280:1304], [("dram", "qkv", qb)], ["a_grow"])
                DMA("sp", c_[:], D["cs_tok"][qb * 128:(qb + 1) * 128, :], (), [cn])
                ACT(gsig[:], grow[:], AF.Sigmoid, ["a_grow"], ["a_gsig"])
                P.buf["a_q_gain"] = P.buf.get("n_qg", {"w": None, "r": {}})
                P.buf["a_q_cs"] = P.buf.get(cn, {"w": None, "r": {}})
                normrope("a_q_", qr[:].rearrange("p (t h d) -> p t h d", t=1, h=8), qg[:], c_, qn[:], 1, 8, tmp, [qrn])
                g4 = gsig[:].rearrange("p (g h b) -> p g h b", g=2, h=4)
                for g in range(2):
                    for h in range(4):
                        TR(pq[0:64, h * 128:(h + 1) * 128], qn[:, 0, 4 * g + h, :], ["a_q_dstA", "a_q_dstB"], ["pbank"])
                    CP("act", qaug[g][0:64, :], pq[0:64, :], ["pbank" for h in range(4)], ["a_qaug%d_q" % g])
                    nch = 1 if qb <= 15 else 2
                    ncol = 128 * nch
                    MEMSET("pool", psumh[:], 0.0, ["a_psumh"])
                    MEMSET("pool", den[:], 0.0, ["a_den"])
                    poc, pocn = banks[1][:, 0:256].rearrange("p (h d) -> p h d", h=4), bn[1]
                    for h in range(4):
                        psc, pscn = banks[0][:, 0:ncol], bn[0]
                        MM(psc, qaug[g][0:64, h * 128:(h + 1) * 128], kcT[g][0:64, 0:ncol], True, True,
                           ["a_qaug%d_q" % g] + ["n_kcT%d_%d" % (g, c) for c in range(2)], [pscn])
                        o0 = OFFC - 8 * qb
                        VTT("dve", scsb[:, 0:ncol], psc, maskc[:, o0:o0 + ncol], ALU.add, [pscn, "n_maskc"], ["a_scsb"])
                        ACT(pun_[:, 0:ncol], scsb[:, 0:ncol], AF.Exp, ["a_scsb", "a_den"], ["a_pun", "a_den%d" % h],
                            accum_out=den[:, h:h + 1])
                        TS("dve", rden[:, h:h + 1], den[:, h:h + 1], 1e-30, None, ALU.max, None, ["a_den%d" % h], ["a_rden%d" % h])
                        RECIP(rden[:, h:h + 1], rden[:, h:h + 1], ["a_rden%d" % h], ["a_rden%d" % h])
                        TS("dve", pbf[:, 0:ncol], pun_[:, 0:ncol], rden[:, h:h + 1], None, ALU.mult, None,
                           ["a_pun", "a_rden%d" % h], ["a_pbf"])
                        STT("pool", psumh[:, 0:ncol], pun_[:, 0:ncol], rden[:, h:h + 1], psumh[:, 0:ncol], ALU.mult, ALU.add,
                            ["a_pun", "a_rden%d" % h, "a_psumh"], ["a_psumh"])
                        for c in range(nch):
                            ptp = pbank[:, 512 + c * 128:512 + (c + 1) * 128]
                            TR(ptp, pbf[:, c * 128:(c + 1) * 128], ["a_pbf"], ["pbank"])
                            CP("act", pTsb[:, c, :], ptp, ["pbank"], ["a_pT%d" % c])
                        for c in range(nch):
                            MM(poc[:, h, :], pTsb[:, c, :], vc[:, g, c, :], (c == 0 and h == 0), c == nch - 1,
                               ["a_pT%d" % c, "n_vc%d_%d" % (g, c)], [pocn], sgc=True)
                    p4 = psumh[:].rearrange("p (j r) -> p j r", r=4)
                    VTT("dve", imp[:], p4[:, :, 0], p4[:, :, 1], ALU.add, ["a_psumh"], ["a_imp"])
                    VTT("dve", imp[:], imp[:], p4[:, :, 2], ALU.add, ["a_imp", "a_psumh"], ["a_imp"])
                    STT("dve", imp[:], p4[:, :, 3], 0.5, imp[:], ALU.mult, ALU.add, ["a_psumh", "a_imp"], ["a_imp"])
                    STT("dve", imp[:, 1:64], p4[:, 0:63, 3], 0.5, imp[:, 1:64], ALU.mult, ALU.add, ["a_psumh", "a_imp"], ["a_imp"])
                    v0 = OFFV - 2 * qb
                    VTT("dve", prio[:], imp[:], atab[:, v0:v0 + 64], ALU.mult, ["a_imp", "n_atab"], ["a_prio"])
                    VTT("dve", prio[:], prio[:], btab[:, v0:v0 + 64], ALU.add, ["a_prio", "n_btab"], ["a_prio"])
                    MEMSET("dve", prio[:, 0:1], 2e4, ["a_prio"])
                    P.op("dve", lambda e: e.max(out=mx[0][:], in_=prio[:]), ["a_prio"], ["a_mx0"])
                    P.op("dve", lambda e: e.match_replace(out=wk[:], in_to_replace=mx[0][:], in_values=prio[:], imm_value=-1e9),
                         ["a_prio", "a_mx0"], ["a_wk"])
                    P.op("dve", lambda e: e.max(out=mx[1][:], in_=wk[:]), ["a_wk"], ["a_mx1"])
                    TS("dve", m1[:], prio[:], mx[1][:, 7:8], None, ALU.is_ge, None, ["a_prio", "a_mx1"], ["a_m1"])
                    TS("dve", m2[:], prio[:], -0.5, None, ALU.is_gt, None, ["a_prio"], ["a_m2"])
                    VTT("dve", m1[:], m1[:], m2[:], ALU.mult, ["a_m1", "a_m2"], ["a_m1"])
                    TS("dve", nm[:, 64:128], m1[:], -NEG, NEG, ALU.mult, ALU.add, ["a_m1"], ["a_nm"])
                    pnm = pbank[:, 768:896]
                    TR(pnm, nm[:], ["a_nm", "n_nm0"], ["pbank"])
                    CP("act", qaug[g][64:128, :].rearrange("p (h q) -> p h q", h=4),
                       pnm[64:128, :].unsqueeze(1).to_broadcast([64, 4, 128]), ["pbank"], ["a_qaug%d_m" % g])
                    pos_, posn = banks[2][:, 0:260].rearrange("p (h d) -> p h d", h=4), bn[2]
                    qa_r = ["a_qaug%d_q" % g, "a_qaug%d_m" % g]
                    step = 0
                    for c in range(qb + 1):
                        pst, pstn = banks[4 + step % 2][:], bn[4 + step % 2]
                        pt_, ptn_ = PT[step % 3], "a_PT%d" % (step % 3)
                        step += 1
                        diag = (c == qb)
                        MM(pst, kselT[g][:, c * 128:(c + 1) * 128], qaug[g][:, :], True, not diag,
                           qa_r + ["n_kT0_%d_%d" % (g, c), "n_erows%d" % g], [pstn])
                        if diag:
                            MM(pst, ident_bf[:], tri4[:].rearrange("p h q -> p (h q)"), False, True,
                               ["ident_bf", "n_tri4"], [pstn])
                        ACT(pt_[:], pst, AF.Exp, [pstn], [ptn_])
                        for h in range(4):
                            MM(pos_[:, h, :], pt_[:, h * 128:(h + 1) * 128], vsel[:, g, c, :], (c == 0 and h == 0), c == qb,
                               [ptn_, "n_vsel_%d" % c, "n_vsel1"], [posn], sgc=True)
                    pow_, pown = banks[3][:, 0:260].rearrange("p (h d) -> p h d", h=4), bn[3]
                    c0 = max(0, qb - 4)
                    for c in range(c0, qb + 1):
                        pst, pstn = banks[4 + step % 2][:], bn[4 + step % 2]
                        pt_, ptn_ = PT[step % 3], "a_PT%d" % (step % 3)
                        step += 1
                        special = (c == qb) or (c == qb - 4)
                        MM(pst, kwinT[g][0:64, c * 128:(c + 1) * 128], qaug[g][0:64, :], True, not special,
                           ["a_qaug%d_q" % g, "n_kT1_%d_%d" % (g, c)], [pstn])
                        if special:
                            mk, mkn = (tri4, "n_tri4") if c == qb else (anti4, "n_anti4")
                            MM(pst, ident_bf[:], mk[:].rearrange("p h q -> p (h q)"), False, True,
                               ["ident_bf", mkn], [pstn])
                        ACT(pt_[:], pst, AF.Exp, [pstn], [ptn_])
                        for h in range(4):
                            MM(pow_[:, h, :], pt_[:, h * 128:(h + 1) * 128], vwin[:, g, c, :], (c == c0 and h == 0), c == qb,
                               [ptn_, "n_vwin_%d" % c, "n_vwin1"], [pown], sgc=True)
                    CP("dve", coef[:, 0, :], g4[:, g, :, 0], ["a_gsig"], ["a_coef0"])
                    RECIP(coef[:, 1, :], pos_[:, :, 64], [posn], ["a_coef1"])
                    VTT("dve", coef[:, 1, :], coef[:, 1, :], g4[:, g, :, 1], ALU.mult, ["a_coef1", "a_gsig"], ["a_coef1"])
                    RECIP(coef[:, 2, :], pow_[:, :, 64], [pown], ["a_coef2"])
                    VTT("dve", coef[:, 2, :], coef[:, 2, :], g4[:, g, :, 2], ALU.mult, ["a_coef2", "a_gsig"], ["a_coef2"])
                    for h in range(4):
                        TS("dve", of32[:], poc[:, h, :], coef[:, 0, h:h + 1], None, ALU.mult, None, [pocn, "a_coef0"], ["a_of32"])
                        STT("dve", of32[:], pos_[:, h, 0:64], coef[:, 1, h:h + 1], of32[:], ALU.mult, ALU.add,
                            [posn, "a_coef1", "a_of32"], ["a_of32"])
                        STT("dve", otm[:, 4 * g + h, :], pow_[:, h, 0:64], coef[:, 2, h:h + 1], of32[:], ALU.mult, ALU.add,
                            [pown, "a_coef2", "a_of32"], ["a_otm%d" % (4 * g + h)])
                for ct in range(4):
                    TR(pq[:, ct * 128:(ct + 1) * 128], otm[:, 2 * ct:2 * ct + 2, :].rearrange("p h d -> p (h d)"),
                       ["a_otm%d" % (2 * ct), "a_otm%d" % (2 * ct + 1)], ["pbank"])
                CP("act", ocsb[:].rearrange("p c q -> p (c q)"), pq[:, :], ["pbank" for ct in range(4)], ["a_ocsb"])
                DMA("sp", ocT_d.rearrange("(ct p) s -> p ct s", p=128)[:, :, qb * 128:(qb + 1) * 128], ocsb[:], ["a_ocsb"],
                    [("dram", "ocT", qb)])

        def mixer_out(l, xr, xname):
            P.barrier(); FA.reset(); BA.reset()
            gain = gains["norm_mixT"]
            xt = [FA.take([128, NFT, TT]) for _ in range(2)]
            rstd = FA.take([128, TT])
            sga = FA.take([128, TT]); sgb = FA.take([128, TT]); sgg = FA.take([128, TT]); ya = FA.take([128, TT]); yb = FA.take([128, TT])
            sq = BA.take([128, NFT, TT]); hT = BA.take([128, NFT, TT])
            wg = BA.take([128, NFT, 2048]); wout = BA.take([128, NFT, D_MODEL])
            wglu = [BA.take([128, 4, 256]) for _ in range(2)]; wup = [BA.take([128, 4, 128]) for _ in range(2)]
            yt_ = BA.take([128, 4, TT]); oct_ = BA.take([128, 4, TT]); mg = BA.take([128, NFT, TT])
            wv = D["w_in"][l].rearrange("(kt p) c -> p kt c", p=128)
            wov = D["w_out"][l].rearrange("(kt p) c -> p kt c", p=128)
            wgv = D["ssm_w_glu"][l].rearrange("(kt p) c -> p kt c", p=128)
            wuv = D["nsa_w_up"][l].rearrange("(kt p) c -> p kt c", p=128)
            for kt in range(NFT):
                DMA("pool", wg[:, kt, :], wv[:, kt, GA0:NCOL], (), ["o_wg%d" % kt])
                DMA("pool", wout[:, kt, :], wov[:, kt, :], (), ["o_wout%d" % kt])
            wgr = ["o_wg%d" % kt for kt in range(NFT)]
            wor = ["o_wout%d" % kt for kt in range(NFT)]
            for tt in range(SEQ // TT):
                tok0 = tt * TT
                xb, xn = xt[tt % 2], "o_xt%d" % (tt % 2)
                DMA("sp", xb[:], xview(xr)[:, :, tok0:tok0 + TT], [dtile(xname, o, tok0) for o in range(NFT)], [xn])
                norm_tile("o_", xb, xn, gain[:, l, :], "g_norm_mixT", hT, "o_hT", sq, rstd, banks[0][:], bn[0])
                hr = ["o_hT_%d" % ft for ft in range(NFT)]
                DMA("sp", yt_[:], yT_d.rearrange("(kt p) s -> p kt s", p=128)[:, :, tok0:tok0 + TT],
                    [("dram", "yT", bt) for bt in range(4)], ["o_yt"])
                DMA("sp", oct_[:], ocT_d.rearrange("(kt p) s -> p kt s", p=128)[:, :, tok0:tok0 + TT],
                    [("dram", "ocT", qb) for qb in range(tok0 // 128, tok0 // 128 + 4)], ["o_oct"])
                for c in range(NFT):
                    wl, wln = wglu[c % 2], "o_wglu%d" % (c % 2)
                    wu_, wun = wup[c % 2], "o_wup%d" % (c % 2)
                    DMA("pool", wl[:, :, 0:128], wgv[:, :, c * 128:(c + 1) * 128], (), [wln + "v"])
                    DMA("pool", wl[:, :, 128:256], wgv[:, :, 1024 + c * 128:1024 + (c + 1) * 128], (), [wln + "g"])
                    DMA("pool", wu_[:], wuv[:, :, c * 128:(c + 1) * 128], (), [wun])
                    pga, pgb, pv_, pgt, pyb = banks[1][:], banks[2][:], banks[3][:], banks[4][:], banks[5][:]
                    for kt in range(NFT):
                        MM(pga, wg[:, kt, c * 128:(c + 1) * 128], hT[:, kt, :], kt == 0, kt == NFT - 1, hr + wgr, [bn[1]])
                    for kt in range(NFT):
                        MM(pgb, wg[:, kt, 1024 + c * 128:1024 + (c + 1) * 128], hT[:, kt, :], kt == 0, kt == NFT - 1, hr + wgr, [bn[2]])
                    for kt in range(4):
                        MM(pv_, wl[:, kt, 0:128], yt_[:, kt, :], kt == 0, kt == 3, [wln + "v", "o_yt"], [bn[3]])
                    for kt in range(4):
                        MM(pgt, wl[:, kt, 128:256], yt_[:, kt, :], kt == 0, kt == 3, [wln + "g", "o_yt"], [bn[4]])
                    for kt in range(4):
                        MM(pyb, wu_[:, kt, :], oct_[:, kt, :], kt == 0, kt == 3, [wun, "o_oct"], [bn[5]])
                    ACT(sga[:], pga, AF.Sigmoid, [bn[1]], ["o_sga"])
                    ACT(sgb[:], pgb, AF.Sigmoid, [bn[2]], ["o_sgb"])
                    ACT(sgg[:], pgt, AF.Sigmoid, [bn[4]], ["o_sgg"])
                    VTT("dve", ya[:], pv_, sgg[:], ALU.mult, [bn[3], "o_sgg"], ["o_ya"])
                    VTT("pool", ya[:], ya[:], sga[:], ALU.mult, ["o_ya", "o_sga"], ["o_ya"])
                    VTT("dve", yb[:], pyb, sgb[:], ALU.mult, [bn[5], "o_sgb"], ["o_yb"])
                    VTT("pool", mg[:, c, :], ya[:], yb[:], ALU.add, ["o_ya", "o_yb"], ["o_mg%d" % c])
                mgr = ["o_mg%d" % c for c in range(NFT)]
                for ot in range(NFT):
                    po, pon = banks[6][:] if ot % 2 == 0 else banks[0][:], bn[6] if ot % 2 == 0 else bn[0]
                    for c in range(NFT):
                        MM(po, wout[:, c, ot * 128:(ot + 1) * 128], mg[:, c, :], c == 0, c == NFT - 1, mgr + wor, [pon])
                    VTT("dve", xb[:, ot, :], po, xb[:, ot, :], ALU.add, [pon, xn] + hr, [xn])
                DMA("sp", xview(xr)[:, :, tok0:tok0 + TT], xb[:], [xn], [dtile(xname, o, tok0) for o in range(NFT)])

        cur, cname = xT_in, "xin"
        for l in range(depth):
            last = (l == depth - 1)
            if "ffn1" in stages:
                ffn(l, cur, cname, xres, "xres", "norm_ffn1T", D["ffn1_wi"], D["ffn1_wo"])
                cur, cname = xres, "xres"
            if "mix" in stages:
                if "proj" in mix_parts:
                    mixer_proj(l, cur, cname)
                if "s5" in mix_parts:
                    s5(l)
                if "nsa" in mix_parts:
                    nsa(l)
                if "out" in mix_parts:
                    mixer_out(l, xres, "xres")
            if "ffn2" in stages:
                dst, dn = (outT, "out") if last else (xres, "xres")
                ffn(l, cur, cname, dst, dn, "norm_ffn2T", D["ffn2_wi"], D["ffn2_wo"])
                cur, cname = xres, "xres"
        final = [(P.dsem[i], 16 * P.dcnt[i]) for i in range(P.NDMA)]
        P.finish(final)
        print("instructions:", P.ninst, "sems:", P.nsem, flush=True)
    return nc


_CACHE = {}


def kernel(**inputs):
    x = np.asarray(inputs["x"], dtype=np.float32)
    if "nc" not in _CACHE:
        _CACHE["nc"] = build_program()
    nc = _CACHE["nc"]
    shared = layout_params(inputs)
    shared.update(host_constants())
    in_maps = []
    ncores = DBG["ncores"]
    for c in range(ncores):
        m = dict(shared)
        m["xT"] = np.ascontiguousarray(x[c // 2].T)
        in_maps.append(m)
    res = run_bass_kernel_spmd(nc, in_maps, core_ids=list(range(ncores)))
    _CACHE["last"] = res
    out = np.stack([np.ascontiguousarray(res.results[(2 * b) % ncores]["outT"].T) for b in range(BATCH)], axis=0)
    return out.astype(np.float32)
```

```python
import numpy as np
from contextlib import ExitStack
import concourse.bass as bass
import concourse.mybir as mybir
from concourse.bass_utils import run_bass_kernel_spmd

F32 = mybir.dt.float32
BF16 = mybir.dt.bfloat16
AF = mybir.ActivationFunctionType
ALU = mybir.AluOpType
AX = mybir.AxisListType

D_MODEL = 1024
SEQ = 4096
BATCH = 4
DEPTH = 4
D_FF = 2816
NFT = D_MODEL // 128
NFF = D_FF // 128
TT = 512
ST = 1024
EPS = 1e-6

ENGS = ("pe", "act", "dve", "pool", "sp")
NO_SAME_ENGINE_WAIT = False


class Prog:
    EPOCH = 20000
    NDMA = 24

    def __init__(self, nc, es):
        self.nc = nc
        self.es = es
        self.streams = {e: [] for e in ENGS}
        self.cnt = {e: 0 for e in ENGS}
        self.cur = {}
        self.nsem = 0
        self.pe_sems = []
        self.own_sems = {e: [] for e in ENGS}
        for e in ENGS:
            self.cur[e] = self._newsem(e)
            self.own_sems[e].append(self.cur[e])
        self.pe_sems.append(self.cur["pe"])
        self.waited = {e: {} for e in ENGS}
        self.buf = {}
        self.xacc = {}
        self.dsem = [self._newsem("dma%d" % i) for i in range(self.NDMA)]
        self.dcnt = [0] * self.NDMA
        self.dnext = 0
        self.ninst = 0

    def _newsem(self, tag):
        self.nsem += 1
        return self.es.enter_context(self.nc.semaphore("s_%s_%d" % (tag, self.nsem)))

    def _wait(self, eng, ev):
        if ev is None:
            return
        sem, val = ev
        if eng == "pe" and any(sem is s for s in self.pe_sems):
            return
        if NO_SAME_ENGINE_WAIT and any(sem is s for s in self.own_sems[eng]):
            return
        w = self.waited[eng]
        if w.get(id(sem), 0) >= val:
            return
        w[id(sem)] = val
        self.streams[eng].append(("wait", sem, val))

    def _deps(self, eng, reads, writes):
        for b in reads:
            st = self.buf.get(b)
            if st is not None:
                self._wait(eng, st["w"])
        for b in writes:
            st = self.buf.get(b)
            if st is not None:
                self._wait(eng, st["w"])
                for ev in st["r"].values():
                    self._wait(eng, ev)

    def _mark(self, key, ev, reads, writes):
        for b in reads:
            st = self.buf.setdefault(b, {"w": None, "r": {}})
            st["r"][key] = ev
        for b in writes:
            self.buf[b] = {"w": ev, "r": {}}

    def _excl(self, eng, names):
        out = []
        for b in names:
            if isinstance(b, str) and (b.startswith("bank") or b.startswith("pbank")):
                st = self.xacc.setdefault(b, {})
                for e2, ev in st.items():
                    if e2 != eng:
                        self._wait(eng, ev)
                out.append(st)
        return out

    def op(self, eng, fn, reads=(), writes=()):
        self._deps(eng, reads, writes)
        xs = self._excl(eng, list(reads) + list(writes))
        if self.cnt[eng] >= self.EPOCH:
            self.cur[eng] = self._newsem(eng)
            self.own_sems[eng].append(self.cur[eng])
            self.cnt[eng] = 0
            if eng == "pe":
                self.pe_sems.append(self.cur[eng])
        self.cnt[eng] += 1
        sem = self.cur[eng]
        ev = (sem, self.cnt[eng])
        self.streams[eng].append(("op", fn, sem))
        self._mark(eng, ev, reads, writes)
        for st in xs:
            st[eng] = ev
        self.ninst += 1
        return ev

    def dma(self, q, out, in_, reads=(), writes=()):
        self._deps(q, reads, writes)
        i = self.dnext
        self.dnext = (self.dnext + 1) % self.NDMA
        sem = self.dsem[i]
        self._wait(q, (sem, 16 * self.dcnt[i]))
        self.dcnt[i] += 1
        ev = (sem, 16 * self.dcnt[i])
        self.streams[q].append(("dma", out, in_, sem))
        self._mark(("dma", i, self.dcnt[i]), ev, reads, writes)
        self.ninst += 1
        return ev

    def barrier(self):
        evs = [(self.cur[e], self.cnt[e]) for e in ENGS if self.cnt[e] > 0]
        evs += [(self.dsem[i], 16 * self.dcnt[i]) for i in range(self.NDMA) if self.dcnt[i] > 0]
        for e in ENGS:
            for ev in evs:
                self._wait(e, ev)
        self.buf = {}
        self.xacc = {}

    def finish(self, final_events):
        for ev in final_events:
            self._wait("sp", ev)
        nc = self.nc
        block = self.es.enter_context(nc.Block())

        def replay(engobj, stream):
            for it in stream:
                if it[0] == "wait":
                    engobj.wait_ge(it[1], it[2])
                elif it[0] == "op":
                    it[1](engobj).then_inc(it[2], 1)
                else:
                    engobj.dma_start(out=it[1], in_=it[2]).then_inc(it[3], 16)

        @block.tensor
        def _(e):
            replay(e, self.streams["pe"])

        @block.scalar
        def _(e):
            replay(e, self.streams["act"])

        @block.vector
        def _(e):
            replay(e, self.streams["dve"])

        @block.gpsimd
        def _(e):
            replay(e, self.streams["pool"])

        @block.sync
        def _(e):
            replay(e, self.streams["sp"])


NCOL = 3864
DBG = {"ncores": 8, "nqb": 32}
QKV0, QKVW = 512, 1304
GA0, GB0 = 1816, 2840
NEG = -30000.0
OFFC = 248
OFFV = 62
F32N = 15360
BFN = 49152
TWO_PI = float(2 * np.pi)
EVEC = [float(7 - k) for k in range(15)] + [float(k) for k in range(9)] + [float(8 * 2 ** s) for s in range(9)]
NE = len(EVEC)


def host_constants():
    c = {}
    pos = np.arange(SEQ, dtype=np.float32)
    inv_freq = (1.0 / (10000.0 ** (np.arange(32, dtype=np.float32) / 32))).astype(np.float32)
    ang = pos[:, None] * inv_freq[None, :]
    c["cs_tok"] = np.concatenate([np.cos(ang), np.sin(ang)], axis=1).astype(np.float32)
    cend = (np.arange(256) * 16 + 31).astype(np.float32)
    angc = cend[:, None] * inv_freq[None, :]
    c["cs_cmp"] = np.concatenate([np.cos(angc), np.sin(angc)], axis=1).astype(np.float32)
    sel = np.zeros((128, 8, 8, 128), np.float32)
    for g8 in range(8):
        for i in range(8):
            for cin in range(16):
                sel[g8 * 16 + cin, g8, i, i * 16 + cin] = 1.0
    c["sel"] = sel.reshape(128, 64, 128)
    c["selT"] = np.ascontiguousarray(sel.transpose(3, 1, 2, 0)).reshape(128, 64, 128)
    ii = np.arange(128) // 16
    c["tmask"] = (ii[None, :] >= ii[:, None]).astype(np.float32)
    c["identf"] = np.eye(128, dtype=np.float32)
    J = np.zeros((128, 128), np.float32)
    for k in range(128):
        J[k, (k + 64) % 128] = 1.0
    c["jmat"] = J
    sg = np.ones((128, 1), np.float32); sg[64:] = -1.0
    c["sgn"] = sg
    c["evec"] = np.broadcast_to(np.asarray(EVEC, np.float32)[None, :], (128, NE)).copy()
    tq = np.arange(128)
    m = np.arange(512)
    c["maskc"] = np.where(16 * (m[None, :] - OFFC) + 31 <= tq[:, None], 0.0, NEG).astype(np.float32)
    curp = (tq >= 64).astype(np.int64)
    mm = np.arange(128)
    jp = np.broadcast_to(mm[None, :] - OFFV, (128, 128))
    A = (jp < (curp[:, None] - 1)).astype(np.float32)
    Bt = np.zeros((128, 128), np.float32)
    forced = (jp == curp[:, None]) | (jp == curp[:, None] - 1)
    invalid = jp > curp[:, None]
    Bt[forced] = (1e4 + (jp + 70))[forced]
    Bt[invalid] = (-1.0 - (jp + 70))[invalid]
    c["atab"] = A
    c["btab"] = Bt
    kk = np.arange(128)
    c["tri"] = np.where(kk[:, None] <= tq[None, :], 0.0, NEG).astype(np.float32)
    c["anti"] = np.where(kk[:, None] > tq[None, :], 0.0, NEG).astype(np.float32)
    key = np.arange(SEQ)
    c["erows"] = (key[None, :] // 64 == np.arange(64)[:, None]).astype(np.float32)
    return c


CONST_SHAPES = {"cs_tok": [SEQ, 64], "cs_cmp": [256, 64], "sel": [128, 64, 128], "selT": [128, 64, 128],
                "tmask": [128, 128], "identf": [128, 128], "jmat": [128, 128], "sgn": [128, 1],
                "evec": [128, NE], "maskc": [128, 512], "atab": [128, 128], "btab": [128, 128],
                "tri": [128, 128], "anti": [128, 128], "erows": [64, SEQ]}


def layout_params(inp):
    L = DEPTH
    f = lambda a: np.ascontiguousarray(np.asarray(a, np.float32))
    o = {}
    trn = lambda a: f(np.asarray(a, np.float32).reshape(L, NFT, 128).transpose(2, 0, 1))
    o["norm_ffn1T"] = trn(inp["norm_ffn1"]); o["norm_ffn2T"] = trn(inp["norm_ffn2"]); o["norm_mixT"] = trn(inp["norm_mix"])
    for k in ("ffn1_wi", "ffn1_wo", "ffn2_wi", "ffn2_wo", "w_in", "ssm_w_glu", "nsa_w_up", "w_out", "cmp_w1", "cmp_w2"):
        o[k] = f(inp[k])
    dup = lambda a: np.concatenate([a, a], axis=0)
    o["lamre"] = f(dup(np.asarray(inp["ssm_lambda_re"]).transpose(2, 0, 1)))
    o["lamim"] = f(dup(np.asarray(inp["ssm_lambda_im"]).transpose(2, 0, 1)))
    o["logdt"] = f(np.broadcast_to(np.asarray(inp["ssm_log_dt"])[None], (128, L, 32)))
    o["bre"] = f(dup(np.asarray(inp["ssm_b_re"]).transpose(2, 0, 1, 3)))
    o["bim"] = f(dup(np.asarray(inp["ssm_b_im"]).transpose(2, 0, 1, 3)))
    o["cre"] = f(dup(np.asarray(inp["ssm_c_re"]).transpose(3, 0, 1, 2)))
    o["cim"] = f(dup(np.asarray(inp["ssm_c_im"]).transpose(3, 0, 1, 2)))
    dsk = np.asarray(inp["ssm_d"])
    o["dvec"] = f(np.tile(dsk.transpose(2, 0, 1), (8, 1, 1)))
    o["qgain"] = f(np.broadcast_to(np.asarray(inp["q_norm"])[None], (128, L, 64)))
    o["kgain"] = f(np.broadcast_to(np.asarray(inp["k_norm"])[None], (128, L, 3, 64)))
    o["peT"] = f(np.asarray(inp["cmp_pe"]).transpose(3, 0, 1, 2))
    return o


PARAM_SHAPES = {"norm_ffn1T": [128, DEPTH, NFT], "norm_ffn2T": [128, DEPTH, NFT], "norm_mixT": [128, DEPTH, NFT],
                "ffn1_wi": [DEPTH, D_MODEL, 2 * D_FF], "ffn1_wo": [DEPTH, D_FF, D_MODEL],
                "ffn2_wi": [DEPTH, D_MODEL, 2 * D_FF], "ffn2_wo": [DEPTH, D_FF, D_MODEL],
                "w_in": [DEPTH, D_MODEL, NCOL], "ssm_w_glu": [DEPTH, 512, 2048], "nsa_w_up": [DEPTH, 512, 1024],
                "w_out": [DEPTH, D_MODEL, D_MODEL], "cmp_w1": [DEPTH, 2, 2048, 128], "cmp_w2": [DEPTH, 2, 128, 64],
                "lamre": [128, DEPTH, 32], "lamim": [128, DEPTH, 32], "logdt": [128, DEPTH, 32],
                "bre": [128, DEPTH, 32, 16], "bim": [128, DEPTH, 32, 16], "cre": [128, DEPTH, 32, 16], "cim": [128, DEPTH, 32, 16],
                "dvec": [128, DEPTH, 32], "qgain": [128, DEPTH, 64], "kgain": [128, DEPTH, 3, 64], "peT": [64, DEPTH, 2, 32]}


def interleave(gens):
    gens = [g for g in gens if g is not None]
    while gens:
        for g in list(gens):
            try:
                next(g)
            except StopIteration:
                gens.remove(g)


class Arena:
    def __init__(self, ap, n, tag):
        self.ap, self.n, self.tag, self.off = ap, n, tag, 0

    def reset(self):
        self.off = 0

    def take(self, shape):
        n = 1
        for v in shape[1:]:
            n *= v
        assert self.off + n <= self.n, (self.tag, self.off, n, self.n)
        v = self.ap[:, self.off:self.off + n]
        self.off += n
        if len(shape) == 3:
            v = v.rearrange("p (a b) -> p a b", a=shape[1])
        elif len(shape) == 4:
            v = v.rearrange("p (a b c) -> p a b c", a=shape[1], b=shape[2])
        return v


def build_program(depth=DEPTH, stages=("ffn1", "mix", "ffn2"), debug=False, mix_parts=("proj", "s5", "nsa", "out")):
    nc = bass.Bass("TRN2", target_bir_lowering=False)
    es = ExitStack()
    D = {}
    for k, shp in list(PARAM_SHAPES.items()) + list(CONST_SHAPES.items()):
        D[k] = nc.dram_tensor(k, list(shp), F32, kind="ExternalInput").ap()
    xT_in = nc.dram_tensor("xT", [D_MODEL, SEQ], F32, kind="ExternalInput").ap()
    outT = nc.dram_tensor("outT", [D_MODEL, SEQ], F32, kind="ExternalOutput").ap()
    skind = "ExternalOutput" if debug else "Internal"
    xres = nc.dram_tensor("xres", [D_MODEL, SEQ], F32, kind=skind).ap()
    uT_d = nc.dram_tensor("uT_d", [512, SEQ], BF16, kind=skind).ap()
    yT_d = nc.dram_tensor("yT_d", [512, SEQ], BF16, kind=skind).ap()
    ocT_d = nc.dram_tensor("ocT_d", [512, SEQ], BF16, kind=skind).ap()
    qkv_d = nc.dram_tensor("qkv_d", [SEQ, QKVW], F32, kind=skind).ap()

    with es:
        P = Prog(nc, es)
        sbt = lambda name, shape, dt: es.enter_context(nc.sbuf_tensor(name, list(shape), dt))
        FA = Arena(sbt("f32arena", [128, F32N], F32), F32N, "f32")
        BA = Arena(sbt("bf16arena", [128, BFN], BF16), BFN, "bf16")
        itmp = sbt("itmp", [128, 8 * NE], mybir.dt.int32)
        gains = {k: sbt("g_" + k, [128, DEPTH, NFT], F32) for k in ("norm_ffn1T", "norm_ffn2T", "norm_mixT")}
        ones_bf = sbt("ones_bf", [128, 128], BF16)
        ident_bf = sbt("ident_bf", [128, 128], BF16)
        identf = sbt("identf_sb", [128, 128], F32)
        banks = [es.enter_context(nc.psum_tensor("bank%d" % i, [128, 512], F32)) for i in range(6)]
        pbank = es.enter_context(nc.psum_tensor("pbank", [128, 1024], BF16))
        pbankB = es.enter_context(nc.psum_tensor("pbankB", [128, 1024], BF16))
        bn = ["bank%d" % i for i in range(6)]

        def MM(out, lhsT, rhs, st, sp, r, w, sgc=False):
            if sgc:
                P.op("pe", lambda e: e.matmul(out, lhsT, rhs, start=st, stop=sp, skip_group_check=True), r, w)
            else:
                P.op("pe", lambda e: e.matmul(out, lhsT, rhs, start=st, stop=sp), r, w)

        def TR(out, in_, r, w):
            P.op("pe", lambda e: e.transpose(out, in_, ident_bf[:]), list(r) + ["ident_bf"], w)

        def ACT(out, in_, func, r, w, **kw):
            P.op("act", lambda e: e.activation(out=out, in_=in_, func=func, **kw), r, w)

        def VTT(eng, out, a, b, op, r, w):
            P.op(eng, lambda e: e.tensor_tensor(out, a, b, op), r, w)

        def TS(eng, out, a, s1, s2, op0, op1, r, w):
            if op1 is None:
                P.op(eng, lambda e: e.tensor_scalar(out, a, s1, s2, op0=op0), r, w)
            else:
                P.op(eng, lambda e: e.tensor_scalar(out, a, s1, s2, op0=op0, op1=op1), r, w)

        def STT(eng, out, in0, scalar, in1, op0, op1, r, w):
            eng = "dve"
            P.op(eng, lambda e: e.scalar_tensor_tensor(out=out, in0=in0, scalar=scalar, in1=in1, op0=op0, op1=op1), r, w)

        def CP(eng, out, in_, r, w):
            if eng == "act":
                ACT(out, in_, AF.Copy, r, w)
            else:
                P.op(eng, lambda e: e.tensor_copy(out, in_), r, w)

        def MEMSET(eng, out, val, w):
            P.op(eng, lambda e: e.memset(out, val), (), w)

        def RECIP(out, in_, r, w):
            P.op("dve", lambda e: e.reciprocal(out, in_), r, w)

        def DMA(q, out, in_, r, w):
            return P.dma(q, out, in_, r, w)

        xview = lambda t: t.rearrange("(ft p) s -> p ft s", p=128)
        dtile = lambda name, ft, tok0: ("dram", name, ft, tok0)

        for k in gains:
            DMA("sp", gains[k][:], D[k][:, :, :], (), ["g_" + k])
        MEMSET("dve", ones_bf[:], 1.0, ["ones"])
        DMA("sp", identf[:], D["identf"][:, :], (), ["identf"])
        DMA("pool", ident_bf[:], D["identf"][:, :], (), ["ident_bf"])

        def norm_tile(pfx, xb, xn, gain_ap, gname, hdst, hname, sq, rstd, ps_n, psn_name):
            ACT(sq[:], xb[:], AF.Square, [xn], [pfx + "sq"])
            for ft in range(NFT):
                MM(ps_n, ones_bf[:], sq[:, ft, :], ft == 0, ft == NFT - 1, [pfx + "sq", "ones"], [psn_name])
            TS("dve", rstd[:], ps_n, 1.0 / D_MODEL, EPS, ALU.mult, ALU.add, [psn_name], [pfx + "rstd"])
            ACT(rstd[:], rstd[:], AF.Sqrt, [pfx + "rstd"], [pfx + "rstd"])
            RECIP(rstd[:], rstd[:], [pfx + "rstd"], [pfx + "rstd"])
            for ft in range(NFT):
                STT("dve" if ft % 2 == 0 else "pool", hdst[:, ft, :], xb[:, ft, :], gain_ap[:, ft:ft + 1], rstd[:],
                    ALU.mult, ALU.mult, [xn, pfx + "rstd", gname], [hname + "_%d" % ft])

        def ffn(l, src, sname, dst, dname, gkey, wi, wo):
            P.barrier(); FA.reset(); BA.reset()
            gain = gains[gkey]
            xt = [FA.take([128, NFT, TT]) for _ in range(2)]
            rstd = FA.take([128, TT])
            sg = [FA.take([128, TT]) for _ in range(2)]
            sq = BA.take([128, NFT, TT])
            hT = BA.take([128, NFT, ST])
            actT = BA.take([128, NFF, ST])
            wi_sb = [BA.take([128, NFT, 256]) for _ in range(3)]
            wo_sb = [BA.take([128, NFF, 128]) for _ in range(2)]
            ps_n, ps_g, ps_u, ps_o = banks[0][:], [banks[1][:], banks[2][:]], [banks[3][:], banks[4][:]], [banks[5][:], banks[0][:]]
            pon_ = [bn[5], bn[0]]
            nst, ntt = SEQ // ST, ST // TT
            wiv = wi[l].rearrange("(kt p) c -> p kt c", p=128)
            wov = wo[l].rearrange("(ft p) c -> p ft c", p=128)
            for s in range(nst):
                for t in range(ntt):
                    tok0 = s * ST + t * TT
                    xb, xn = xt[t % 2], "f_xt%d" % (t % 2)
                    DMA("sp", xb[:], xview(src)[:, :, tok0:tok0 + TT], [dtile(sname, o, tok0) for o in range(NFT)], [xn])
                    norm_tile("f_", xb, xn, gain[:, l, :], "g_" + gkey, hT[:, :, t * TT:(t + 1) * TT], "f_hT%d" % t,
                              sq, rstd, ps_n, bn[0])
                for f in range(NFF):
                    wb, wn = wi_sb[f % 3], "f_wi%d" % (f % 3)
                    DMA("pool", wb[:, :, 0:128], wiv[:, :, f * 128:(f + 1) * 128], (), [wn + "g"])
                    DMA("pool", wb[:, :, 128:256], wiv[:, :, D_FF + f * 128:D_FF + (f + 1) * 128], (), [wn + "u"])
                    for t in range(ntt):
                        pg, pgn, pu, pun = ps_g[t % 2], bn[1 + t % 2], ps_u[t % 2], bn[3 + t % 2]
                        hr = ["f_hT%d_%d" % (t, ft) for ft in range(NFT)]
                        for kt in range(NFT):
                            MM(pg, wb[:, kt, 0:128], hT[:, kt, t * TT:(t + 1) * TT], kt == 0, kt == NFT - 1, hr + [wn + "g"], [pgn])
                        for kt in range(NFT):
                            MM(pu, wb[:, kt, 128:256], hT[:, kt, t * TT:(t + 1) * TT], kt == 0, kt == NFT - 1, hr + [wn + "u"], [pun])
                        sgb, sgn = sg[t % 2], "f_sg%d" % (t % 2)
                        ACT(sgb[:], pg, AF.Silu, [pgn], [sgn])
                        TT_ = "dve"
                        VTT(TT_, actT[:, f, t * TT:(t + 1) * TT], sgb[:], pu, ALU.mult, [sgn, pun], ["f_act%d_%d" % (f, t)])
                for ot in range(NFT):
                    wb, wn = wo_sb[ot % 2], "f_wo%d" % (ot % 2)
                    DMA("pool", wb[:], wov[:, :, ot * 128:(ot + 1) * 128], (), [wn])
                    for t in range(ntt):
                        tok0 = s * ST + t * TT
                        po, pon = ps_o[t % 2], pon_[t % 2]
                        for f in range(NFF):
                            MM(po, wb[:, f, :], actT[:, f, t * TT:(t + 1) * TT], f == 0, f == NFF - 1,
                               ["f_act%d_%d" % (f, t), wn], [pon])
                        rb, rn = sg[t % 2], "f_sg%d" % (t % 2)
                        DMA("sp", rb[:], src[ot * 128:(ot + 1) * 128, tok0:tok0 + TT], [dtile(sname, ot, tok0)], [rn])
                        STT("dve", rb[:], po, 0.5, rb[:], ALU.mult, ALU.add, [pon, rn], [rn])
                        DMA("sp", dst[ot * 128:(ot + 1) * 128, tok0:tok0 + TT], rb[:], [rn], [dtile(dname, ot, tok0)])

        def mixer_proj(l, src, sname):
            P.barrier(); FA.reset(); BA.reset()
            gain = gains["norm_mixT"]
            xt = [FA.take([128, NFT, TT]) for _ in range(2)]
            rstd = FA.take([128, TT])
            qrow = [FA.take([128, QKVW]) for _ in range(2)]
            sq = BA.take([128, NFT, TT])
            hT = BA.take([128, NFT, TT])
            wq = BA.take([128, NFT, GA0])
            ub = [BA.take([128, TT]) for _ in range(2)]
            wv = D["w_in"][l].rearrange("(kt p) c -> p kt c", p=128)
            for kt in range(NFT):
                DMA("pool", wq[:, kt, :], wv[:, kt, 0:GA0], (), ["m_wq%d" % kt])
            wqr = ["m_wq%d" % kt for kt in range(NFT)]
            for tt in range(SEQ // TT):
                tok0 = tt * TT
                xb, xn = xt[tt % 2], "m_xt%d" % (tt % 2)
                DMA("sp", xb[:], xview(src)[:, :, tok0:tok0 + TT], [dtile(sname, o, tok0) for o in range(NFT)], [xn])
                norm_tile("m_", xb, xn, gain[:, l, :], "g_norm_mixT", hT, "m_hT", sq, rstd, banks[0][:], bn[0])
                hr = ["m_hT_%d" % ft for ft in range(NFT)]
                for c in range(4):
                    pb, pn = banks[1 + c % 2][:], bn[1 + c % 2]
                    for kt in range(NFT):
                        MM(pb, wq[:, kt, c * 128:(c + 1) * 128], hT[:, kt, :], kt == 0, kt == NFT - 1, hr + wqr, [pn])
                    u_, un = ub[c % 2], "m_ub%d" % (c % 2)
                    CP("act", u_[:], pb, [pn], [un])
                    DMA("sp", uT_d[c * 128:(c + 1) * 128, tok0:tok0 + TT], u_[:], [un], [("dram", "uT", c, tt)])
                for sub in range(4):
                    qr, qn = qrow[sub % 2], "m_qrow%d" % (sub % 2)
                    for bi, (c0, c1) in enumerate(((512, 1024), (1024, 1536), (1536, GA0))):
                        pb, pn = banks[3 + bi][:], bn[3 + bi]
                        for kt in range(NFT):
                            MM(pb[:, 0:c1 - c0], hT[:, kt, sub * 128:(sub + 1) * 128], wq[:, kt, c0:c1], kt == 0, kt == NFT - 1,
                               hr + wqr, [pn])
                        CP("dve" if bi != 1 else "act", qr[:, c0 - 512:c1 - 512], pb[:, 0:c1 - c0], [pn], [qn + "_%d" % bi])
                    r0 = tok0 + sub * 128
                    DMA("sp", qkv_d[r0:r0 + 128, :], qr[:], [qn + "_%d" % bi for bi in range(3)], [("dram", "qkv", r0 // 128)])

        def range_reduce(x, tmp, n, r):
            iv = itmp[:, 0:n]
            TS("dve", tmp, x, 1.0 / TWO_PI, 64.5, ALU.mult, ALU.add, [r], ["s_rtmp"])
            CP("dve", iv, tmp, ["s_rtmp"], ["s_itmp"])
            CP("dve", tmp, iv, ["s_itmp"], ["s_rtmp"])
            TS("dve", tmp, tmp, -64.0, -TWO_PI, ALU.add, ALU.mult, ["s_rtmp"], ["s_rtmp"])
            VTT("dve", x, x, tmp, ALU.add, [r, "s_rtmp"], [r])
            TS("dve", tmp, x, float(np.pi), -TWO_PI, ALU.is_gt, ALU.mult, [r], ["s_rtmp"])
            VTT("dve", x, x, tmp, ALU.add, [r, "s_rtmp"], [r])
            TS("dve", tmp, x, float(-np.pi), TWO_PI, ALU.is_lt, ALU.mult, [r], ["s_rtmp"])
            VTT("dve", x, x, tmp, ALU.add, [r, "s_rtmp"], [r])

        def s5(l):
            P.barrier(); FA.reset(); BA.reset()
            G8 = 8
            lr = FA.take([128, G8]); li = FA.take([128, G8]); ldt = FA.take([128, G8])
            dvc = FA.take([128, G8]); sgn = FA.take([128, 1]); evec = FA.take([128, NE])
            braw = [FA.take([128, G8, 16]) for _ in range(2)]
            craw = [FA.take([128, G8, 16]) for _ in range(2)]
            bbar = [FA.take([128, G8, 16]) for _ in range(2)]
            t8 = [FA.take([128, G8]) for _ in range(6)]
            marg = FA.take([128, G8, NE]); ang = FA.take([128, G8, NE]); ang2 = FA.take([128, G8, NE])
            rtmp = FA.take([128, G8, NE])
            are = FA.take([128, G8, NE]); aim = FA.take([128, G8, NE])
            S = FA.take([128, G8, 15, 16]); R = FA.take([128, G8, 9, 16])
            T1 = FA.take([128, G8, 15, 16]); T2 = FA.take([128, G8, 15, 16])
            xpad = [FA.take([128, 1024]) for _ in range(2)]
            tmaskf = FA.take([128, 128]); jmat = FA.take([128, 128])
            ttmp = FA.take([128, 128]); mtmp = FA.take([128, 128]); ms = [FA.take([128, 128]) for _ in range(2)]
            sel = BA.take([128, 64, 128]); selT = BA.take([128, 64, 128])
            utile = BA.take([128, SEQ]); ytile = BA.take([128, SEQ])
            usb = BA.take([128, 512]); xprev = BA.take([128, 512])
            yact = [BA.take([128, 512]) for _ in range(8)]
            Tg = BA.take([128, 128]); Gg = BA.take([128, 128]); Hg = BA.take([128, 128])
            DMA("pool", sel[:], D["sel"][:, :, :], (), ["s_sel"])
            DMA("pool", selT[:], D["selT"][:, :, :], (), ["s_selT"])
            DMA("sp", tmaskf[:], D["tmask"][:, :], (), ["s_tmask"])
            DMA("sp", jmat[:], D["jmat"][:, :], (), ["s_jmat"])
            DMA("sp", sgn[:], D["sgn"][:, :], (), ["s_sgn"])
            DMA("sp", evec[:], D["evec"][:, :], (), ["s_evec"])
            MEMSET("pool", xpad[0][:, 0:512], 0.0, ["s_xp0z"])
            MEMSET("pool", xpad[1][:, 0:512], 0.0, ["s_xp1z"])
            for bt in range(4):
                g0 = bt * 8
                gs = slice(g0, g0 + 8)
                DMA("sp", lr[:], D["lamre"][:, l, gs], (), ["s_lr"])
                DMA("sp", li[:], D["lamim"][:, l, gs], (), ["s_li"])
                DMA("sp", ldt[:], D["logdt"][:, l, gs], (), ["s_ldt"])
                DMA("sp", dvc[:], D["dvec"][:, l, gs], (), ["s_dvc"])
                DMA("sp", braw[0][:], D["bre"][:, l, gs, :], (), ["s_bre"])
                DMA("sp", braw[1][:], D["bim"][:, l, gs, :], (), ["s_bim"])
                DMA("sp", craw[0][:], D["cre"][:, l, gs, :], (), ["s_cre"])
                DMA("sp", craw[1][:], D["cim"][:, l, gs, :], (), ["s_cim"])
                DMA("sp", utile[:], uT_d[bt * 128:(bt + 1) * 128, :], [("dram", "uT", bt, tt) for tt in range(8)], ["s_utile"])
                dt_, lrdt, lidt, inv, fre, fim = t8
                ACT(dt_[:], ldt[:], AF.Exp, ["s_ldt"], ["s_dt"])
                VTT("dve", lrdt[:], lr[:], dt_[:], ALU.mult, ["s_lr", "s_dt"], ["s_lrdt"])
                VTT("dve", lidt[:], li[:], dt_[:], ALU.mult, ["s_li", "s_dt"], ["s_lidt"])
                bc_e = lambda t: t[:].unsqueeze(2).to_broadcast([128, G8, NE])
                ev_b = evec[:].unsqueeze(1).to_broadcast([128, G8, NE])
                VTT("dve", marg[:], bc_e(lrdt), ev_b, ALU.mult, ["s_lrdt", "s_evec"], ["s_marg"])
                VTT("dve", ang[:], bc_e(lidt), ev_b, ALU.mult, ["s_lidt", "s_evec"], ["s_ang"])
                TS("dve", ang2[:], ang[:], float(np.pi / 2), None, ALU.add, None, ["s_ang"], ["s_ang2"])
                ACT(marg[:], marg[:], AF.Exp, ["s_marg"], ["s_marg"])
                fl = lambda t: t[:].rearrange("p g e -> p (g e)")
                range_reduce(fl(ang), fl(rtmp), G8 * NE, "s_ang")
                range_reduce(fl(ang2), fl(rtmp), G8 * NE, "s_ang2")
                ACT(ang[:], ang[:], AF.Sin, ["s_ang"], ["s_ang"])
                ACT(ang2[:], ang2[:], AF.Sin, ["s_ang2"], ["s_ang2"])
                VTT("dve", are[:], marg[:], ang2[:], ALU.mult, ["s_marg", "s_ang2"], ["s_are"])
                VTT("dve", aim[:], marg[:], ang[:], ALU.mult, ["s_marg", "s_ang"], ["s_aim"])
                a1r, a1i = are[:, :, 6], aim[:, :, 6]
                u0, u1 = T1[:, :, 0, 0], T1[:, :, 0, 1]
                VTT("dve", inv[:], lr[:], lr[:], ALU.mult, ["s_lr"], ["s_inv"])
                VTT("dve", u0, li[:], li[:], ALU.mult, ["s_li", "s_T1h", "s_T2h"], ["s_T1"])
                VTT("dve", inv[:], inv[:], u0, ALU.add, ["s_inv", "s_T1"], ["s_inv"])
                RECIP(inv[:], inv[:], ["s_inv"], ["s_inv"])
                TS("dve", u0, a1r, -1.0, None, ALU.add, None, ["s_are"], ["s_T1"])
                VTT("dve", fre[:], u0, lr[:], ALU.mult, ["s_T1", "s_lr"], ["s_fre"])
                VTT("dve", u1, a1i, li[:], ALU.mult, ["s_aim", "s_li"], ["s_T1b"])
                VTT("dve", fre[:], fre[:], u1, ALU.add, ["s_fre", "s_T1b"], ["s_fre"])
                VTT("dve", fre[:], fre[:], inv[:], ALU.mult, ["s_fre", "s_inv"], ["s_fre"])
                VTT("dve", fim[:], a1i, lr[:], ALU.mult, ["s_aim", "s_lr"], ["s_fim"])
                VTT("dve", u1, u0, li[:], ALU.mult, ["s_T1", "s_li"], ["s_T1b"])
                VTT("dve", fim[:], fim[:], u1, ALU.subtract, ["s_fim", "s_T1b"], ["s_fim"])
                VTT("dve", fim[:], fim[:], inv[:], ALU.mult, ["s_fim", "s_inv"], ["s_fim"])
                bc_c = lambda t: t[:].unsqueeze(2).to_broadcast([128, G8, 16])
                w0, w1_ = T2[:, :, 0, :], T2[:, :, 1, :]
                VTT("dve", w0, bc_c(fre), braw[0][:], ALU.mult, ["s_fre", "s_bre"], ["s_T2"])
                VTT("dve", w1_, bc_c(fim), braw[1][:], ALU.mult, ["s_fim", "s_bim"], ["s_T2b"])
                VTT("dve", bbar[0][:], w0, w1_, ALU.subtract, ["s_T2", "s_T2b"], ["s_bbr"])
                VTT("dve", w0, bc_c(fre), braw[1][:], ALU.mult, ["s_fre", "s_bim"], ["s_T2"])
                VTT("dve", w1_, bc_c(fim), braw[0][:], ALU.mult, ["s_fim", "s_bre"], ["s_T2b"])
                VTT("dve", bbar[1][:], w0, w1_, ALU.add, ["s_T2", "s_T2b"], ["s_bbi"])
                lo, hi = slice(0, 64), slice(64, 128)
                Ab = lambda t, h, k0, k1, n: t[h, :, k0:k1].unsqueeze(3).to_broadcast([64, G8, k1 - k0, 16])
                Zb = lambda t, h, n: t[h, :, :].unsqueeze(2).to_broadcast([64, G8, n, 16])
                XD = ["s_bbr", "s_bbi", "s_fre", "s_fim", "s_T1", "s_T1b", "s_T2", "s_T2b"]
                VTT("dve", T1[lo], Ab(are, lo, 0, 15, 15), Zb(bbar[0], lo, 15), ALU.mult, ["s_are"] + XD, ["s_T1"])
                VTT("pool", T2[lo], Ab(aim, lo, 0, 15, 15), Zb(bbar[1], lo, 15), ALU.mult, ["s_aim"] + XD, ["s_T2"])
                VTT("dve", S[lo], T1[lo], T2[lo], ALU.subtract, ["s_T1", "s_T2"], ["s_Slo"])
                VTT("dve", T1[hi], Ab(are, hi, 0, 15, 15), Zb(bbar[1], hi, 15), ALU.mult, ["s_are"] + XD, ["s_T1h"])
                VTT("pool", T2[hi], Ab(aim, hi, 0, 15, 15), Zb(bbar[0], hi, 15), ALU.mult, ["s_aim"] + XD, ["s_T2h"])
                VTT("dve", S[hi], T1[hi], T2[hi], ALU.add, ["s_T1h", "s_T2h"], ["s_Shi"])
                T1r, T2r = T1[:, :, 0:9, :], T2[:, :, 0:9, :]
                VTT("dve", T1r[lo], Ab(are, lo, 15, 24, 9), Zb(craw[0], lo, 9), ALU.mult, ["s_are", "s_cre", "s_Slo"], ["s_T1"])
                VTT("pool", T2r[lo], Ab(aim, lo, 15, 24, 9), Zb(craw[1], lo, 9), ALU.mult, ["s_aim", "s_cim", "s_Slo"], ["s_T2"])
                VTT("dve", R[lo], T1r[lo], T2r[lo], ALU.subtract, ["s_T1", "s_T2"], ["s_Rlo"])
                VTT("dve", T1r[hi], Ab(are, hi, 15, 24, 9), Zb(craw[1], hi, 9), ALU.mult, ["s_are", "s_cim", "s_Shi"], ["s_T1h"])
                VTT("pool", T2r[hi], Ab(aim, hi, 15, 24, 9), Zb(craw[0], hi, 9), ALU.mult, ["s_aim", "s_cre", "s_Shi"], ["s_T2h"])
                STT("dve", R[hi], T1r[hi], -1.0, T2r[hi], ALU.mult, ALU.subtract, ["s_T1h", "s_T2h"], ["s_Rhi"])
                TS("dve", aim[:, :, 24:33], aim[:, :, 24:33], sgn[:, 0:1], None, ALU.mult, None, ["s_aim", "s_sgn"], ["s_aims"])
                Sr, Rr = ["s_Slo", "s_Shi"], ["s_Rlo", "s_Rhi"]
                for gg in range(8):
                    g = g0 + gg
                    Pm = S[:, gg, 7:15, :].rearrange("p i c -> p (i c)")
                    Gs = S[:, gg, 0:8, :].rearrange("p i c -> p (i c)")
                    Qm = R[:, gg, 0:8, :].rearrange("p j c -> p (j c)")
                    Hm = R[:, gg, 1:9, :].rearrange("p j c -> p (j c)")
                    pt, ptn = banks[0][:, 0:128], bn[0]
                    MM(pt, Pm, Qm, True, True, Sr + Rr, [ptn])
                    VTT("dve", ttmp[:], pt, tmaskf[:], ALU.mult, [ptn, "s_tmask"], ["s_ttmp"])
                    STT("dve", Tg[:], identf[:], dvc[:, gg:gg + 1], ttmp[:], ALU.mult, ALU.add, ["identf", "s_dvc", "s_ttmp"], ["s_Tg"])
                    pg_, pgn = banks[0][:, 128:256], bn[0]
                    MM(pg_, Gs, identf[:], True, True, Sr + ["identf"], [pgn])
                    CP("act", Gg[:], pg_, [pgn], ["s_Gg"])
                    CP("act", Hg[:], Hm, Rr, ["s_Hg"])
                    pu, pun = banks[1][:], bn[1]
                    for i in range(8):
                        MM(pu, sel[:, gg * 8 + i, :], utile[:, i::8], i == 0, i == 7, ["s_sel", "s_utile"], [pun])
                    CP("act", usb[:], pu, [pun], ["s_usb"])
                    pv, pvn = banks[2][:], bn[2]
                    MM(pv, Gg[:], usb[:], True, True, ["s_Gg", "s_usb"], [pvn])
                    CP("dve", xpad[0][:, 512:1024], pv, [pvn], ["s_xp0"])
                    cur = 0
                    for s_ in range(9):
                        sh = 2 ** s_
                        m_, mn = ms[s_ % 2], "s_ms%d" % (s_ % 2)
                        TS("pool", mtmp[:], jmat[:], aim[:, gg, 24 + s_:25 + s_], None, ALU.mult, None, ["s_jmat", "s_aims", "s_aim"], ["s_mtmp"])
                        STT("pool", m_[:], identf[:], are[:, gg, 24 + s_:25 + s_], mtmp[:], ALU.mult, ALU.add,
                            ["identf", "s_are", "s_mtmp"], [mn])
                        pz, pzn = banks[3 + s_ % 2][:], bn[3 + s_ % 2]
                        xc, xcn = xpad[cur], "s_xp%d" % cur
                        xn_, xnn = xpad[1 - cur], "s_xp%d" % (1 - cur)
                        MM(pz, m_[:], xc[:, 512 - sh:1024 - sh], True, True, [mn, xcn, "s_xp%dz" % cur], [pzn])
                        VTT("dve", xn_[:, 512:1024], pz, xc[:, 512:1024], ALU.add, [pzn, xcn], [xnn])
                        cur = 1 - cur
                    CP("act", xprev[:], xpad[cur][:, 511:1023], ["s_xp%d" % cur, "s_xp%dz" % cur], ["s_xprev"])
                    py, pyn = banks[5][:], bn[5]
                    MM(py, Tg[:], usb[:], True, False, ["s_Tg", "s_usb"], [pyn])
                    MM(py, Hg[:], xprev[:], False, True, ["s_Hg", "s_xprev"], [pyn])
                    ACT(yact[gg][:], py, AF.Gelu_apprx_tanh, [pyn], ["s_yact%d" % gg])
                for j in range(8):
                    pj, pjn = banks[1 + j % 2][:], bn[1 + j % 2]
                    for gg in range(8):
                        MM(pj, selT[:, gg * 8 + j, :], yact[gg][:], gg == 0, gg == 7, ["s_selT", "s_yact%d" % gg], [pjn])
                    CP("dve" if j % 2 == 0 else "act", ytile[:, j::8], pj, [pjn], ["s_ytile%d" % j])
                DMA("sp", yT_d[bt * 128:(bt + 1) * 128, :], ytile[:], ["s_ytile%d" % j for j in range(8)], [("dram", "yT", bt)])

        def normrope(pfx, src, gain, cs, dst, T, H, tmp, r):
            sqv, xn, ssq, t1, t2 = tmp
            n = T * H
            v4 = lambda t, w: t[:, 0:n * w].rearrange("p (t h d) -> p t h d", t=T, h=H)
            sq4, xn4 = v4(sqv, 64), v4(xn, 64)
            VTT("dve", sq4, src, src, ALU.mult, r, [pfx + "sq"])
            yield
            P.op("dve", lambda e: e.tensor_reduce(out=ssq[:, 0:n], in_=sqv[:, 0:n * 64].rearrange("p (a d) -> p a d", d=64),
                                                  axis=AX.X, op=ALU.add), [pfx + "sq"], [pfx + "ssq"])
            yield
            TS("dve", ssq[:, 0:n], ssq[:, 0:n], 1.0 / 64, EPS, ALU.mult, ALU.add, [pfx + "ssq"], [pfx + "ssq"])
            yield
            ACT(ssq[:, 0:n], ssq[:, 0:n], AF.Sqrt, [pfx + "ssq"], [pfx + "ssq"])
            yield
            RECIP(ssq[:, 0:n], ssq[:, 0:n], [pfx + "ssq"], [pfx + "ssq"])
            yield
            rs4 = ssq[:, 0:n].rearrange("p (t h) -> p t h", t=T).unsqueeze(3).to_broadcast([128, T, H, 64])
            VTT("dve", xn4, src, rs4, ALU.mult, list(r) + [pfx + "ssq"], [pfx + "xn"])
            yield
            g4 = gain.unsqueeze(2).to_broadcast([128, T, H, 64])
            VTT("pool", xn4, xn4, g4, ALU.mult, [pfx + "xn", pfx + "gain"], [pfx + "xn"])
            yield
            c4 = cs[:, 0:32].unsqueeze(1).unsqueeze(1).to_broadcast([128, T, H, 32])
            s4 = cs[:, 32:64].unsqueeze(1).unsqueeze(1).to_broadcast([128, T, H, 32])
            x1, x2 = xn4[:, :, :, 0:32], xn4[:, :, :, 32:64]
            a4, b4 = v4(t1, 32), v4(t2, 32)
            VTT("dve", a4, x1, c4, ALU.mult, [pfx + "xn", pfx + "cs"], [pfx + "t1"])
            yield
            VTT("pool", b4, x2, s4, ALU.mult, [pfx + "xn", pfx + "cs"], [pfx + "t2"])
            yield
            VTT("dve", dst[:, :, :, 0:32], a4, b4, ALU.subtract, [pfx + "t1", pfx + "t2"], [pfx + "dstA"])
            yield
            VTT("dve", a4, x2, c4, ALU.mult, [pfx + "xn", pfx + "cs", pfx + "dstA"], [pfx + "t1"])
            yield
            VTT("pool", b4, x1, s4, ALU.mult, [pfx + "xn", pfx + "cs", pfx + "dstA"], [pfx + "t2"])
            yield
            VTT("dve", dst[:, :, :, 32:64], a4, b4, ALU.add, [pfx + "t1", pfx + "t2"], [pfx + "dstB"])
            yield

        def nsa(l):
            P.barrier(); FA.reset(); BA.reset()
            NB = SEQ // 128
            kselT = [BA.take([128, SEQ]) for _ in range(2)]
            kwinT = [BA.take([128, SEQ]) for _ in range(2)]
            vsel = BA.take([128, 2, NB, 65]); vwin = BA.take([128, 2, NB, 65])
            kcT = [BA.take([128, 256]) for _ in range(2)]
            vc = BA.take([128, 2, 2, 64])
            tri = BA.take([128, 128]); anti = BA.take([128, 128])
            tri4 = BA.take([128, 4, 128]); anti4 = BA.take([128, 4, 128])
            w2sb = BA.take([128, 2, 64])
            bf_mark = BA.off
            qg = FA.take([128, 1, 64]); kg = FA.take([128, 3, 64])
            maskc = FA.take([128, 512]); atab = FA.take([128, 128]); btab = FA.take([128, 128])
            bias = FA.take([128, 2])
            cs = [FA.take([128, 64]) for _ in range(2)]
            tmp = (FA.take([128, 768]), FA.take([128, 768]), FA.take([128, 16]), FA.take([128, 384]), FA.take([128, 384]))
            f_mark = FA.off
            for g in range(2):
                DMA("pool", kselT[g][64:128, :], D["erows"][:, :], (), ["n_erows%d" % g])
            MEMSET("pool", vsel[:, :, :, 64:65], 1.0, ["n_vsel1"])
            MEMSET("pool", vwin[:, :, :, 64:65], 1.0, ["n_vwin1"])
            DMA("pool", tri[:], D["tri"][:, :], (), ["n_tri"])
            DMA("pool", anti[:], D["anti"][:, :], (), ["n_anti"])
            CP("dve", tri4[:], tri[:].unsqueeze(1).to_broadcast([128, 4, 128]), ["n_tri"], ["n_tri4"])
            CP("dve", anti4[:], anti[:].unsqueeze(1).to_broadcast([128, 4, 128]), ["n_anti"], ["n_anti4"])
            DMA("sp", qg[:, 0, :], D["qgain"][:, l, :], (), ["n_qg"])
            DMA("sp", kg[:], D["kgain"][:, l, :, :], (), ["n_kg"])
            TS("dve", qg[:], qg[:], 0.125, None, ALU.mult, None, ["n_qg"], ["n_qg"])
            DMA("sp", maskc[:], D["maskc"][:, :], (), ["n_maskc"])
            DMA("sp", atab[:], D["atab"][:, :], (), ["n_atab"])
            DMA("sp", btab[:], D["btab"][:, :], (), ["n_btab"])
            DMA("pool", w2sb[:], D["cmp_w2"][l].rearrange("t h d -> h t d"), (), ["n_w2"])
            kcmpT = BA.take([128, SEQ]); vcmpT = BA.take([128, SEQ])
            w1sb = [BA.take([128, 32, 128]) for _ in range(2)]
            peT = BA.take([64, 2, 32])
            kb = BA.take([128, 2, 2, 64])
            cb = BA.take([128, 2, 128])
            hidT = BA.take([128, 256])
            kcn = BA.take([128, 1, 2, 64])
            kvrow = [FA.take([128, 768]) for _ in range(2)]
            kcraw = FA.take([128, 2, 2, 64])
            for ty in range(2):
                w1v = D["cmp_w1"][l, ty].rearrange("(l d) h -> d l h", d=64)
                DMA("pool", w1sb[ty][0:64], w1v, (), ["n_w1_%d_lo" % ty])
                DMA("pool", w1sb[ty][64:128], w1v, (), ["n_w1_%d_hi" % ty])
            DMA("pool", peT[0:64], D["peT"][:, l, :, :], (), ["n_peT"])
            for ty in range(2):
                pb, pn = banks[0][:, ty:ty + 1], bn[0]
                for ll in range(32):
                    MM(pb, w1sb[ty][0:64, ll, :], peT[0:64, ty, ll:ll + 1], ll == 0, ll == 31, ["n_w1_%d_lo" % ty, "n_peT"], [pn])
                CP("dve", bias[:, ty:ty + 1], pb, [pn], ["n_bias%d" % ty])
            pq = pbank[:, 0:512]
            for tb in range(NB):
                kr, krn = kvrow[tb % 2], "n_kvrow%d" % (tb % 2)
                c_, cn = cs[tb % 2], "n_cs%d" % (tb % 2)
                DMA("sp", kr[:], qkv_d[tb * 128:(tb + 1) * 128, 512:1280], [("dram", "qkv", tb)], [krn])
                DMA("sp", c_[:], D["cs_tok"][tb * 128:(tb + 1) * 128, :], (), [cn])
                src = kr[:, 256:768].rearrange("p (t x h d) -> p t x h d", t=2, x=2, h=2)[:, :, 0]
                P.buf["n_k_gain"] = P.buf.get("n_kg", {"w": None, "r": {}})
                P.buf["n_k_cs"] = P.buf.get(cn, {"w": None, "r": {}})
                for _ in normrope("n_k_", src, kg[:, 1:3, :], c_, kb[:], 2, 2, tmp, [krn]):
                    pass
                for ty in range(2):
                    for h in range(2):
                        po = pq[0:64, (ty * 2 + h) * 128:(ty * 2 + h + 1) * 128]
                        TR(po, kb[:, ty, h, :], ["n_k_dstA", "n_k_dstB"], ["pbank"])
                        dstt = (kselT if ty == 0 else kwinT)[h]
                        CP("act" if h == 0 else "dve", dstt[0:64, tb * 128:(tb + 1) * 128], po, ["pbank"],
                           ["n_kT%d_%d_%d" % (ty, h, tb)])
                vsrc = kr[:, 256:768].rearrange("p (t x h d) -> p t x h d", t=2, x=2, h=2)
                CP("pool", vsel[:, :, tb, 0:64], vsrc[:, 0, 1], [krn], ["n_vsel_%d" % tb])
                CP("pool", vwin[:, :, tb, 0:64], vsrc[:, 1, 1], [krn], ["n_vwin_%d" % tb])
                CP("pool", cb[:], kr[:, 0:256].rearrange("p (t c) -> p t c", t=2), [krn], ["n_cb"])
                for ty in range(2):
                    po = pbank[:, 512 + ty * 128:512 + (ty + 1) * 128]
                    TR(po, cb[:, ty, :], ["n_cb"], ["pbank"])
                    CP("act" if ty == 0 else "dve", (kcmpT if ty == 0 else vcmpT)[:, tb * 128:(tb + 1) * 128], po,
                       ["pbank"], ["n_cT%d_%d" % (ty, tb)])
            MEMSET("dve", hidT[:, 255:256], 0.0, ["n_hid255"])
            for ty in range(2):
                xT_ = kcmpT if ty == 0 else vcmpT
                xr = ["n_cT%d_%d" % (ty, tb) for tb in range(NB)]
                for g in range(2):
                    ph, phn = banks[1][:, 0:255], bn[1]
                    hs = slice(64 * g, 64 * g + 64)
                    for ll in range(32):
                        MM(ph, w1sb[ty][hs, ll, :], xT_[hs, ll:ll + 16 * 254 + 1:16], ll == 0, ll == 31,
                           xr + ["n_w1_%d_%s" % (ty, "lo" if g == 0 else "hi")], [phn])
                    ACT(hidT[:, 0:255], ph, AF.Gelu_apprx_tanh, [phn, "n_bias%d" % ty], ["n_hidT"], bias=bias[:, ty:ty + 1])
                    for c in range(2):
                        po, pon = banks[2][:, c * 64:(c + 1) * 64], bn[2]
                        MM(po, hidT[:, c * 128:(c + 1) * 128], w2sb[:, ty, :], True, True, ["n_hidT", "n_hid255", "n_w2"], [pon])
                        if ty == 1:
                            CP("act", vc[:, g, c, :], po, [pon], ["n_vc%d_%d" % (g, c)])
                        else:
                            CP("act", kcraw[:, c, g, :], po, [pon], ["n_kcraw%d_%d" % (c, g)])
            for c in range(2):
                c_, cn = cs[c], "n_cs%d" % c
                DMA("sp", c_[:], D["cs_cmp"][c * 128:(c + 1) * 128, :], (), [cn])
                P.buf["n_c_gain"] = P.buf.get("n_kg", {"w": None, "r": {}})
                P.buf["n_c_cs"] = P.buf.get(cn, {"w": None, "r": {}})
                for _ in normrope("n_c_", kcraw[:, c:c + 1, :, :], kg[:, 0:1, :], c_, kcn[:], 1, 2, tmp,
                                  ["n_kcraw%d_%d" % (c, g) for g in range(2)]):
                    pass
                for g in range(2):
                    po = pq[0:64, g * 128:(g + 1) * 128]
                    TR(po, kcn[:, 0, g, :], ["n_c_dstA", "n_c_dstB"], ["pbank"])
                    CP("act", kcT[g][0:64, c * 128:(c + 1) * 128], po, ["pbank"], ["n_kcT%d_%d" % (g, c)])
            P.barrier()
            BA.off = bf_mark; FA.off = f_mark
            qaug = [[BA.take([128, 512]) for _ in range(2)] for _ in range(3)]
            PT = [BA.take([128, 512]) for _ in range(3)]
            qn = BA.take([128, 1, 8, 64])
            ident4 = BA.take([128, 4, 128])
            nm = BA.take([128, 2, 128])
            otm = [BA.take([128, 8, 64]) for _ in range(2)]
            ocsb = BA.take([128, 4, 128])
            maskc_bf = BA.take([128, 512])
            qrow = [FA.take([128, 512]) for _ in range(3)]
            cs3 = [FA.take([128, 64]) for _ in range(3)]
            grow = [FA.take([128, 24]) for _ in range(3)]
            gsig = [FA.take([128, 24]) for _ in range(3)]
            pun = [FA.take([128, 2, 256]) for _ in range(4)]
            psumh = FA.take([128, 2, 256])
            den = FA.take([128, 4, 2]); rden = FA.take([128, 4, 2]); cfcs = [FA.take([128, 8]) for _ in range(2)]; thr = FA.take([128, 2])
            CC = FA.take([128, 2, 32]); SS = FA.take([128, 2, 32]); sqv = FA.take([128, 512]); ssq = FA.take([128, 8])
            ta = FA.take([128, 8, 32]); tb = FA.take([128, 8, 32]); tc_ = FA.take([128, 8, 32]); td = FA.take([128, 8, 32])
            dd = FA.take([128, 8, 64])
            imp = FA.take([128, 2, 64]); prio = FA.take([128, 2, 64]); wk = FA.take([128, 2, 64])
            m1 = FA.take([128, 2, 64]); m2 = FA.take([128, 2, 64])
            mx = [FA.take([128, 2, 8]) for _ in range(2)]
            coef = FA.take([128, 2, 3, 4])
            ocs = [FA.take([128, 8, 64]) for _ in range(2)]
            tA = [FA.take([128, 4, 64]) for _ in range(2)]; tB = [FA.take([128, 4, 64]) for _ in range(2)]; tC = [FA.take([128, 4, 64]) for _ in range(2)]
            MEMSET("dve", nm[:, :, 0:64], 0.0, ["n_nm0"])
            CP("dve", maskc_bf[:], maskc[:], ["n_maskc"], ["n_maskcb"])
            CP("dve", ident4[:], ident_bf[:].unsqueeze(1).to_broadcast([128, 4, 128]), ["ident_bf"], ["n_ident4"])
            psc_b, psc_n = banks[0], bn[0]
            poc_b, poc_n = banks[1], bn[1]
            st_b = [(banks[2][:], bn[2]), (banks[3][:], bn[3])]
            pos_b, pos_n = banks[4], bn[4]
            pow_b, pow_n = banks[5], bn[5]
            pqA = pbank
            stepc = [0]

            def stageA_prep(qb):
                pa = qb % 3
                qr, qrn = qrow[pa], "a_qrow%d" % pa
                c_, cn = cs3[pa], "a_cs%d" % pa
                gr, grn, gs_, gsn = grow[pa], "a_grow%d" % pa, gsig[pa], "a_gsig%d" % pa
                DMA("sp", qr[:], qkv_d[qb * 128:(qb + 1) * 128, 0:512], [("dram", "qkv", qb)], [qrn])
                DMA("sp", gr[:], qkv_d[qb * 128:(qb + 1) * 128, 1280:1304], [("dram", "qkv", qb)], [grn])
                DMA("sp", c_[:], D["cs_tok"][qb * 128:(qb + 1) * 128, :], (), [cn])
                yield
                yield
                q3 = qr[:].rearrange("p (h d) -> p h d", h=8)
                x1, x2 = q3[:, :, 0:32], q3[:, :, 32:64]
                g2 = qg[:, 0, :].rearrange("p (a d) -> p a d", a=2)
                VTT("pool", CC[:], c_[:, 0:32].unsqueeze(1).to_broadcast([128, 2, 32]), g2, ALU.mult, [cn, "n_qg"], ["a_CC"])
                VTT("pool", SS[:], c_[:, 32:64].unsqueeze(1).to_broadcast([128, 2, 32]), g2, ALU.mult, [cn, "n_qg"], ["a_SS"])
                VTT("dve", sqv[:], qr[:], qr[:], ALU.mult, [qrn], ["a_sqv"])
                bc8 = lambda t: t.unsqueeze(1).to_broadcast([128, 8, 32])
                VTT("pool", ta[:], x1, bc8(CC[:, 0, :]), ALU.mult, [qrn, "a_CC"], ["a_ta"])
                yield
                P.op("dve", lambda e: e.tensor_reduce(out=ssq[:], in_=sqv[:].rearrange("p (h d) -> p h d", h=8), axis=AX.X, op=ALU.add),
                     ["a_sqv"], ["a_ssq"])
                VTT("pool", tb[:], x2, bc8(SS[:, 1, :]), ALU.mult, [qrn, "a_SS"], ["a_tb"])
                VTT("pool", tc_[:], x2, bc8(CC[:, 1, :]), ALU.mult, [qrn, "a_CC"], ["a_tc"])
                yield
                TS("dve", ssq[:], ssq[:], 1.0 / 64, EPS, ALU.mult, ALU.add, ["a_ssq"], ["a_ssq"])
                VTT("pool", td[:], x1, bc8(SS[:, 0, :]), ALU.mult, [qrn, "a_SS"], ["a_td"])
                VTT("pool", dd[:, :, 0:32], ta[:], tb[:], ALU.subtract, ["a_ta", "a_tb"], ["a_dd1"])
                yield
                ACT(gs_[:], gr[:], AF.Sigmoid, [grn], [gsn])
                yield
                ACT(ssq[:], ssq[:], AF.Sqrt, ["a_ssq"], ["a_ssq"])
                VTT("pool", dd[:, :, 32:64], tc_[:], td[:], ALU.add, ["a_tc", "a_td"], ["a_dd2"])
                yield
                RECIP(ssq[:], ssq[:], ["a_ssq"], ["a_ssq"])
                yield
                VTT("dve", qn[:, 0], dd[:], ssq[:].unsqueeze(2).to_broadcast([128, 8, 64]), ALU.mult, ["a_dd1", "a_dd2", "a_ssq"], ["a_qn"])
                yield
                for hh in range(8):
                    TR(pqA[0:64, hh * 128:(hh + 1) * 128], qn[:, 0, hh, :], ["a_qn"], ["pbank"])
                yield
                for g in range(2):
                    CP("dve", qaug[pa][g][0:64, :], pqA[0:64, g * 512:(g + 1) * 512], ["pbank"], ["a_qaug%d_%d_q" % (pa, g)])
                yield

            def stageA_rest(qb):
                pa = qb % 3
                gs_, gsn = gsig[pa], "a_gsig%d" % pa
                nch = 1 if qb <= 15 else 2
                ncol = 128 * nch
                o0 = OFFC - 8 * qb
                poc = poc_b[:, 0:512].rearrange("p (h d) -> p h d", h=8)
                MEMSET("pool", den[:], 0.0, ["a_den"])
                if ncol < 256:
                    MEMSET("pool", psumh[:, :, ncol:256], 0.0, ["a_psumhz"])
                slots = [(g, hp) for hp in range(2) for g in range(2)]
                for (g, hp) in slots:
                    si = 2 * g + hp
                    psc = psc_b[:, 0:2 * ncol].rearrange("p (h n) -> p h n", h=2)
                    for hl in range(2):
                        h = 2 * hp + hl
                        MM(psc[:, hl, :], qaug[pa][g][0:64, h * 128:(h + 1) * 128], kcT[g][0:64, 0:ncol], True, False,
                           ["a_qaug%d_%d_q" % (pa, g)] + ["n_kcT%d_%d" % (g, c) for c in range(2)], [psc_n])
                        MM(psc[:, hl, :], ident_bf[:], maskc_bf[:, o0:o0 + ncol], False, True, ["ident_bf", "n_maskcb"], [psc_n])
                    yield
                    for hl in range(2):
                        ACT(pun[si][:, hl, 0:ncol], psc[:, hl, :], AF.Exp, [psc_n, "a_den"], ["a_pun%d_%d" % (si, hl), "a_den%d_%d" % (si, hl)],
                            accum_out=den[:, si, hl:hl + 1])
                    yield
                dn = ["a_den%d_%d" % (si, hl) for si in range(4) for hl in range(2)]
                if qb == 0:
                    TS("dve", den[:], den[:], 1e-30, None, ALU.max, None, dn + ["a_den"], dn + ["a_den"])
                    yield
                RECIP(rden[:], den[:], dn + ["a_den"], ["a_rden"])
                yield
                g8 = gs_[:].rearrange("p (h b) -> p h b", b=3)
                VTT("pool", cfcs[qb % 2][:], g8[:, :, 0], rden[:].rearrange("p s h -> p (s h)"), ALU.mult, [gsn, "a_rden"], ["a_cfc%d" % (qb % 2)])
                for k in range(4):
                    for g in range(2):
                        si, hl = 2 * g + k // 2, k % 2
                        if k == 0:
                            TS("dve", psumh[:, g, 0:ncol], pun[si][:, hl, 0:ncol], rden[:, si, hl:hl + 1], None, ALU.mult, None,
                               ["a_pun%d_%d" % (si, hl), "a_rden"], ["a_psumh%d" % g])
                        else:
                            STT("dve", psumh[:, g, 0:ncol], pun[si][:, hl, 0:ncol], rden[:, si, hl:hl + 1], psumh[:, g, 0:ncol],
                                ALU.mult, ALU.add, ["a_pun%d_%d" % (si, hl), "a_rden", "a_psumh%d" % g], ["a_psumh%d" % g])
                    yield
                pr = ["a_psumh0", "a_psumh1", "a_psumhz"]
                p4 = psumh[:].rearrange("p g (j r) -> p g j r", r=4)
                VTT("dve", imp[:], p4[:, :, :, 0], p4[:, :, :, 1], ALU.add, pr, ["a_imp"])
                STT("dve", wk[:], p4[:, :, :, 3], 0.5, p4[:, :, :, 2], ALU.mult, ALU.add, pr, ["a_wk"])
                yield
                VTT("dve", imp[:], imp[:], wk[:], ALU.add, ["a_imp", "a_wk"], ["a_imp"])
                yield
                STT("dve", imp[:, :, 1:64], p4[:, :, 0:63, 3], 0.5, imp[:, :, 1:64], ALU.mult, ALU.add, pr + ["a_imp"], ["a_imp"])
                yield
                v0 = OFFV - 2 * qb
                VTT("dve", prio[:], imp[:], atab[:, v0:v0 + 64].unsqueeze(1).to_broadcast([128, 2, 64]), ALU.mult, ["a_imp", "n_atab"], ["a_prio"])
                yield
                VTT("dve", prio[:], prio[:], btab[:, v0:v0 + 64].unsqueeze(1).to_broadcast([128, 2, 64]), ALU.add, ["a_prio", "n_btab"], ["a_prio"])
                yield
                MEMSET("dve", prio[:, :, 0:1], 2e4, ["a_prio"])
                yield
                for g in range(2):
                    P.op("dve", lambda e, g=g: e.max(out=mx[0][:, g, :], in_=prio[:, g, :]), ["a_prio"], ["a_mx0_%d" % g])
                yield
                for g in range(2):
                    P.op("dve", lambda e, g=g: e.match_replace(out=wk[:, g, :], in_to_replace=mx[0][:, g, :], in_values=prio[:, g, :],
                                                               imm_value=-1e9), ["a_prio", "a_mx0_%d" % g, "a_wk"], ["a_wk%d" % g])
                yield
                for g in range(2):
                    P.op("dve", lambda e, g=g: e.max(out=mx[1][:, g, :], in_=wk[:, g, :]), ["a_wk%d" % g], ["a_mx1_%d" % g])
                yield
                TS("dve", thr[:], mx[1][:, :, 7], -0.5, None, ALU.max, None, ["a_mx1_0", "a_mx1_1"], ["a_thr"])
                yield
                for g in range(2):
                    TS("dve", nm[:, g, 64:128], prio[:, g, :], thr[:, g:g + 1], NEG, ALU.is_lt, ALU.mult, ["a_prio", "a_thr"], ["a_nm%d" % g])
                yield
                for g in range(2):
                    pnm = pqA[:, g * 128:(g + 1) * 128]
                    TR(pnm, nm[:, g, :], ["a_nm%d" % g, "n_nm0"], ["pbank"])
                yield
                for g in range(2):
                    pnm = pqA[:, g * 128:(g + 1) * 128]
                    CP("dve", qaug[pa][g][64:128, :].rearrange("p (h q) -> p h q", h=4),
                       pnm[64:128, :].unsqueeze(1).to_broadcast([64, 4, 128]), ["pbank"], ["a_qaug%d_%d_m" % (pa, g)])
                yield

            def alloc_step():
                k = stepc[0]
                stepc[0] += 1
                return st_b[k % 2], (PT[k % 3], "a_PT%d" % (k % 3))

            def pathII_steps(qb):
                pa = qb % 3
                nch = 1 if qb <= 15 else 2
                o0 = OFFC - 8 * qb
                poc = poc_b[:, 0:512].rearrange("p (h d) -> p h d", h=8)
                steps = []
                state = {"first": True}
                for c in range(nch):
                    for g in range(2):
                        def mk(c=c, g=g):
                            (pst, pstn), (pt_, ptn_) = alloc_step()

                            def s_():
                                MM(pst, kcT[g][0:64, c * 128:(c + 1) * 128], qaug[pa][g][0:64, :], True, False,
                                   ["a_qaug%d_%d_q" % (pa, g), "n_kcT%d_%d" % (g, c)], [pstn])
                                MM(pst, maskc_bf[:, o0 + c * 128:o0 + (c + 1) * 128], ident4[:].rearrange("p h q -> p (h q)"), False, True,
                                   ["n_maskcb", "n_ident4"], [pstn])

                            def e_():
                                ACT(pt_[:], pst, AF.Exp, [pstn], [ptn_])

                            def p_():
                                for h in range(4):
                                    MM(poc[:, 4 * g + h, :], pt_[:, h * 128:(h + 1) * 128], vc[:, g, c, :], state["first"], c == nch - 1,
                                       [ptn_, "n_vc%d_%d" % (g, c)], [poc_n], sgc=True)
                                    state["first"] = False
                            return {"s": s_, "e": e_, "p": p_}
                        steps.append(mk)
                return steps

            def B_steps(qb):
                pa = qb % 3
                p2 = qb % 2
                gs_, gsn = gsig[pa], "a_gsig%d" % pa
                g4 = gs_[:].rearrange("p (g h b) -> p g h b", g=2, h=4)
                steps = []
                for g in range(2):
                    qa = qaug[pa][g]
                    qa_r = ["a_qaug%d_%d_q" % (pa, g), "a_qaug%d_%d_m" % (pa, g)]
                    pos_ = pos_b[:, 0:260].rearrange("p (h d) -> p h d", h=4)
                    pow_ = pow_b[:, 0:260].rearrange("p (h d) -> p h d", h=4)
                    state = {"fs": True, "fw": True}
                    for c in range(qb + 1):
                        def mk(c=c, g=g, qa=qa, qa_r=qa_r, pos_=pos_, state=state):
                            (pst, pstn), (pt_, ptn_) = alloc_step()
                            diag = (c == qb)

                            def s_():
                                MM(pst, kselT[g][:, c * 128:(c + 1) * 128], qa[:, :], True, not diag,
                                   qa_r + ["n_kT0_%d_%d" % (g, c), "n_erows%d" % g], [pstn])
                                if diag:
                                    MM(pst, ident_bf[:], tri4[:].rearrange("p h q -> p (h q)"), False, True, ["ident_bf", "n_tri4"], [pstn])

                            def e_():
                                ACT(pt_[:], pst, AF.Exp, [pstn], [ptn_])

                            def p_():
                                for h in range(4):
                                    MM(pos_[:, h, :], pt_[:, h * 128:(h + 1) * 128], vsel[:, g, c, :], state["fs"], c == qb,
                                       [ptn_, "n_vsel_%d" % c, "n_vsel1"], [pos_n], sgc=True)
                                    state["fs"] = False
                            return {"s": s_, "e": e_, "p": p_}
                        steps.append(mk)
                    c0 = max(0, qb - 4)
                    for c in range(c0, qb + 1):
                        def mk(c=c, g=g, qa=qa, qa_r=qa_r, pow_=pow_, pos_=pos_, state=state, last=(c == qb)):
                            (pst, pstn), (pt_, ptn_) = alloc_step()
                            special = (c == qb) or (c == qb - 4)

                            def s_():
                                MM(pst, kwinT[g][0:64, c * 128:(c + 1) * 128], qa[0:64, :], True, not special,
                                   [qa_r[0], "n_kT1_%d_%d" % (g, c)], [pstn])
                                if special:
                                    mk_, mkn = (tri4, "n_tri4") if c == qb else (anti4, "n_anti4")
                                    MM(pst, ident_bf[:], mk_[:].rearrange("p h q -> p (h q)"), False, True, ["ident_bf", mkn], [pstn])

                            def e_():
                                ACT(pt_[:], pst, AF.Exp, [pstn], [ptn_])

                            def p_():
                                for h in range(4):
                                    MM(pow_[:, h, :], pt_[:, h * 128:(h + 1) * 128], vwin[:, g, c, :], state["fw"], c == qb,
                                       [ptn_, "n_vwin_%d" % c, "n_vwin1"], [pow_n], sgc=True)
                                    state["fw"] = False

                            def post():
                                cf = coef[:, g]
                                RECIP(cf[:, 1, :], pos_[:, :, 64], [pos_n], ["a_cf1_%d" % g])
                                RECIP(cf[:, 2, :], pow_[:, :, 64], [pow_n], ["a_cf2_%d" % g])
                                yield
                                VTT("dve", cf[:, 1, :], cf[:, 1, :], g4[:, g, :, 1], ALU.mult, ["a_cf1_%d" % g, gsn], ["a_cf1_%d" % g])
                                VTT("dve", cf[:, 2, :], cf[:, 2, :], g4[:, g, :, 2], ALU.mult, ["a_cf2_%d" % g, gsn], ["a_cf2_%d" % g])
                                yield
                                VTT("dve", tA[g][:], pos_[:, :, 0:64], cf[:, 1, :].unsqueeze(2).to_broadcast([128, 4, 64]), ALU.mult,
                                    [pos_n, "a_cf1_%d" % g], ["a_tA%d" % g])
                                VTT("dve", tB[g][:], pow_[:, :, 0:64], cf[:, 2, :].unsqueeze(2).to_broadcast([128, 4, 64]), ALU.mult,
                                    [pow_n, "a_cf2_%d" % g], ["a_tB%d" % g])
                                yield
                                VTT("pool", tC[g][:], tA[g][:], ocs[p2][:, 4 * g:4 * g + 4, :], ALU.add, ["a_tA%d" % g, "a_ocs%d" % p2], ["a_tC%d" % g])
                                yield
                                VTT("pool", otm[p2][:, 4 * g:4 * g + 4, :], tC[g][:], tB[g][:], ALU.add, ["a_tC%d" % g, "a_tB%d" % g],
                                    ["a_otm%d_%d" % (p2, g)])
                                yield
                            d = {"s": s_, "e": e_, "p": p_}
                            if last:
                                d["post"] = post
                            return d
                        steps.append(mk)
                return steps

            def run_steps(mks):
                if not mks:
                    return
                cur_ = mks[0]()
                cur_["s"]()
                for i in range(len(mks)):
                    nxt = None
                    if i + 1 < len(mks):
                        nxt = mks[i + 1]()
                        nxt["s"]()
                    cur_["e"]()
                    cur_["p"]()
                    if "post" in cur_:
                        for _ in cur_["post"]():
                            yield
                    yield
                    cur_ = nxt

            def stageB(qb, nxt_qb):
                pa = qb % 2
                poc = poc_b[:, 0:512].rearrange("p (h d) -> p h d", h=8)
                VTT("dve", ocs[pa][:], poc, cfcs[pa][:].unsqueeze(2).to_broadcast([128, 8, 64]), ALU.mult, [poc_n, "a_cfc%d" % pa], ["a_ocs%d" % pa])
                yield
                mks = B_steps(qb)
                if nxt_qb is not None:
                    mks = mks + pathII_steps(nxt_qb)
                for _ in run_steps(mks):
                    yield
                for ct in range(4):
                    TR(pbankB[:, ct * 128:(ct + 1) * 128], otm[pa][:, 2 * ct:2 * ct + 2, :].rearrange("p h d -> p (h d)"),
                       ["a_otm%d_0" % pa, "a_otm%d_1" % pa], ["pbankB"])
                CP("act", ocsb[:].rearrange("p c q -> p (c q)"), pbankB[:, 0:512], ["pbankB"], ["a_ocsb"])
                DMA("sp", ocT_d.rearrange("(ct p) s -> p ct s", p=128)[:, :, qb * 128:(qb + 1) * 128], ocsb[:], ["a_ocsb"],
                    [("dram", "ocT", qb)])
                yield

            nqb = min(NB, DBG["nqb"])
            if nqb > 0:
                interleave([stageA_prep(0)])
                interleave([stageA_rest(0), stageA_prep(1) if nqb > 1 else None])
                interleave([run_steps(pathII_steps(0))])
            for qb in range(nqb):
                nx = qb + 1 if qb + 1 < nqb else None
                interleave([stageB(qb, nx), stageA_rest(qb + 1) if nx is not None else None,
                            stageA_prep(qb + 2) if qb + 2 < nqb else None])

        def mixer_out(l, xr, xname):
            P.barrier(); FA.reset(); BA.reset()
            gain = gains["norm_mixT"]
            xt = [FA.take([128, NFT, TT]) for _ in range(2)]
            rstd = FA.take([128, TT])
            sga = FA.take([128, TT]); sgb = FA.take([128, TT]); sgg = FA.take([128, TT]); ya = FA.take([128, TT]); yb = FA.take([128, TT])
            sq = BA.take([128, NFT, TT]); hT = BA.take([128, NFT, TT])
            wg = BA.take([128, NFT, 2048]); wout = BA.take([128, NFT, D_MODEL])
            wglu = [BA.take([128, 4, 256]) for _ in range(2)]; wup = [BA.take([128, 4, 128]) for _ in range(2)]
            yt_ = BA.take([128, 4, TT]); oct_ = BA.take([128, 4, TT]); mg = BA.take([128, NFT, TT])
            wv = D["w_in"][l].rearrange("(kt p) c -> p kt c", p=128)
            wov = D["w_out"][l].rearrange("(kt p) c -> p kt c", p=128)
            wgv = D["ssm_w_glu"][l].rearrange("(kt p) c -> p kt c", p=128)
            wuv = D["nsa_w_up"][l].rearrange("(kt p) c -> p kt c", p=128)
            for kt in range(NFT):
                DMA("pool", wg[:, kt, :], wv[:, kt, GA0:NCOL], (), ["o_wg%d" % kt])
                DMA("pool", wout[:, kt, :], wov[:, kt, :], (), ["o_wout%d" % kt])
            wgr = ["o_wg%d" % kt for kt in range(NFT)]
            wor = ["o_wout%d" % kt for kt in range(NFT)]
            for tt in range(SEQ // TT):
                tok0 = tt * TT
                xb, xn = xt[tt % 2], "o_xt%d" % (tt % 2)
                DMA("sp", xb[:], xview(xr)[:, :, tok0:tok0 + TT], [dtile(xname, o, tok0) for o in range(NFT)], [xn])
                norm_tile("o_", xb, xn, gain[:, l, :], "g_norm_mixT", hT, "o_hT", sq, rstd, banks[0][:], bn[0])
                hr = ["o_hT_%d" % ft for ft in range(NFT)]
                DMA("sp", yt_[:], yT_d.rearrange("(kt p) s -> p kt s", p=128)[:, :, tok0:tok0 + TT],
                    [("dram", "yT", bt) for bt in range(4)], ["o_yt"])
                DMA("sp", oct_[:], ocT_d.rearrange("(kt p) s -> p kt s", p=128)[:, :, tok0:tok0 + TT],
                    [("dram", "ocT", qb) for qb in range(tok0 // 128, tok0 // 128 + 4)], ["o_oct"])
                for c in range(NFT):
                    wl, wln = wglu[c % 2], "o_wglu%d" % (c % 2)
                    wu_, wun = wup[c % 2], "o_wup%d" % (c % 2)
                    DMA("pool", wl[:, :, 0:128], wgv[:, :, c * 128:(c + 1) * 128], (), [wln + "v"])
                    DMA("pool", wl[:, :, 128:256], wgv[:, :, 1024 + c * 128:1024 + (c + 1) * 128], (), [wln + "g"])
                    DMA("pool", wu_[:], wuv[:, :, c * 128:(c + 1) * 128], (), [wun])
                    pga, pgb, pv_, pgt, pyb = banks[1][:], banks[2][:], banks[3][:], banks[4][:], banks[5][:]
                    for kt in range(NFT):
                        MM(pga, wg[:, kt, c * 128:(c + 1) * 128], hT[:, kt, :], kt == 0, kt == NFT - 1, hr + wgr, [bn[1]])
                    for kt in range(NFT):
                        MM(pgb, wg[:, kt, 1024 + c * 128:1024 + (c + 1) * 128], hT[:, kt, :], kt == 0, kt == NFT - 1, hr + wgr, [bn[2]])
                    for kt in range(4):
                        MM(pv_, wl[:, kt, 0:128], yt_[:, kt, :], kt == 0, kt == 3, [wln + "v", "o_yt"], [bn[3]])
                    for kt in range(4):
                        MM(pgt, wl[:, kt, 128:256], yt_[:, kt, :], kt == 0, kt == 3, [wln + "g", "o_yt"], [bn[4]])
                    for kt in range(4):
                        MM(pyb, wu_[:, kt, :], oct_[:, kt, :], kt == 0, kt == 3, [wun, "o_oct"], [bn[5]])
                    ACT(sga[:], pga, AF.Sigmoid, [bn[1]], ["o_sga"])
                    ACT(sgb[:], pgb, AF.Sigmoid, [bn[2]], ["o_sgb"])
                    ACT(sgg[:], pgt, AF.Sigmoid, [bn[4]], ["o_sgg"])
                    VTT("dve", ya[:], pv_, sgg[:], ALU.mult, [bn[3], "o_sgg"], ["o_ya"])
                    VTT("pool", ya[:], ya[:], sga[:], ALU.mult, ["o_ya", "o_sga"], ["o_ya"])
                    VTT("dve", yb[:], pyb, sgb[:], ALU.mult, [bn[5], "o_sgb"], ["o_yb"])
                    VTT("pool", mg[:, c, :], ya[:], yb[:], ALU.add, ["o_ya", "o_yb"], ["o_mg%d" % c])
                mgr = ["o_mg%d" % c for c in range(NFT)]
                for ot in range(NFT):
                    po, pon = banks[0][:], bn[0]
                    for c in range(NFT):
                        MM(po, wout[:, c, ot * 128:(ot + 1) * 128], mg[:, c, :], c == 0, c == NFT - 1, mgr + wor, [pon])
                    VTT("dve", xb[:, ot, :], po, xb[:, ot, :], ALU.add, [pon, xn] + hr, [xn])
                DMA("sp", xview(xr)[:, :, tok0:tok0 + TT], xb[:], [xn], [dtile(xname, o, tok0) for o in range(NFT)])

        cur, cname = xT_in, "xin"
        for l in range(depth):
            last = (l == depth - 1)
            if "ffn1" in stages:
                ffn(l, cur, cname, xres, "xres", "norm_ffn1T", D["ffn1_wi"], D["ffn1_wo"])
                cur, cname = xres, "xres"
            if "mix" in stages:
                if "proj" in mix_parts:
                    mixer_proj(l, cur, cname)
                if "s5" in mix_parts:
                    s5(l)
                if "nsa" in mix_parts:
                    nsa(l)
                if "out" in mix_parts:
                    mixer_out(l, xres, "xres")
            if "ffn2" in stages:
                dst, dn = (outT, "out") if last else (xres, "xres")
                ffn(l, cur, cname, dst, dn, "norm_ffn2T", D["ffn2_wi"], D["ffn2_wo"])
                cur, cname = xres, "xres"
        final = [(P.dsem[i], 16 * P.dcnt[i]) for i in range(P.NDMA)]
        P.finish(final)
        print("instructions:", P.ninst, "sems:", P.nsem, flush=True)
    return nc


_CACHE = {}


def kernel(**inputs):
    x = np.asarray(inputs["x"], dtype=np.float32)
    if "nc" not in _CACHE:
        _CACHE["nc"] = build_program()
    nc = _CACHE["nc"]
    shared = layout_params(inputs)
    shared.update(host_constants())
    in_maps = []
    ncores = DBG["ncores"]
    for c in range(ncores):
        m = dict(shared)
        m["xT"] = np.ascontiguousarray(x[c // 2].T)
        in_maps.append(m)
    res = run_bass_kernel_spmd(nc, in_maps, core_ids=list(range(ncores)))
    _CACHE["last"] = res
    out = np.stack([np.ascontiguousarray(res.results[(2 * b) % ncores]["outT"].T) for b in range(BATCH)], axis=0)
    return out.astype(np.float32)
```

```python
import numpy as np
from contextlib import ExitStack
import concourse.bass as bass
import concourse.mybir as mybir
from concourse.bass_utils import run_bass_kernel_spmd

F32 = mybir.dt.float32
BF16 = mybir.dt.bfloat16
AF = mybir.ActivationFunctionType
ALU = mybir.AluOpType
AX = mybir.AxisListType

D_MODEL = 1024
SEQ = 4096
BATCH = 4
DEPTH = 4
D_FF = 2816
NFT = D_MODEL // 128
NFF = D_FF // 128
TT = 512
ST = 1024
EPS = 1e-6

ENGS = ("pe", "act", "dve", "pool", "sp")
NO_SAME_ENGINE_WAIT = False


class Prog:
    EPOCH = 20000
    NDMA = 24

    def __init__(self, nc, es):
        self.nc = nc
        self.es = es
        self.streams = {e: [] for e in ENGS}
        self.cnt = {e: 0 for e in ENGS}
        self.cur = {}
        self.nsem = 0
        self.pe_sems = []
        self.own_sems = {e: [] for e in ENGS}
        for e in ENGS:
            self.cur[e] = self._newsem(e)
            self.own_sems[e].append(self.cur[e])
        self.pe_sems.append(self.cur["pe"])
        self.waited = {e: {} for e in ENGS}
        self.buf = {}
        self.xacc = {}
        self.dsem = [self._newsem("dma%d" % i) for i in range(self.NDMA)]
        self.dcnt = [0] * self.NDMA
        self.dnext = 0
        self.ninst = 0

    def _newsem(self, tag):
        self.nsem += 1
        return self.es.enter_context(self.nc.semaphore("s_%s_%d" % (tag, self.nsem)))

    def _wait(self, eng, ev):
        if ev is None:
            return
        sem, val = ev
        if eng == "pe" and any(sem is s for s in self.pe_sems):
            return
        if NO_SAME_ENGINE_WAIT and any(sem is s for s in self.own_sems[eng]):
            return
        w = self.waited[eng]
        if w.get(id(sem), 0) >= val:
            return
        w[id(sem)] = val
        self.streams[eng].append(("wait", sem, val))

    def _deps(self, eng, reads, writes):
        for b in reads:
            st = self.buf.get(b)
            if st is not None:
                self._wait(eng, st["w"])
        for b in writes:
            st = self.buf.get(b)
            if st is not None:
                self._wait(eng, st["w"])
                for ev in st["r"].values():
                    self._wait(eng, ev)

    def _mark(self, key, ev, reads, writes):
        for b in reads:
            st = self.buf.setdefault(b, {"w": None, "r": {}})
            st["r"][key] = ev
        for b in writes:
            self.buf[b] = {"w": ev, "r": {}}

    def _excl(self, eng, names):
        out = []
        for b in names:
            if isinstance(b, str) and (b.startswith("bank") or b.startswith("pbank")):
                st = self.xacc.setdefault(b, {})
                for e2, ev in st.items():
                    if e2 != eng:
                        self._wait(eng, ev)
                out.append(st)
        return out

    def op(self, eng, fn, reads=(), writes=()):
        self._deps(eng, reads, writes)
        xs = self._excl(eng, list(reads) + list(writes))
        if self.cnt[eng] >= self.EPOCH:
            self.cur[eng] = self._newsem(eng)
            self.own_sems[eng].append(self.cur[eng])
            self.cnt[eng] = 0
            if eng == "pe":
                self.pe_sems.append(self.cur[eng])
        self.cnt[eng] += 1
        sem = self.cur[eng]
        ev = (sem, self.cnt[eng])
        self.streams[eng].append(("op", fn, sem))
        self._mark(eng, ev, reads, writes)
        for st in xs:
            st[eng] = ev
        self.ninst += 1
        return ev

    def dma(self, q, out, in_, reads=(), writes=()):
        self._deps(q, reads, writes)
        i = self.dnext
        self.dnext = (self.dnext + 1) % self.NDMA
        sem = self.dsem[i]
        self._wait(q, (sem, 16 * self.dcnt[i]))
        self.dcnt[i] += 1
        ev = (sem, 16 * self.dcnt[i])
        self.streams[q].append(("dma", out, in_, sem))
        self._mark(("dma", i, self.dcnt[i]), ev, reads, writes)
        self.ninst += 1
        return ev

    def barrier(self):
        evs = [(self.cur[e], self.cnt[e]) for e in ENGS if self.cnt[e] > 0]
        evs += [(self.dsem[i], 16 * self.dcnt[i]) for i in range(self.NDMA) if self.dcnt[i] > 0]
        for e in ENGS:
            for ev in evs:
                self._wait(e, ev)
        self.buf = {}
        self.xacc = {}

    def finish(self, final_events):
        for ev in final_events:
            self._wait("sp", ev)
        nc = self.nc
        block = self.es.enter_context(nc.Block())

        def replay(engobj, stream):
            for it in stream:
                if it[0] == "wait":
                    engobj.wait_ge(it[1], it[2])
                elif it[0] == "op":
                    it[1](engobj).then_inc(it[2], 1)
                else:
                    engobj.dma_start(out=it[1], in_=it[2]).then_inc(it[3], 16)

        @block.tensor
        def _(e):
            replay(e, self.streams["pe"])

        @block.scalar
        def _(e):
            replay(e, self.streams["act"])

        @block.vector
        def _(e):
            replay(e, self.streams["dve"])

        @block.gpsimd
        def _(e):
            replay(e, self.streams["pool"])

        @block.sync
        def _(e):
            replay(e, self.streams["sp"])


NCOL = 3864
DBG = {"ncores": 8, "nqb": 32, "s5_nb": 4, "s5_ng": 4, "s5_inv": 1}
QKV0, QKVW = 512, 1304
GA0, GB0 = 1816, 2840
NEG = -30000.0
OFFC = 248
OFFV = 62
F32N = 15360
BFN = 49152
TWO_PI = float(2 * np.pi)
EVEC = [float(7 - k) for k in range(15)] + [float(k) for k in range(9)] + [float(8 * 2 ** s) for s in range(9)]
NE = len(EVEC)


def host_constants():
    c = {}
    pos = np.arange(SEQ, dtype=np.float32)
    inv_freq = (1.0 / (10000.0 ** (np.arange(32, dtype=np.float32) / 32))).astype(np.float32)
    ang = pos[:, None] * inv_freq[None, :]
    c["cs_tok"] = np.concatenate([np.cos(ang), np.sin(ang)], axis=1).astype(np.float32)
    cend = (np.arange(256) * 16 + 31).astype(np.float32)
    angc = cend[:, None] * inv_freq[None, :]
    c["cs_cmp"] = np.concatenate([np.cos(angc), np.sin(angc)], axis=1).astype(np.float32)
    sel = np.zeros((128, 8, 8, 128), np.float32)
    for g8 in range(8):
        for i in range(8):
            for cin in range(16):
                sel[g8 * 16 + cin, g8, i, i * 16 + cin] = 1.0
    c["sel"] = sel.reshape(128, 64, 128)
    c["selT"] = np.ascontiguousarray(sel.transpose(3, 1, 2, 0)).reshape(128, 64, 128)
    ii = np.arange(128) // 16
    c["tmask"] = (ii[None, :] >= ii[:, None]).astype(np.float32)
    c["identf"] = np.eye(128, dtype=np.float32)
    J = np.zeros((128, 128), np.float32)
    for k in range(128):
        J[k, (k + 64) % 128] = 1.0
    c["jmat"] = J
    sg = np.ones((128, 1), np.float32); sg[64:] = -1.0
    c["sgn"] = sg
    c["evec"] = np.broadcast_to(np.asarray(EVEC, np.float32)[None, :], (128, NE)).copy()
    tq = np.arange(128)
    m = np.arange(512)
    c["maskc"] = np.where(16 * (m[None, :] - OFFC) + 31 <= tq[:, None], 0.0, NEG).astype(np.float32)
    curp = (tq >= 64).astype(np.int64)
    mm = np.arange(128)
    jp = np.broadcast_to(mm[None, :] - OFFV, (128, 128))
    A = (jp < (curp[:, None] - 1)).astype(np.float32)
    Bt = np.zeros((128, 128), np.float32)
    forced = (jp == curp[:, None]) | (jp == curp[:, None] - 1)
    invalid = jp > curp[:, None]
    Bt[forced] = (1e4 + (jp + 70))[forced]
    Bt[invalid] = (-1.0 - (jp + 70))[invalid]
    c["atab"] = A
    c["btab"] = Bt
    kk = np.arange(128)
    c["tri"] = np.where(kk[:, None] <= tq[None, :], 0.0, NEG).astype(np.float32)
    c["anti"] = np.where(kk[:, None] > tq[None, :], 0.0, NEG).astype(np.float32)
    key = np.arange(SEQ)
    c["erows"] = (key[None, :] // 64 == np.arange(64)[:, None]).astype(np.float32)
    return c


CONST_SHAPES = {"cs_tok": [SEQ, 64], "cs_cmp": [256, 64], "sel": [128, 64, 128], "selT": [128, 64, 128],
                "tmask": [128, 128], "identf": [128, 128], "jmat": [128, 128], "sgn": [128, 1],
                "evec": [128, NE], "maskc": [128, 512], "atab": [128, 128], "btab": [128, 128],
                "tri": [128, 128], "anti": [128, 128], "erows": [64, SEQ]}


def layout_params(inp):
    L = DEPTH
    f = lambda a: np.ascontiguousarray(np.asarray(a, np.float32))
    o = {}
    trn = lambda a: f(np.asarray(a, np.float32).reshape(L, NFT, 128).transpose(2, 0, 1))
    o["norm_ffn1T"] = trn(inp["norm_ffn1"]); o["norm_ffn2T"] = trn(inp["norm_ffn2"]); o["norm_mixT"] = trn(inp["norm_mix"])
    for k in ("ffn1_wi", "ffn1_wo", "ffn2_wi", "ffn2_wo", "w_in", "ssm_w_glu", "nsa_w_up", "w_out", "cmp_w1", "cmp_w2"):
        o[k] = f(inp[k])
    dup = lambda a: np.concatenate([a, a], axis=0)
    o["lamre"] = f(dup(np.asarray(inp["ssm_lambda_re"]).transpose(2, 0, 1)))
    o["lamim"] = f(dup(np.asarray(inp["ssm_lambda_im"]).transpose(2, 0, 1)))
    o["logdt"] = f(np.broadcast_to(np.asarray(inp["ssm_log_dt"])[None], (128, L, 32)))
    o["bre"] = f(dup(np.asarray(inp["ssm_b_re"]).transpose(2, 0, 1, 3)))
    o["bim"] = f(dup(np.asarray(inp["ssm_b_im"]).transpose(2, 0, 1, 3)))
    o["cre"] = f(dup(np.asarray(inp["ssm_c_re"]).transpose(3, 0, 1, 2)))
    o["cim"] = f(dup(np.asarray(inp["ssm_c_im"]).transpose(3, 0, 1, 2)))
    dsk = np.asarray(inp["ssm_d"])
    o["dvec"] = f(np.tile(dsk.transpose(2, 0, 1), (8, 1, 1)))
    o["qgain"] = f(np.broadcast_to(np.asarray(inp["q_norm"])[None], (128, L, 64)))
    o["kgain"] = f(np.broadcast_to(np.asarray(inp["k_norm"])[None], (128, L, 3, 64)))
    o["peT"] = f(np.asarray(inp["cmp_pe"]).transpose(3, 0, 1, 2))
    return o


PARAM_SHAPES = {"norm_ffn1T": [128, DEPTH, NFT], "norm_ffn2T": [128, DEPTH, NFT], "norm_mixT": [128, DEPTH, NFT],
                "ffn1_wi": [DEPTH, D_MODEL, 2 * D_FF], "ffn1_wo": [DEPTH, D_FF, D_MODEL],
                "ffn2_wi": [DEPTH, D_MODEL, 2 * D_FF], "ffn2_wo": [DEPTH, D_FF, D_MODEL],
                "w_in": [DEPTH, D_MODEL, NCOL], "ssm_w_glu": [DEPTH, 512, 2048], "nsa_w_up": [DEPTH, 512, 1024],
                "w_out": [DEPTH, D_MODEL, D_MODEL], "cmp_w1": [DEPTH, 2, 2048, 128], "cmp_w2": [DEPTH, 2, 128, 64],
                "lamre": [128, DEPTH, 32], "lamim": [128, DEPTH, 32], "logdt": [128, DEPTH, 32],
                "bre": [128, DEPTH, 32, 16], "bim": [128, DEPTH, 32, 16], "cre": [128, DEPTH, 32, 16], "cim": [128, DEPTH, 32, 16],
                "dvec": [128, DEPTH, 32], "qgain": [128, DEPTH, 64], "kgain": [128, DEPTH, 3, 64], "peT": [64, DEPTH, 2, 32]}


def interleave(gens):
    gens = [g for g in gens if g is not None]
    while gens:
        for g in list(gens):
            try:
                next(g)
            except StopIteration:
                gens.remove(g)


class Arena:
    def __init__(self, ap, n, tag):
        self.ap, self.n, self.tag, self.off = ap, n, tag, 0

    def reset(self):
        self.off = 0

    def take(self, shape):
        n = 1
        for v in shape[1:]:
            n *= v
        assert self.off + n <= self.n, (self.tag, self.off, n, self.n)
        v = self.ap[:, self.off:self.off + n]
        self.off += n
        if len(shape) == 3:
            v = v.rearrange("p (a b) -> p a b", a=shape[1])
        elif len(shape) == 4:
            v = v.rearrange("p (a b c) -> p a b c", a=shape[1], b=shape[2])
        return v


def build_program(depth=DEPTH, stages=("ffn1", "mix", "ffn2"), debug=False, mix_parts=("proj", "s5", "nsa", "out")):
    nc = bass.Bass("TRN2", target_bir_lowering=False)
    es = ExitStack()
    D = {}
    for k, shp in list(PARAM_SHAPES.items()) + list(CONST_SHAPES.items()):
        D[k] = nc.dram_tensor(k, list(shp), F32, kind="ExternalInput").ap()
    xT_in = nc.dram_tensor("xT", [D_MODEL, SEQ], F32, kind="ExternalInput").ap()
    outT = nc.dram_tensor("outT", [D_MODEL, SEQ], F32, kind="ExternalOutput").ap()
    skind = "ExternalOutput" if debug else "Internal"
    xres = nc.dram_tensor("xres", [D_MODEL, SEQ], F32, kind=skind).ap()
    uT_d = nc.dram_tensor("uT_d", [512, SEQ], BF16, kind=skind).ap()
    yT_d = nc.dram_tensor("yT_d", [512, SEQ], BF16, kind=skind).ap()
    ocT_d = nc.dram_tensor("ocT_d", [512, SEQ], BF16, kind=skind).ap()
    qkv_d = nc.dram_tensor("qkv_d", [SEQ, QKVW], F32, kind=skind).ap()

    with es:
        P = Prog(nc, es)
        sbt = lambda name, shape, dt: es.enter_context(nc.sbuf_tensor(name, list(shape), dt))
        FA = Arena(sbt("f32arena", [128, F32N], F32), F32N, "f32")
        BA = Arena(sbt("bf16arena", [128, BFN], BF16), BFN, "bf16")
        itmp = sbt("itmp", [128, 8 * NE], mybir.dt.int32)
        gains = {k: sbt("g_" + k, [128, DEPTH, NFT], F32) for k in ("norm_ffn1T", "norm_ffn2T", "norm_mixT")}
        ones_bf = sbt("ones_bf", [128, 128], BF16)
        ident_bf = sbt("ident_bf", [128, 128], BF16)
        identf = sbt("identf_sb", [128, 128], F32)
        banks = [es.enter_context(nc.psum_tensor("bank%d" % i, [128, 512], F32)) for i in range(6)]
        pbank = es.enter_context(nc.psum_tensor("pbank", [128, 1024], BF16))
        pbankB = es.enter_context(nc.psum_tensor("pbankB", [128, 1024], BF16))
        bn = ["bank%d" % i for i in range(6)]

        def MM(out, lhsT, rhs, st, sp, r, w, sgc=False):
            if sgc:
                P.op("pe", lambda e: e.matmul(out, lhsT, rhs, start=st, stop=sp, skip_group_check=True), r, w)
            else:
                P.op("pe", lambda e: e.matmul(out, lhsT, rhs, start=st, stop=sp), r, w)

        def TR(out, in_, r, w):
            P.op("pe", lambda e: e.transpose(out, in_, ident_bf[:]), list(r) + ["ident_bf"], w)

        def ACT(out, in_, func, r, w, **kw):
            P.op("act", lambda e: e.activation(out=out, in_=in_, func=func, **kw), r, w)

        def VTT(eng, out, a, b, op, r, w):
            P.op(eng, lambda e: e.tensor_tensor(out, a, b, op), r, w)

        def TS(eng, out, a, s1, s2, op0, op1, r, w):
            if op1 is None:
                P.op(eng, lambda e: e.tensor_scalar(out, a, s1, s2, op0=op0), r, w)
            else:
                P.op(eng, lambda e: e.tensor_scalar(out, a, s1, s2, op0=op0, op1=op1), r, w)

        def STT(eng, out, in0, scalar, in1, op0, op1, r, w):
            eng = "dve"
            P.op(eng, lambda e: e.scalar_tensor_tensor(out=out, in0=in0, scalar=scalar, in1=in1, op0=op0, op1=op1), r, w)

        def CP(eng, out, in_, r, w):
            if eng == "act":
                ACT(out, in_, AF.Copy, r, w)
            else:
                P.op(eng, lambda e: e.tensor_copy(out, in_), r, w)

        def MEMSET(eng, out, val, w):
            P.op(eng, lambda e: e.memset(out, val), (), w)

        def RECIP(out, in_, r, w):
            P.op("dve", lambda e: e.reciprocal(out, in_), r, w)

        def DMA(q, out, in_, r, w):
            return P.dma(q, out, in_, r, w)

        xview = lambda t: t.rearrange("(ft p) s -> p ft s", p=128)
        dtile = lambda name, ft, tok0: ("dram", name, ft, tok0)

        for k in gains:
            DMA("sp", gains[k][:], D[k][:, :, :], (), ["g_" + k])
        MEMSET("dve", ones_bf[:], 1.0, ["ones"])
        DMA("sp", identf[:], D["identf"][:, :], (), ["identf"])
        DMA("pool", ident_bf[:], D["identf"][:, :], (), ["ident_bf"])

        def norm_tile(pfx, xb, xn, gain_ap, gname, hdst, hname, sq, rstd, ps_n, psn_name):
            ACT(sq[:], xb[:], AF.Square, [xn], [pfx + "sq"])
            for ft in range(NFT):
                MM(ps_n, ones_bf[:], sq[:, ft, :], ft == 0, ft == NFT - 1, [pfx + "sq", "ones"], [psn_name])
            TS("dve", rstd[:], ps_n, 1.0 / D_MODEL, EPS, ALU.mult, ALU.add, [psn_name], [pfx + "rstd"])
            ACT(rstd[:], rstd[:], AF.Sqrt, [pfx + "rstd"], [pfx + "rstd"])
            RECIP(rstd[:], rstd[:], [pfx + "rstd"], [pfx + "rstd"])
            for ft in range(NFT):
                STT("dve" if ft % 2 == 0 else "pool", hdst[:, ft, :], xb[:, ft, :], gain_ap[:, ft:ft + 1], rstd[:],
                    ALU.mult, ALU.mult, [xn, pfx + "rstd", gname], [hname + "_%d" % ft])

        def ffn(l, src, sname, dst, dname, gkey, wi, wo):
            P.barrier(); FA.reset(); BA.reset()
            gain = gains[gkey]
            xt = [FA.take([128, NFT, TT]) for _ in range(2)]
            rstd = FA.take([128, TT])
            sg = [FA.take([128, TT]) for _ in range(2)]
            sq = BA.take([128, NFT, TT])
            hT = BA.take([128, NFT, ST])
            actT = BA.take([128, NFF, ST])
            wi_sb = [BA.take([128, NFT, 256]) for _ in range(3)]
            wo_sb = [BA.take([128, NFF, 128]) for _ in range(2)]
            ps_n, ps_g, ps_u, ps_o = banks[0][:], [banks[1][:], banks[2][:]], [banks[3][:], banks[4][:]], [banks[5][:], banks[0][:]]
            pon_ = [bn[5], bn[0]]
            nst, ntt = SEQ // ST, ST // TT
            wiv = wi[l].rearrange("(kt p) c -> p kt c", p=128)
            wov = wo[l].rearrange("(ft p) c -> p ft c", p=128)
            for s in range(nst):
                for t in range(ntt):
                    tok0 = s * ST + t * TT
                    xb, xn = xt[t % 2], "f_xt%d" % (t % 2)
                    DMA("sp", xb[:], xview(src)[:, :, tok0:tok0 + TT], [dtile(sname, o, tok0) for o in range(NFT)], [xn])
                    norm_tile("f_", xb, xn, gain[:, l, :], "g_" + gkey, hT[:, :, t * TT:(t + 1) * TT], "f_hT%d" % t,
                              sq, rstd, ps_n, bn[0])
                for f in range(NFF):
                    wb, wn = wi_sb[f % 3], "f_wi%d" % (f % 3)
                    DMA("pool", wb[:, :, 0:128], wiv[:, :, f * 128:(f + 1) * 128], (), [wn + "g"])
                    DMA("pool", wb[:, :, 128:256], wiv[:, :, D_FF + f * 128:D_FF + (f + 1) * 128], (), [wn + "u"])
                    for t in range(ntt):
                        pg, pgn, pu, pun = ps_g[t % 2], bn[1 + t % 2], ps_u[t % 2], bn[3 + t % 2]
                        hr = ["f_hT%d_%d" % (t, ft) for ft in range(NFT)]
                        for kt in range(NFT):
                            MM(pg, wb[:, kt, 0:128], hT[:, kt, t * TT:(t + 1) * TT], kt == 0, kt == NFT - 1, hr + [wn + "g"], [pgn])
                        for kt in range(NFT):
                            MM(pu, wb[:, kt, 128:256], hT[:, kt, t * TT:(t + 1) * TT], kt == 0, kt == NFT - 1, hr + [wn + "u"], [pun])
                        sgb, sgn = sg[t % 2], "f_sg%d" % (t % 2)
                        ACT(sgb[:], pg, AF.Silu, [pgn], [sgn])
                        TT_ = "dve"
                        VTT(TT_, actT[:, f, t * TT:(t + 1) * TT], sgb[:], pu, ALU.mult, [sgn, pun], ["f_act%d_%d" % (f, t)])
                for ot in range(NFT):
                    wb, wn = wo_sb[ot % 2], "f_wo%d" % (ot % 2)
                    DMA("pool", wb[:], wov[:, :, ot * 128:(ot + 1) * 128], (), [wn])
                    for t in range(ntt):
                        tok0 = s * ST + t * TT
                        po, pon = ps_o[t % 2], pon_[t % 2]
                        for f in range(NFF):
                            MM(po, wb[:, f, :], actT[:, f, t * TT:(t + 1) * TT], f == 0, f == NFF - 1,
                               ["f_act%d_%d" % (f, t), wn], [pon])
                        rb, rn = sg[t % 2], "f_sg%d" % (t % 2)
                        DMA("sp", rb[:], src[ot * 128:(ot + 1) * 128, tok0:tok0 + TT], [dtile(sname, ot, tok0)], [rn])
                        STT("dve", rb[:], po, 0.5, rb[:], ALU.mult, ALU.add, [pon, rn], [rn])
                        DMA("sp", dst[ot * 128:(ot + 1) * 128, tok0:tok0 + TT], rb[:], [rn], [dtile(dname, ot, tok0)])

        def mixer_proj(l, src, sname):
            P.barrier(); FA.reset(); BA.reset()
            gain = gains["norm_mixT"]
            xt = [FA.take([128, NFT, TT]) for _ in range(2)]
            rstd = FA.take([128, TT])
            qrow = [FA.take([128, QKVW]) for _ in range(2)]
            sq = BA.take([128, NFT, TT])
            hT = BA.take([128, NFT, TT])
            wq = BA.take([128, NFT, GA0])
            ub = [BA.take([128, TT]) for _ in range(2)]
            wv = D["w_in"][l].rearrange("(kt p) c -> p kt c", p=128)
            for kt in range(NFT):
                DMA("pool", wq[:, kt, :], wv[:, kt, 0:GA0], (), ["m_wq%d" % kt])
            wqr = ["m_wq%d" % kt for kt in range(NFT)]
            for tt in range(SEQ // TT):
                tok0 = tt * TT
                xb, xn = xt[tt % 2], "m_xt%d" % (tt % 2)
                DMA("sp", xb[:], xview(src)[:, :, tok0:tok0 + TT], [dtile(sname, o, tok0) for o in range(NFT)], [xn])
                norm_tile("m_", xb, xn, gain[:, l, :], "g_norm_mixT", hT, "m_hT", sq, rstd, banks[0][:], bn[0])
                hr = ["m_hT_%d" % ft for ft in range(NFT)]
                for c in range(4):
                    pb, pn = banks[1 + c % 2][:], bn[1 + c % 2]
                    for kt in range(NFT):
                        MM(pb, wq[:, kt, c * 128:(c + 1) * 128], hT[:, kt, :], kt == 0, kt == NFT - 1, hr + wqr, [pn])
                    u_, un = ub[c % 2], "m_ub%d" % (c % 2)
                    CP("act", u_[:], pb, [pn], [un])
                    DMA("sp", uT_d[c * 128:(c + 1) * 128, tok0:tok0 + TT], u_[:], [un], [("dram", "uT", c, tt)])
                for sub in range(4):
                    qr, qn = qrow[sub % 2], "m_qrow%d" % (sub % 2)
                    for bi, (c0, c1) in enumerate(((512, 1024), (1024, 1536), (1536, GA0))):
                        pb, pn = banks[3 + bi][:], bn[3 + bi]
                        for kt in range(NFT):
                            MM(pb[:, 0:c1 - c0], hT[:, kt, sub * 128:(sub + 1) * 128], wq[:, kt, c0:c1], kt == 0, kt == NFT - 1,
                               hr + wqr, [pn])
                        CP("dve" if bi != 1 else "act", qr[:, c0 - 512:c1 - 512], pb[:, 0:c1 - c0], [pn], [qn + "_%d" % bi])
                    r0 = tok0 + sub * 128
                    DMA("sp", qkv_d[r0:r0 + 128, :], qr[:], [qn + "_%d" % bi for bi in range(3)], [("dram", "qkv", r0 // 128)])

        def range_reduce(x, tmp, n, r):
            iv = itmp[:, 0:n]
            TS("dve", tmp, x, 1.0 / TWO_PI, 64.5, ALU.mult, ALU.add, [r], ["s_rtmp"])
            CP("dve", iv, tmp, ["s_rtmp"], ["s_itmp"])
            CP("dve", tmp, iv, ["s_itmp"], ["s_rtmp"])
            TS("dve", tmp, tmp, -64.0, -TWO_PI, ALU.add, ALU.mult, ["s_rtmp"], ["s_rtmp"])
            VTT("dve", x, x, tmp, ALU.add, [r, "s_rtmp"], [r])
            TS("dve", tmp, x, float(np.pi), -TWO_PI, ALU.is_gt, ALU.mult, [r], ["s_rtmp"])
            VTT("dve", x, x, tmp, ALU.add, [r, "s_rtmp"], [r])
            TS("dve", tmp, x, float(-np.pi), TWO_PI, ALU.is_lt, ALU.mult, [r], ["s_rtmp"])
            VTT("dve", x, x, tmp, ALU.add, [r, "s_rtmp"], [r])

        def s5(l):
            P.barrier(); FA.reset(); BA.reset()
            G8 = 8
            lr = FA.take([128, G8]); li = FA.take([128, G8]); ldt = FA.take([128, G8])
            dvc = FA.take([128, G8]); sgn = FA.take([128, 1]); evec = FA.take([128, NE])
            braw = [FA.take([128, G8, 16]) for _ in range(2)]
            craw = [FA.take([128, G8, 16]) for _ in range(2)]
            bbar = [FA.take([128, G8, 16]) for _ in range(2)]
            t8 = [FA.take([128, G8]) for _ in range(6)]
            marg = FA.take([128, G8, NE]); ang = FA.take([128, G8, NE]); ang2 = FA.take([128, G8, NE])
            rtmp = FA.take([128, G8, NE])
            are = FA.take([128, G8, NE]); aim = FA.take([128, G8, NE])
            S = FA.take([128, G8, 15, 16]); R = FA.take([128, G8, 9, 16])
            T1 = FA.take([128, G8, 15, 16]); T2 = FA.take([128, G8, 15, 16])
            xpad = [[FA.take([128, 1024]) for _ in range(2)] for _ in range(2)]
            tmaskf = FA.take([128, 128]); jmat = FA.take([128, 128])
            ttmp = [FA.take([128, 128]) for _ in range(2)]; mtmp = [FA.take([128, 128]) for _ in range(2)]
            mtmp2 = [FA.take([128, 128]) for _ in range(2)]
            ms = [[FA.take([128, 128]) for _ in range(2)] for _ in range(2)]
            sel = BA.take([128, 64, 128]); selT = BA.take([128, 64, 128])
            utile = BA.take([128, SEQ]); ytile = BA.take([128, SEQ])
            usb = [BA.take([128, 512]) for _ in range(2)]; xprev = [BA.take([128, 512]) for _ in range(2)]
            yact = [BA.take([128, 512]) for _ in range(8)]
            Tg = [BA.take([128, 128]) for _ in range(2)]; Gg = [BA.take([128, 128]) for _ in range(2)]; Hg = [BA.take([128, 128]) for _ in range(2)]
            DMA("pool", sel[:], D["sel"][:, :, :], (), ["s_sel"])
            DMA("pool", selT[:], D["selT"][:, :, :], (), ["s_selT"])
            DMA("sp", tmaskf[:], D["tmask"][:, :], (), ["s_tmask"])
            DMA("sp", jmat[:], D["jmat"][:, :], (), ["s_jmat"])
            DMA("sp", sgn[:], D["sgn"][:, :], (), ["s_sgn"])
            DMA("sp", evec[:], D["evec"][:, :], (), ["s_evec"])
            for pp in range(2):
                for kk in range(2):
                    MEMSET("pool", xpad[pp][kk][:, 0:512], 0.0, ["s_xpz"])
            for bt in range(DBG["s5_nb"]):
                g0 = bt * 8
                gs = slice(g0, g0 + 8)
                DMA("sp", lr[:], D["lamre"][:, l, gs], (), ["s_lr"])
                DMA("sp", li[:], D["lamim"][:, l, gs], (), ["s_li"])
                DMA("sp", ldt[:], D["logdt"][:, l, gs], (), ["s_ldt"])
                DMA("sp", dvc[:], D["dvec"][:, l, gs], (), ["s_dvc"])
                DMA("sp", braw[0][:], D["bre"][:, l, gs, :], (), ["s_bre"])
                DMA("sp", braw[1][:], D["bim"][:, l, gs, :], (), ["s_bim"])
                DMA("sp", craw[0][:], D["cre"][:, l, gs, :], (), ["s_cre"])
                DMA("sp", craw[1][:], D["cim"][:, l, gs, :], (), ["s_cim"])
                DMA("sp", utile[:], uT_d[bt * 128:(bt + 1) * 128, :], [("dram", "uT", bt, tt) for tt in range(8)], ["s_utile"])
                dt_, lrdt, lidt, inv, fre, fim = t8
                ACT(dt_[:], ldt[:], AF.Exp, ["s_ldt"], ["s_dt"])
                VTT("dve", lrdt[:], lr[:], dt_[:], ALU.mult, ["s_lr", "s_dt"], ["s_lrdt"])
                VTT("dve", lidt[:], li[:], dt_[:], ALU.mult, ["s_li", "s_dt"], ["s_lidt"])
                bc_e = lambda t: t[:].unsqueeze(2).to_broadcast([128, G8, NE])
                ev_b = evec[:].unsqueeze(1).to_broadcast([128, G8, NE])
                VTT("dve", marg[:], bc_e(lrdt), ev_b, ALU.mult, ["s_lrdt", "s_evec"], ["s_marg"])
                VTT("dve", ang[:], bc_e(lidt), ev_b, ALU.mult, ["s_lidt", "s_evec"], ["s_ang"])
                TS("dve", ang2[:], ang[:], float(np.pi / 2), None, ALU.add, None, ["s_ang"], ["s_ang2"])
                ACT(marg[:], marg[:], AF.Exp, ["s_marg"], ["s_marg"])
                fl = lambda t: t[:].rearrange("p g e -> p (g e)")
                range_reduce(fl(ang), fl(rtmp), G8 * NE, "s_ang")
                range_reduce(fl(ang2), fl(rtmp), G8 * NE, "s_ang2")
                ACT(ang[:], ang[:], AF.Sin, ["s_ang"], ["s_ang"])
                ACT(ang2[:], ang2[:], AF.Sin, ["s_ang2"], ["s_ang2"])
                VTT("dve", are[:], marg[:], ang2[:], ALU.mult, ["s_marg", "s_ang2"], ["s_are"])
                VTT("dve", aim[:], marg[:], ang[:], ALU.mult, ["s_marg", "s_ang"], ["s_aim"])
                a1r, a1i = are[:, :, 6], aim[:, :, 6]
                u0, u1 = T1[:, :, 0, 0], T1[:, :, 0, 1]
                VTT("dve", inv[:], lr[:], lr[:], ALU.mult, ["s_lr"], ["s_inv"])
                VTT("dve", u0, li[:], li[:], ALU.mult, ["s_li", "s_T1h", "s_T2h"], ["s_T1"])
                VTT("dve", inv[:], inv[:], u0, ALU.add, ["s_inv", "s_T1"], ["s_inv"])
                RECIP(inv[:], inv[:], ["s_inv"], ["s_inv"])
                TS("dve", u0, a1r, -1.0, None, ALU.add, None, ["s_are"], ["s_T1"])
                VTT("dve", fre[:], u0, lr[:], ALU.mult, ["s_T1", "s_lr"], ["s_fre"])
                VTT("dve", u1, a1i, li[:], ALU.mult, ["s_aim", "s_li"], ["s_T1b"])
                VTT("dve", fre[:], fre[:], u1, ALU.add, ["s_fre", "s_T1b"], ["s_fre"])
                VTT("dve", fre[:], fre[:], inv[:], ALU.mult, ["s_fre", "s_inv"], ["s_fre"])
                VTT("dve", fim[:], a1i, lr[:], ALU.mult, ["s_aim", "s_lr"], ["s_fim"])
                VTT("dve", u1, u0, li[:], ALU.mult, ["s_T1", "s_li"], ["s_T1b"])
                VTT("dve", fim[:], fim[:], u1, ALU.subtract, ["s_fim", "s_T1b"], ["s_fim"])
                VTT("dve", fim[:], fim[:], inv[:], ALU.mult, ["s_fim", "s_inv"], ["s_fim"])
                bc_c = lambda t: t[:].unsqueeze(2).to_broadcast([128, G8, 16])
                w0, w1_ = T2[:, :, 0, :], T2[:, :, 1, :]
                VTT("dve", w0, bc_c(fre), braw[0][:], ALU.mult, ["s_fre", "s_bre"], ["s_T2"])
                VTT("dve", w1_, bc_c(fim), braw[1][:], ALU.mult, ["s_fim", "s_bim"], ["s_T2b"])
                VTT("dve", bbar[0][:], w0, w1_, ALU.subtract, ["s_T2", "s_T2b"], ["s_bbr"])
                VTT("dve", w0, bc_c(fre), braw[1][:], ALU.mult, ["s_fre", "s_bim"], ["s_T2"])
                VTT("dve", w1_, bc_c(fim), braw[0][:], ALU.mult, ["s_fim", "s_bre"], ["s_T2b"])
                VTT("dve", bbar[1][:], w0, w1_, ALU.add, ["s_T2", "s_T2b"], ["s_bbi"])
                lo, hi = slice(0, 64), slice(64, 128)
                Ab = lambda t, h, k0, k1, n: t[h, :, k0:k1].unsqueeze(3).to_broadcast([64, G8, k1 - k0, 16])
                Zb = lambda t, h, n: t[h, :, :].unsqueeze(2).to_broadcast([64, G8, n, 16])
                XD = ["s_bbr", "s_bbi", "s_fre", "s_fim", "s_T1", "s_T1b", "s_T2", "s_T2b"]
                VTT("dve", T1[lo], Ab(are, lo, 0, 15, 15), Zb(bbar[0], lo, 15), ALU.mult, ["s_are"] + XD, ["s_T1"])
                VTT("pool", T2[lo], Ab(aim, lo, 0, 15, 15), Zb(bbar[1], lo, 15), ALU.mult, ["s_aim"] + XD, ["s_T2"])
                VTT("dve", S[lo], T1[lo], T2[lo], ALU.subtract, ["s_T1", "s_T2"], ["s_Slo"])
                VTT("dve", T1[hi], Ab(are, hi, 0, 15, 15), Zb(bbar[1], hi, 15), ALU.mult, ["s_are"] + XD, ["s_T1h"])
                VTT("pool", T2[hi], Ab(aim, hi, 0, 15, 15), Zb(bbar[0], hi, 15), ALU.mult, ["s_aim"] + XD, ["s_T2h"])
                VTT("dve", S[hi], T1[hi], T2[hi], ALU.add, ["s_T1h", "s_T2h"], ["s_Shi"])
                T1r, T2r = T1[:, :, 0:9, :], T2[:, :, 0:9, :]
                VTT("dve", T1r[lo], Ab(are, lo, 15, 24, 9), Zb(craw[0], lo, 9), ALU.mult, ["s_are", "s_cre", "s_Slo"], ["s_T1"])
                VTT("pool", T2r[lo], Ab(aim, lo, 15, 24, 9), Zb(craw[1], lo, 9), ALU.mult, ["s_aim", "s_cim", "s_Slo"], ["s_T2"])
                VTT("dve", R[lo], T1r[lo], T2r[lo], ALU.subtract, ["s_T1", "s_T2"], ["s_Rlo"])
                VTT("dve", T1r[hi], Ab(are, hi, 15, 24, 9), Zb(craw[1], hi, 9), ALU.mult, ["s_are", "s_cim", "s_Shi"], ["s_T1h"])
                VTT("pool", T2r[hi], Ab(aim, hi, 15, 24, 9), Zb(craw[0], hi, 9), ALU.mult, ["s_aim", "s_cre", "s_Shi"], ["s_T2h"])
                STT("dve", R[hi], T1r[hi], -1.0, T2r[hi], ALU.mult, ALU.subtract, ["s_T1h", "s_T2h"], ["s_Rhi"])
                TS("dve", aim[:, :, 24:33], aim[:, :, 24:33], sgn[:, 0:1], -1.0, ALU.mult, ALU.mult, ["s_aim", "s_sgn"], ["s_aims"])
                Sr, Rr = ["s_Slo", "s_Shi"], ["s_Rlo", "s_Rhi"]
                def s5_group(gg, par):
                    q = "%d" % par
                    usb_, xprev_, Tg_, Gg_, Hg_ = usb[par], xprev[par], Tg[par], Gg[par], Hg[par]
                    xp, ms_, mtmp_, mtmp2_, ttmp_ = xpad[par], ms[par], mtmp[par], mtmp2[par], ttmp[par]
                    Pm = S[:, gg, 7:15, :].rearrange("p i c -> p (i c)")
                    Gs = S[:, gg, 0:8, :].rearrange("p i c -> p (i c)")
                    Qm = R[:, gg, 0:8, :].rearrange("p j c -> p (j c)")
                    Hm = R[:, gg, 1:9, :].rearrange("p j c -> p (j c)")
                    pt, ptn = banks[0][:, 0:128], bn[0]
                    MM(pt, Pm, Qm, True, True, Sr + Rr, [ptn])
                    VTT("dve", ttmp_[:], pt, tmaskf[:], ALU.mult, [ptn, "s_tmask"], ["s_ttmp" + q])
                    yield
                    pu, pun = banks[1][:], bn[1]
                    for i in range(8):
                        MM(pu, sel[:, gg * 8 + i, :], utile[:, i::8], i == 0, i == 7, ["s_sel", "s_utile"], [pun])
                    CP("act", usb_[:], pu, [pun], ["s_usb" + q])
                    yield
                    CP("act", Hg_[:], Hm, Rr, ["s_Hg" + q])
                    STT("dve", Tg_[:], identf[:], dvc[:, gg:gg + 1], ttmp_[:], ALU.mult, ALU.add, ["identf", "s_dvc", "s_ttmp" + q], ["s_Tg" + q])
                    yield
                    pg_, pgn = banks[0][:, 128:256], bn[0]
                    MM(pg_, Gs, identf[:], True, True, Sr + ["identf"], [pgn])
                    CP("act", Gg_[:], pg_, [pgn], ["s_Gg" + q])
                    yield
                    pv, pvn = banks[2][:], bn[2]
                    MM(pv, Gg_[:], usb_[:], True, True, ["s_Gg" + q, "s_usb" + q], [pvn])
                    CP("dve", xp[0][:, 512:1024], pv, [pvn], ["s_xp0_" + q])
                    yield
                    cur = 0
                    pz, pzn = banks[3 + par][:], bn[3 + par]
                    for s_ in range(9):
                        sh = 2 ** s_
                        xc, xcn = xp[cur], "s_xp%d_%s" % (cur, q)
                        xn_, xnn = xp[1 - cur], "s_xp%d_%s" % (1 - cur, q)
                        MM(pz, jmat[:], xc[:, 512 - sh:1024 - sh], True, True, ["s_jmat", xcn, "s_xpz"], [pzn])
                        STT("dve", xn_[:, 512:1024], xc[:, 512 - sh:1024 - sh], are[:, gg, 24 + s_:25 + s_], xc[:, 512:1024],
                            ALU.mult, ALU.add, [xcn, "s_xpz", "s_are"], [xnn])
                        yield
                        STT("dve", xn_[:, 512:1024], pz, aim[:, gg, 24 + s_:25 + s_], xn_[:, 512:1024], ALU.mult, ALU.add,
                            [pzn, xnn, "s_aims", "s_aim"], [xnn])
                        yield
                        cur = 1 - cur
                    CP("act", xprev_[:], xp[cur][:, 511:1023], ["s_xp%d_%s" % (cur, q), "s_xpz"], ["s_xprev" + q])
                    yield
                    py, pyn = banks[5][:], bn[5]
                    MM(py, Tg_[:], usb_[:], True, False, ["s_Tg" + q, "s_usb" + q], [pyn])
                    MM(py, Hg_[:], xprev_[:], False, True, ["s_Hg" + q, "s_xprev" + q], [pyn])
                    ACT(yact[gg][:], py, AF.Gelu_apprx_tanh, [pyn], ["s_yact%d" % gg])
                    yield

                for gp in range(DBG["s5_ng"]):
                    interleave([s5_group(2 * gp, 0), s5_group(2 * gp + 1, 1)])
                if not DBG["s5_inv"]:
                    continue
                for j in range(8):
                    pj, pjn = banks[1 + j % 2][:], bn[1 + j % 2]
                    for gg in range(8):
                        MM(pj, selT[:, gg * 8 + j, :], yact[gg][:], gg == 0, gg == 7, ["s_selT", "s_yact%d" % gg], [pjn])
                    CP("dve" if j % 2 == 0 else "act", ytile[:, j::8], pj, [pjn], ["s_ytile%d" % j])
                DMA("sp", yT_d[bt * 128:(bt + 1) * 128, :], ytile[:], ["s_ytile%d" % j for j in range(8)], [("dram", "yT", bt)])

        def normrope(pfx, src, gain, cs, dst, T, H, tmp, r):
            sqv, xn, ssq, t1, t2 = tmp
            n = T * H
            v4 = lambda t, w: t[:, 0:n * w].rearrange("p (t h d) -> p t h d", t=T, h=H)
            sq4, xn4 = v4(sqv, 64), v4(xn, 64)
            VTT("dve", sq4, src, src, ALU.mult, r, [pfx + "sq"])
            yield
            P.op("dve", lambda e: e.tensor_reduce(out=ssq[:, 0:n], in_=sqv[:, 0:n * 64].rearrange("p (a d) -> p a d", d=64),
                                                  axis=AX.X, op=ALU.add), [pfx + "sq"], [pfx + "ssq"])
            yield
            TS("dve", ssq[:, 0:n], ssq[:, 0:n], 1.0 / 64, EPS, ALU.mult, ALU.add, [pfx + "ssq"], [pfx + "ssq"])
            yield
            ACT(ssq[:, 0:n], ssq[:, 0:n], AF.Sqrt, [pfx + "ssq"], [pfx + "ssq"])
            yield
            RECIP(ssq[:, 0:n], ssq[:, 0:n], [pfx + "ssq"], [pfx + "ssq"])
            yield
            rs4 = ssq[:, 0:n].rearrange("p (t h) -> p t h", t=T).unsqueeze(3).to_broadcast([128, T, H, 64])
            VTT("dve", xn4, src, rs4, ALU.mult, list(r) + [pfx + "ssq"], [pfx + "xn"])
            yield
            g4 = gain.unsqueeze(2).to_broadcast([128, T, H, 64])
            VTT("pool", xn4, xn4, g4, ALU.mult, [pfx + "xn", pfx + "gain"], [pfx + "xn"])
            yield
            c4 = cs[:, 0:32].unsqueeze(1).unsqueeze(1).to_broadcast([128, T, H, 32])
            s4 = cs[:, 32:64].unsqueeze(1).unsqueeze(1).to_broadcast([128, T, H, 32])
            x1, x2 = xn4[:, :, :, 0:32], xn4[:, :, :, 32:64]
            a4, b4 = v4(t1, 32), v4(t2, 32)
            VTT("dve", a4, x1, c4, ALU.mult, [pfx + "xn", pfx + "cs"], [pfx + "t1"])
            yield
            VTT("pool", b4, x2, s4, ALU.mult, [pfx + "xn", pfx + "cs"], [pfx + "t2"])
            yield
            VTT("dve", dst[:, :, :, 0:32], a4, b4, ALU.subtract, [pfx + "t1", pfx + "t2"], [pfx + "dstA"])
            yield
            VTT("dve", a4, x2, c4, ALU.mult, [pfx + "xn", pfx + "cs", pfx + "dstA"], [pfx + "t1"])
            yield
            VTT("pool", b4, x1, s4, ALU.mult, [pfx + "xn", pfx + "cs", pfx + "dstA"], [pfx + "t2"])
            yield
            VTT("dve", dst[:, :, :, 32:64], a4, b4, ALU.add, [pfx + "t1", pfx + "t2"], [pfx + "dstB"])
            yield

        def nsa(l):
            P.barrier(); FA.reset(); BA.reset()
            NB = SEQ // 128
            kselT = [BA.take([128, SEQ]) for _ in range(2)]
            kwinT = [BA.take([128, SEQ]) for _ in range(2)]
            vsel = BA.take([128, 2, NB, 65]); vwin = BA.take([128, 2, NB, 65])
            kcT = [BA.take([128, 256]) for _ in range(2)]
            vc = BA.take([128, 2, 2, 64])
            tri = BA.take([128, 128]); anti = BA.take([128, 128])
            tri4 = BA.take([128, 4, 128]); anti4 = BA.take([128, 4, 128])
            w2sb = BA.take([128, 2, 64])
            bf_mark = BA.off
            qg = FA.take([128, 1, 64]); kg = FA.take([128, 3, 64])
            maskc = FA.take([128, 512]); atab = FA.take([128, 128]); btab = FA.take([128, 128])
            bias = FA.take([128, 2])
            cs = [FA.take([128, 64]) for _ in range(2)]
            tmp = (FA.take([128, 768]), FA.take([128, 768]), FA.take([128, 16]), FA.take([128, 384]), FA.take([128, 384]))
            f_mark = FA.off
            for g in range(2):
                DMA("pool", kselT[g][64:128, :], D["erows"][:, :], (), ["n_erows%d" % g])
            MEMSET("pool", vsel[:, :, :, 64:65], 1.0, ["n_vsel1"])
            MEMSET("pool", vwin[:, :, :, 64:65], 1.0, ["n_vwin1"])
            DMA("pool", tri[:], D["tri"][:, :], (), ["n_tri"])
            DMA("pool", anti[:], D["anti"][:, :], (), ["n_anti"])
            CP("dve", tri4[:], tri[:].unsqueeze(1).to_broadcast([128, 4, 128]), ["n_tri"], ["n_tri4"])
            CP("dve", anti4[:], anti[:].unsqueeze(1).to_broadcast([128, 4, 128]), ["n_anti"], ["n_anti4"])
            DMA("sp", qg[:, 0, :], D["qgain"][:, l, :], (), ["n_qg"])
            DMA("sp", kg[:], D["kgain"][:, l, :, :], (), ["n_kg"])
            TS("dve", qg[:], qg[:], 0.125, None, ALU.mult, None, ["n_qg"], ["n_qg"])
            DMA("sp", maskc[:], D["maskc"][:, :], (), ["n_maskc"])
            DMA("sp", atab[:], D["atab"][:, :], (), ["n_atab"])
            DMA("sp", btab[:], D["btab"][:, :], (), ["n_btab"])
            DMA("pool", w2sb[:], D["cmp_w2"][l].rearrange("t h d -> h t d"), (), ["n_w2"])
            kcmpT = BA.take([128, SEQ]); vcmpT = BA.take([128, SEQ])
            w1sb = [BA.take([128, 32, 128]) for _ in range(2)]
            peT = BA.take([64, 2, 32])
            kb = BA.take([128, 2, 2, 64])
            cb = BA.take([128, 2, 128])
            hidT = BA.take([128, 256])
            kcn = BA.take([128, 1, 2, 64])
            kvrow = [FA.take([128, 768]) for _ in range(2)]
            kcraw = FA.take([128, 2, 2, 64])
            for ty in range(2):
                w1v = D["cmp_w1"][l, ty].rearrange("(l d) h -> d l h", d=64)
                DMA("pool", w1sb[ty][0:64], w1v, (), ["n_w1_%d_lo" % ty])
                DMA("pool", w1sb[ty][64:128], w1v, (), ["n_w1_%d_hi" % ty])
            DMA("pool", peT[0:64], D["peT"][:, l, :, :], (), ["n_peT"])
            for ty in range(2):
                pb, pn = banks[0][:, ty:ty + 1], bn[0]
                for ll in range(32):
                    MM(pb, w1sb[ty][0:64, ll, :], peT[0:64, ty, ll:ll + 1], ll == 0, ll == 31, ["n_w1_%d_lo" % ty, "n_peT"], [pn])
                CP("dve", bias[:, ty:ty + 1], pb, [pn], ["n_bias%d" % ty])
            pq = pbank[:, 0:512]
            for tb in range(NB):
                kr, krn = kvrow[tb % 2], "n_kvrow%d" % (tb % 2)
                c_, cn = cs[tb % 2], "n_cs%d" % (tb % 2)
                DMA("sp", kr[:], qkv_d[tb * 128:(tb + 1) * 128, 512:1280], [("dram", "qkv", tb)], [krn])
                DMA("sp", c_[:], D["cs_tok"][tb * 128:(tb + 1) * 128, :], (), [cn])
                src = kr[:, 256:768].rearrange("p (t x h d) -> p t x h d", t=2, x=2, h=2)[:, :, 0]
                P.buf["n_k_gain"] = P.buf.get("n_kg", {"w": None, "r": {}})
                P.buf["n_k_cs"] = P.buf.get(cn, {"w": None, "r": {}})
                for _ in normrope("n_k_", src, kg[:, 1:3, :], c_, kb[:], 2, 2, tmp, [krn]):
                    pass
                for ty in range(2):
                    for h in range(2):
                        po = pq[0:64, (ty * 2 + h) * 128:(ty * 2 + h + 1) * 128]
                        TR(po, kb[:, ty, h, :], ["n_k_dstA", "n_k_dstB"], ["pbank"])
                        dstt = (kselT if ty == 0 else kwinT)[h]
                        CP("act" if h == 0 else "dve", dstt[0:64, tb * 128:(tb + 1) * 128], po, ["pbank"],
                           ["n_kT%d_%d_%d" % (ty, h, tb)])
                vsrc = kr[:, 256:768].rearrange("p (t x h d) -> p t x h d", t=2, x=2, h=2)
                CP("pool", vsel[:, :, tb, 0:64], vsrc[:, 0, 1], [krn], ["n_vsel_%d" % tb])
                CP("pool", vwin[:, :, tb, 0:64], vsrc[:, 1, 1], [krn], ["n_vwin_%d" % tb])
                CP("pool", cb[:], kr[:, 0:256].rearrange("p (t c) -> p t c", t=2), [krn], ["n_cb"])
                for ty in range(2):
                    po = pbank[:, 512 + ty * 128:512 + (ty + 1) * 128]
                    TR(po, cb[:, ty, :], ["n_cb"], ["pbank"])
                    CP("act" if ty == 0 else "dve", (kcmpT if ty == 0 else vcmpT)[:, tb * 128:(tb + 1) * 128], po,
                       ["pbank"], ["n_cT%d_%d" % (ty, tb)])
            MEMSET("dve", hidT[:, 255:256], 0.0, ["n_hid255"])
            for ty in range(2):
                xT_ = kcmpT if ty == 0 else vcmpT
                xr = ["n_cT%d_%d" % (ty, tb) for tb in range(NB)]
                for g in range(2):
                    ph, phn = banks[1][:, 0:255], bn[1]
                    hs = slice(64 * g, 64 * g + 64)
                    for ll in range(32):
                        MM(ph, w1sb[ty][hs, ll, :], xT_[hs, ll:ll + 16 * 254 + 1:16], ll == 0, ll == 31,
                           xr + ["n_w1_%d_%s" % (ty, "lo" if g == 0 else "hi")], [phn])
                    ACT(hidT[:, 0:255], ph, AF.Gelu_apprx_tanh, [phn, "n_bias%d" % ty], ["n_hidT"], bias=bias[:, ty:ty + 1])
                    for c in range(2):
                        po, pon = banks[2][:, c * 64:(c + 1) * 64], bn[2]
                        MM(po, hidT[:, c * 128:(c + 1) * 128], w2sb[:, ty, :], True, True, ["n_hidT", "n_hid255", "n_w2"], [pon])
                        if ty == 1:
                            CP("act", vc[:, g, c, :], po, [pon], ["n_vc%d_%d" % (g, c)])
                        else:
                            CP("act", kcraw[:, c, g, :], po, [pon], ["n_kcraw%d_%d" % (c, g)])
            for c in range(2):
                c_, cn = cs[c], "n_cs%d" % c
                DMA("sp", c_[:], D["cs_cmp"][c * 128:(c + 1) * 128, :], (), [cn])
                P.buf["n_c_gain"] = P.buf.get("n_kg", {"w": None, "r": {}})
                P.buf["n_c_cs"] = P.buf.get(cn, {"w": None, "r": {}})
                for _ in normrope("n_c_", kcraw[:, c:c + 1, :, :], kg[:, 0:1, :], c_, kcn[:], 1, 2, tmp,
                                  ["n_kcraw%d_%d" % (c, g) for g in range(2)]):
                    pass
                for g in range(2):
                    po = pq[0:64, g * 128:(g + 1) * 128]
                    TR(po, kcn[:, 0, g, :], ["n_c_dstA", "n_c_dstB"], ["pbank"])
                    CP("act", kcT[g][0:64, c * 128:(c + 1) * 128], po, ["pbank"], ["n_kcT%d_%d" % (g, c)])
            P.barrier()
            BA.off = bf_mark; FA.off = f_mark
            qaug = [[BA.take([128, 512]) for _ in range(2)] for _ in range(3)]
            PT = [BA.take([128, 512]) for _ in range(3)]
            qn = BA.take([128, 1, 8, 64])
            ident4 = BA.take([128, 4, 128])
            nm = BA.take([128, 2, 128])
            otm = [BA.take([128, 8, 64]) for _ in range(2)]
            ocsb = BA.take([128, 4, 128])
            maskc_bf = BA.take([128, 512])
            qrow = [FA.take([128, 512]) for _ in range(3)]
            cs3 = [FA.take([128, 64]) for _ in range(3)]
            grow = [FA.take([128, 24]) for _ in range(3)]
            gsig = [FA.take([128, 24]) for _ in range(3)]
            pun = [FA.take([128, 2, 256]) for _ in range(4)]
            psumh = FA.take([128, 2, 256])
            den = FA.take([128, 4, 2]); rden = FA.take([128, 4, 2]); cfcs = [FA.take([128, 8]) for _ in range(2)]; thr = FA.take([128, 2])
            CC = FA.take([128, 2, 32]); SS = FA.take([128, 2, 32]); sqv = FA.take([128, 512]); ssq = FA.take([128, 8])
            ta = FA.take([128, 8, 32]); tb = FA.take([128, 8, 32]); tc_ = FA.take([128, 8, 32]); td = FA.take([128, 8, 32])
            dd = FA.take([128, 8, 64])
            imp = FA.take([128, 2, 64]); prio = FA.take([128, 2, 64]); wk = FA.take([128, 2, 64])
            m1 = FA.take([128, 2, 64]); m2 = FA.take([128, 2, 64])
            mx = [FA.take([128, 2, 8]) for _ in range(2)]
            coef = FA.take([128, 2, 3, 4])
            ocs = [FA.take([128, 8, 64]) for _ in range(2)]
            tA = [FA.take([128, 4, 64]) for _ in range(2)]; tB = [FA.take([128, 4, 64]) for _ in range(2)]; tC = [FA.take([128, 4, 64]) for _ in range(2)]
            MEMSET("dve", nm[:, :, 0:64], 0.0, ["n_nm0"])
            CP("dve", maskc_bf[:], maskc[:], ["n_maskc"], ["n_maskcb"])
            CP("dve", ident4[:], ident_bf[:].unsqueeze(1).to_broadcast([128, 4, 128]), ["ident_bf"], ["n_ident4"])
            psc_b, psc_n = banks[0], bn[0]
            poc_b, poc_n = banks[1], bn[1]
            st_b = [(banks[2][:], bn[2]), (banks[3][:], bn[3])]
            pos_b, pos_n = banks[4], bn[4]
            pow_b, pow_n = banks[5], bn[5]
            pqA = pbank
            stepc = [0]

            def stageA_prep(qb):
                pa = qb % 3
                qr, qrn = qrow[pa], "a_qrow%d" % pa
                c_, cn = cs3[pa], "a_cs%d" % pa
                gr, grn, gs_, gsn = grow[pa], "a_grow%d" % pa, gsig[pa], "a_gsig%d" % pa
                DMA("sp", qr[:], qkv_d[qb * 128:(qb + 1) * 128, 0:512], [("dram", "qkv", qb)], [qrn])
                DMA("sp", gr[:], qkv_d[qb * 128:(qb + 1) * 128, 1280:1304], [("dram", "qkv", qb)], [grn])
                DMA("sp", c_[:], D["cs_tok"][qb * 128:(qb + 1) * 128, :], (), [cn])
                yield
                yield
                q3 = qr[:].rearrange("p (h d) -> p h d", h=8)
                x1, x2 = q3[:, :, 0:32], q3[:, :, 32:64]
                g2 = qg[:, 0, :].rearrange("p (a d) -> p a d", a=2)
                VTT("pool", CC[:], c_[:, 0:32].unsqueeze(1).to_broadcast([128, 2, 32]), g2, ALU.mult, [cn, "n_qg"], ["a_CC"])
                VTT("pool", SS[:], c_[:, 32:64].unsqueeze(1).to_broadcast([128, 2, 32]), g2, ALU.mult, [cn, "n_qg"], ["a_SS"])
                VTT("dve", sqv[:], qr[:], qr[:], ALU.mult, [qrn], ["a_sqv"])
                bc8 = lambda t: t.unsqueeze(1).to_broadcast([128, 8, 32])
                VTT("pool", ta[:], x1, bc8(CC[:, 0, :]), ALU.mult, [qrn, "a_CC"], ["a_ta"])
                yield
                P.op("dve", lambda e: e.tensor_reduce(out=ssq[:], in_=sqv[:].rearrange("p (h d) -> p h d", h=8), axis=AX.X, op=ALU.add),
                     ["a_sqv"], ["a_ssq"])
                VTT("pool", tb[:], x2, bc8(SS[:, 1, :]), ALU.mult, [qrn, "a_SS"], ["a_tb"])
                VTT("pool", tc_[:], x2, bc8(CC[:, 1, :]), ALU.mult, [qrn, "a_CC"], ["a_tc"])
                yield
                TS("dve", ssq[:], ssq[:], 1.0 / 64, EPS, ALU.mult, ALU.add, ["a_ssq"], ["a_ssq"])
                VTT("pool", td[:], x1, bc8(SS[:, 0, :]), ALU.mult, [qrn, "a_SS"], ["a_td"])
                VTT("pool", dd[:, :, 0:32], ta[:], tb[:], ALU.subtract, ["a_ta", "a_tb"], ["a_dd1"])
                yield
                ACT(gs_[:], gr[:], AF.Sigmoid, [grn], [gsn])
                yield
                ACT(ssq[:], ssq[:], AF.Sqrt, ["a_ssq"], ["a_ssq"])
                VTT("pool", dd[:, :, 32:64], tc_[:], td[:], ALU.add, ["a_tc", "a_td"], ["a_dd2"])
                yield
                RECIP(ssq[:], ssq[:], ["a_ssq"], ["a_ssq"])
                yield
                VTT("dve", qn[:, 0], dd[:], ssq[:].unsqueeze(2).to_broadcast([128, 8, 64]), ALU.mult, ["a_dd1", "a_dd2", "a_ssq"], ["a_qn"])
                yield
                for hh in range(8):
                    TR(pqA[0:64, hh * 128:(hh + 1) * 128], qn[:, 0, hh, :], ["a_qn"], ["pbank"])
                for g in range(2):
                    CP("dve", qaug[pa][g][0:64, :], pqA[0:64, g * 512:(g + 1) * 512], ["pbank"], ["a_qaug%d_%d_q" % (pa, g)])
                yield

            def stageA_rest(qb):
                pa = qb % 3
                gs_, gsn = gsig[pa], "a_gsig%d" % pa
                nch = 1 if qb <= 15 else 2
                ncol = 128 * nch
                o0 = OFFC - 8 * qb
                poc = poc_b[:, 0:512].rearrange("p (h d) -> p h d", h=8)
                MEMSET("pool", den[:], 0.0, ["a_den"])
                if ncol < 256:
                    MEMSET("pool", psumh[:, :, ncol:256], 0.0, ["a_psumhz"])
                slots = [(g, hp) for hp in range(2) for g in range(2)]
                for (g, hp) in slots:
                    si = 2 * g + hp
                    psc = psc_b[:, 0:2 * ncol].rearrange("p (h n) -> p h n", h=2)
                    for hl in range(2):
                        h = 2 * hp + hl
                        MM(psc[:, hl, :], qaug[pa][g][0:64, h * 128:(h + 1) * 128], kcT[g][0:64, 0:ncol], True, False,
                           ["a_qaug%d_%d_q" % (pa, g)] + ["n_kcT%d_%d" % (g, c) for c in range(2)], [psc_n])
                        MM(psc[:, hl, :], ident_bf[:], maskc_bf[:, o0:o0 + ncol], False, True, ["ident_bf", "n_maskcb"], [psc_n])
                    yield
                    for hl in range(2):
                        ACT(pun[si][:, hl, 0:ncol], psc[:, hl, :], AF.Exp, [psc_n, "a_den"], ["a_pun%d_%d" % (si, hl), "a_den%d_%d" % (si, hl)],
                            accum_out=den[:, si, hl:hl + 1])
                    yield
                dn = ["a_den%d_%d" % (si, hl) for si in range(4) for hl in range(2)]
                if qb == 0:
                    TS("dve", den[:], den[:], 1e-30, None, ALU.max, None, dn + ["a_den"], dn + ["a_den"])
                    yield
                RECIP(rden[:], den[:], dn + ["a_den"], ["a_rden"])
                yield
                g8 = gs_[:].rearrange("p (h b) -> p h b", b=3)
                VTT("pool", cfcs[qb % 2][:], g8[:, :, 0], rden[:].rearrange("p s h -> p (s h)"), ALU.mult, [gsn, "a_rden"], ["a_cfc%d" % (qb % 2)])
                for k in range(4):
                    for g in range(2):
                        si, hl = 2 * g + k // 2, k % 2
                        if k == 0:
                            TS("dve", psumh[:, g, 0:ncol], pun[si][:, hl, 0:ncol], rden[:, si, hl:hl + 1], None, ALU.mult, None,
                               ["a_pun%d_%d" % (si, hl), "a_rden"], ["a_psumh%d" % g])
                        else:
                            STT("dve", psumh[:, g, 0:ncol], pun[si][:, hl, 0:ncol], rden[:, si, hl:hl + 1], psumh[:, g, 0:ncol],
                                ALU.mult, ALU.add, ["a_pun%d_%d" % (si, hl), "a_rden", "a_psumh%d" % g], ["a_psumh%d" % g])
                    yield
                pr = ["a_psumh0", "a_psumh1", "a_psumhz"]
                p4 = psumh[:].rearrange("p g (j r) -> p g j r", r=4)
                VTT("dve", imp[:], p4[:, :, :, 0], p4[:, :, :, 1], ALU.add, pr, ["a_imp"])
                STT("dve", wk[:], p4[:, :, :, 3], 0.5, p4[:, :, :, 2], ALU.mult, ALU.add, pr, ["a_wk"])
                yield
                VTT("dve", imp[:], imp[:], wk[:], ALU.add, ["a_imp", "a_wk"], ["a_imp"])
                yield
                STT("dve", imp[:, :, 1:64], p4[:, :, 0:63, 3], 0.5, imp[:, :, 1:64], ALU.mult, ALU.add, pr + ["a_imp"], ["a_imp"])
                yield
                v0 = OFFV - 2 * qb
                VTT("dve", prio[:], imp[:], atab[:, v0:v0 + 64].unsqueeze(1).to_broadcast([128, 2, 64]), ALU.mult, ["a_imp", "n_atab"], ["a_prio"])
                yield
                VTT("dve", prio[:], prio[:], btab[:, v0:v0 + 64].unsqueeze(1).to_broadcast([128, 2, 64]), ALU.add, ["a_prio", "n_btab"], ["a_prio"])
                yield
                MEMSET("dve", prio[:, :, 0:1], 2e4, ["a_prio"])
                yield
                for g in range(2):
                    P.op("dve", lambda e, g=g: e.max(out=mx[0][:, g, :], in_=prio[:, g, :]), ["a_prio"], ["a_mx0_%d" % g])
                yield
                for g in range(2):
                    P.op("dve", lambda e, g=g: e.match_replace(out=wk[:, g, :], in_to_replace=mx[0][:, g, :], in_values=prio[:, g, :],
                                                               imm_value=-1e9), ["a_prio", "a_mx0_%d" % g, "a_wk"], ["a_wk%d" % g])
                yield
                for g in range(2):
                    P.op("dve", lambda e, g=g: e.max(out=mx[1][:, g, :], in_=wk[:, g, :]), ["a_wk%d" % g], ["a_mx1_%d" % g])
                yield
                TS("dve", thr[:], mx[1][:, :, 7], -0.5, None, ALU.max, None, ["a_mx1_0", "a_mx1_1"], ["a_thr"])
                yield
                for g in range(2):
                    TS("dve", nm[:, g, 64:128], prio[:, g, :], thr[:, g:g + 1], NEG, ALU.is_lt, ALU.mult, ["a_prio", "a_thr"], ["a_nm%d" % g])
                yield
                for g in range(2):
                    pnm = pqA[:, g * 128:(g + 1) * 128]
                    TR(pnm, nm[:, g, :], ["a_nm%d" % g, "n_nm0"], ["pbank"])
                for g in range(2):
                    pnm = pqA[:, g * 128:(g + 1) * 128]
                    CP("dve", qaug[pa][g][64:128, :].rearrange("p (h q) -> p h q", h=4),
                       pnm[64:128, :].unsqueeze(1).to_broadcast([64, 4, 128]), ["pbank"], ["a_qaug%d_%d_m" % (pa, g)])
                yield

            def alloc_step():
                k = stepc[0]
                stepc[0] += 1
                return st_b[k % 2], (PT[k % 3], "a_PT%d" % (k % 3))

            def pathII_steps(qb):
                pa = qb % 3
                nch = 1 if qb <= 15 else 2
                o0 = OFFC - 8 * qb
                poc = poc_b[:, 0:512].rearrange("p (h d) -> p h d", h=8)
                steps = []
                state = {"first": True}
                for c in range(nch):
                    for g in range(2):
                        def mk(c=c, g=g):
                            (pst, pstn), (pt_, ptn_) = alloc_step()

                            def s_():
                                MM(pst, kcT[g][0:64, c * 128:(c + 1) * 128], qaug[pa][g][0:64, :], True, False,
                                   ["a_qaug%d_%d_q" % (pa, g), "n_kcT%d_%d" % (g, c)], [pstn])
                                MM(pst, maskc_bf[:, o0 + c * 128:o0 + (c + 1) * 128], ident4[:].rearrange("p h q -> p (h q)"), False, True,
                                   ["n_maskcb", "n_ident4"], [pstn])

                            def e_():
                                ACT(pt_[:], pst, AF.Exp, [pstn], [ptn_])

                            def p_():
                                for h in range(4):
                                    MM(poc[:, 4 * g + h, :], pt_[:, h * 128:(h + 1) * 128], vc[:, g, c, :], state["first"], c == nch - 1,
                                       [ptn_, "n_vc%d_%d" % (g, c)], [poc_n], sgc=True)
                                    state["first"] = False
                            return {"s": s_, "e": e_, "p": p_}
                        steps.append(mk)
                return steps

            def B_steps(qb):
                pa = qb % 3
                p2 = qb % 2
                gs_, gsn = gsig[pa], "a_gsig%d" % pa
                g4 = gs_[:].rearrange("p (g h b) -> p g h b", g=2, h=4)
                steps = []
                for g in range(2):
                    qa = qaug[pa][g]
                    qa_r = ["a_qaug%d_%d_q" % (pa, g), "a_qaug%d_%d_m" % (pa, g)]
                    pos_ = pos_b[:, 0:260].rearrange("p (h d) -> p h d", h=4)
                    pow_ = pow_b[:, 0:260].rearrange("p (h d) -> p h d", h=4)
                    state = {"fs": True, "fw": True}
                    for c in range(qb + 1):
                        def mk(c=c, g=g, qa=qa, qa_r=qa_r, pos_=pos_, state=state):
                            (pst, pstn), (pt_, ptn_) = alloc_step()
                            diag = (c == qb)

                            def s_():
                                MM(pst, kselT[g][:, c * 128:(c + 1) * 128], qa[:, :], True, not diag,
                                   qa_r + ["n_kT0_%d_%d" % (g, c), "n_erows%d" % g], [pstn])
                                if diag:
                                    MM(pst, ident_bf[:], tri4[:].rearrange("p h q -> p (h q)"), False, True, ["ident_bf", "n_tri4"], [pstn])

                            def e_():
                                ACT(pt_[:], pst, AF.Exp, [pstn], [ptn_])

                            def p_():
                                for h in range(4):
                                    MM(pos_[:, h, :], pt_[:, h * 128:(h + 1) * 128], vsel[:, g, c, :], state["fs"], c == qb,
                                       [ptn_, "n_vsel_%d" % c, "n_vsel1"], [pos_n], sgc=True)
                                    state["fs"] = False
                            return {"s": s_, "e": e_, "p": p_}
                        steps.append(mk)
                    c0 = max(0, qb - 4)
                    for c in range(c0, qb + 1):
                        def mk(c=c, g=g, qa=qa, qa_r=qa_r, pow_=pow_, pos_=pos_, state=state, last=(c == qb)):
                            (pst, pstn), (pt_, ptn_) = alloc_step()
                            special = (c == qb) or (c == qb - 4)

                            def s_():
                                MM(pst, kwinT[g][0:64, c * 128:(c + 1) * 128], qa[0:64, :], True, not special,
                                   [qa_r[0], "n_kT1_%d_%d" % (g, c)], [pstn])
                                if special:
                                    mk_, mkn = (tri4, "n_tri4") if c == qb else (anti4, "n_anti4")
                                    MM(pst, ident_bf[:], mk_[:].rearrange("p h q -> p (h q)"), False, True, ["ident_bf", mkn], [pstn])

                            def e_():
                                ACT(pt_[:], pst, AF.Exp, [pstn], [ptn_])

                            def p_():
                                for h in range(4):
                                    MM(pow_[:, h, :], pt_[:, h * 128:(h + 1) * 128], vwin[:, g, c, :], state["fw"], c == qb,
                                       [ptn_, "n_vwin_%d" % c, "n_vwin1"], [pow_n], sgc=True)
                                    state["fw"] = False

                            def post():
                                cf = coef[:, g]
                                RECIP(cf[:, 1, :], pos_[:, :, 64], [pos_n], ["a_cf1_%d" % g])
                                RECIP(cf[:, 2, :], pow_[:, :, 64], [pow_n], ["a_cf2_%d" % g])
                                yield
                                VTT("dve", cf[:, 1, :], cf[:, 1, :], g4[:, g, :, 1], ALU.mult, ["a_cf1_%d" % g, gsn], ["a_cf1_%d" % g])
                                VTT("dve", cf[:, 2, :], cf[:, 2, :], g4[:, g, :, 2], ALU.mult, ["a_cf2_%d" % g, gsn], ["a_cf2_%d" % g])
                                yield
                                VTT("dve", tA[g][:], pos_[:, :, 0:64], cf[:, 1, :].unsqueeze(2).to_broadcast([128, 4, 64]), ALU.mult,
                                    [pos_n, "a_cf1_%d" % g], ["a_tA%d" % g])
                                VTT("dve", tB[g][:], pow_[:, :, 0:64], cf[:, 2, :].unsqueeze(2).to_broadcast([128, 4, 64]), ALU.mult,
                                    [pow_n, "a_cf2_%d" % g], ["a_tB%d" % g])
                                yield
                                VTT("pool", tC[g][:], tA[g][:], ocs[p2][:, 4 * g:4 * g + 4, :], ALU.add, ["a_tA%d" % g, "a_ocs%d" % p2], ["a_tC%d" % g])
                                yield
                                VTT("pool", otm[p2][:, 4 * g:4 * g + 4, :], tC[g][:], tB[g][:], ALU.add, ["a_tC%d" % g, "a_tB%d" % g],
                                    ["a_otm%d_%d" % (p2, g)])
                                yield
                            d = {"s": s_, "e": e_, "p": p_}
                            if last:
                                d["post"] = post
                            return d
                        steps.append(mk)
                return steps

            def run_steps(mks):
                if not mks:
                    return
                cur_ = mks[0]()
                cur_["s"]()
                for i in range(len(mks)):
                    nxt = None
                    if i + 1 < len(mks):
                        nxt = mks[i + 1]()
                        nxt["s"]()
                    cur_["e"]()
                    cur_["p"]()
                    if "post" in cur_:
                        for _ in cur_["post"]():
                            yield
                    yield
                    cur_ = nxt

            def stageB(qb, nxt_qb):
                pa = qb % 2
                poc = poc_b[:, 0:512].rearrange("p (h d) -> p h d", h=8)
                VTT("dve", ocs[pa][:], poc, cfcs[pa][:].unsqueeze(2).to_broadcast([128, 8, 64]), ALU.mult, [poc_n, "a_cfc%d" % pa], ["a_ocs%d" % pa])
                yield
                mks = B_steps(qb)
                if nxt_qb is not None:
                    mks = mks + pathII_steps(nxt_qb)
                for _ in run_steps(mks):
                    yield
                for ct in range(4):
                    TR(pbankB[:, ct * 128:(ct + 1) * 128], otm[pa][:, 2 * ct:2 * ct + 2, :].rearrange("p h d -> p (h d)"),
                       ["a_otm%d_0" % pa, "a_otm%d_1" % pa], ["pbankB"])
                CP("act", ocsb[:].rearrange("p c q -> p (c q)"), pbankB[:, 0:512], ["pbankB"], ["a_ocsb"])
                DMA("sp", ocT_d.rearrange("(ct p) s -> p ct s", p=128)[:, :, qb * 128:(qb + 1) * 128], ocsb[:], ["a_ocsb"],
                    [("dram", "ocT", qb)])
                yield

            nqb = min(NB, DBG["nqb"])
            if nqb > 0:
                interleave([stageA_prep(0)])
                interleave([stageA_rest(0), stageA_prep(1) if nqb > 1 else None])
                interleave([run_steps(pathII_steps(0))])
            for qb in range(nqb):
                nx = qb + 1 if qb + 1 < nqb else None
                interleave([stageB(qb, nx), stageA_rest(qb + 1) if nx is not None else None,
                            stageA_prep(qb + 2) if qb + 2 < nqb else None])

        def mixer_out(l, xr, xname):
            P.barrier(); FA.reset(); BA.reset()
            gain = gains["norm_mixT"]
            xt = [FA.take([128, NFT, TT]) for _ in range(2)]
            rstd = FA.take([128, TT])
            sga = FA.take([128, TT]); sgb = FA.take([128, TT]); sgg = FA.take([128, TT]); ya = FA.take([128, TT]); yb = FA.take([128, TT])
            sq = BA.take([128, NFT, TT]); hT = BA.take([128, NFT, TT])
            wg = BA.take([128, NFT, 2048]); wout = BA.take([128, NFT, D_MODEL])
            wglu = [BA.take([128, 4, 256]) for _ in range(2)]; wup = [BA.take([128, 4, 128]) for _ in range(2)]
            yt_ = BA.take([128, 4, TT]); oct_ = BA.take([128, 4, TT]); mg = BA.take([128, NFT, TT])
            wv = D["w_in"][l].rearrange("(kt p) c -> p kt c", p=128)
            wov = D["w_out"][l].rearrange("(kt p) c -> p kt c", p=128)
            wgv = D["ssm_w_glu"][l].rearrange("(kt p) c -> p kt c", p=128)
            wuv = D["nsa_w_up"][l].rearrange("(kt p) c -> p kt c", p=128)
            for kt in range(NFT):
                DMA("pool", wg[:, kt, :], wv[:, kt, GA0:NCOL], (), ["o_wg%d" % kt])
                DMA("pool", wout[:, kt, :], wov[:, kt, :], (), ["o_wout%d" % kt])
            wgr = ["o_wg%d" % kt for kt in range(NFT)]
            wor = ["o_wout%d" % kt for kt in range(NFT)]
            for tt in range(SEQ // TT):
                tok0 = tt * TT
                xb, xn = xt[tt % 2], "o_xt%d" % (tt % 2)
                DMA("sp", xb[:], xview(xr)[:, :, tok0:tok0 + TT], [dtile(xname, o, tok0) for o in range(NFT)], [xn])
                norm_tile("o_", xb, xn, gain[:, l, :], "g_norm_mixT", hT, "o_hT", sq, rstd, banks[0][:], bn[0])
                hr = ["o_hT_%d" % ft for ft in range(NFT)]
                DMA("sp", yt_[:], yT_d.rearrange("(kt p) s -> p kt s", p=128)[:, :, tok0:tok0 + TT],
                    [("dram", "yT", bt) for bt in range(4)], ["o_yt"])
                DMA("sp", oct_[:], ocT_d.rearrange("(kt p) s -> p kt s", p=128)[:, :, tok0:tok0 + TT],
                    [("dram", "ocT", qb) for qb in range(tok0 // 128, tok0 // 128 + 4)], ["o_oct"])
                for c in range(NFT):
                    wl, wln = wglu[c % 2], "o_wglu%d" % (c % 2)
                    wu_, wun = wup[c % 2], "o_wup%d" % (c % 2)
                    DMA("pool", wl[:, :, 0:128], wgv[:, :, c * 128:(c + 1) * 128], (), [wln + "v"])
                    DMA("pool", wl[:, :, 128:256], wgv[:, :, 1024 + c * 128:1024 + (c + 1) * 128], (), [wln + "g"])
                    DMA("pool", wu_[:], wuv[:, :, c * 128:(c + 1) * 128], (), [wun])
                    pga, pgb, pv_, pgt, pyb = banks[1][:], banks[2][:], banks[3][:], banks[4][:], banks[5][:]
                    for kt in range(NFT):
                        MM(pga, wg[:, kt, c * 128:(c + 1) * 128], hT[:, kt, :], kt == 0, kt == NFT - 1, hr + wgr, [bn[1]])
                    for kt in range(NFT):
                        MM(pgb, wg[:, kt, 1024 + c * 128:1024 + (c + 1) * 128], hT[:, kt, :], kt == 0, kt == NFT - 1, hr + wgr, [bn[2]])
                    for kt in range(4):
                        MM(pv_, wl[:, kt, 0:128], yt_[:, kt, :], kt == 0, kt == 3, [wln + "v", "o_yt"], [bn[3]])
                    for kt in range(4):
                        MM(pgt, wl[:, kt, 128:256], yt_[:, kt, :], kt == 0, kt == 3, [wln + "g", "o_yt"], [bn[4]])
                    for kt in range(4):
                        MM(pyb, wu_[:, kt, :], oct_[:, kt, :], kt == 0, kt == 3, [wun, "o_oct"], [bn[5]])
                    ACT(sga[:], pga, AF.Sigmoid, [bn[1]], ["o_sga"])
                    ACT(sgb[:], pgb, AF.Sigmoid, [bn[2]], ["o_sgb"])
                    ACT(sgg[:], pgt, AF.Sigmoid, [bn[4]], ["o_sgg"])
                    VTT("dve", ya[:], pv_, sgg[:], ALU.mult, [bn[3], "o_sgg"], ["o_ya"])
                    VTT("pool", ya[:], ya[:], sga[:], ALU.mult, ["o_ya", "o_sga"], ["o_ya"])
                    VTT("dve", yb[:], pyb, sgb[:], ALU.mult, [bn[5], "o_sgb"], ["o_yb"])
                    VTT("pool", mg[:, c, :], ya[:], yb[:], ALU.add, ["o_ya", "o_yb"], ["o_mg%d" % c])
                mgr = ["o_mg%d" % c for c in range(NFT)]
                for ot in range(NFT):
                    po, pon = banks[0][:], bn[0]
                    for c in range(NFT):
                        MM(po, wout[:, c, ot * 128:(ot + 1) * 128], mg[:, c, :], c == 0, c == NFT - 1, mgr + wor, [pon])
                    VTT("dve", xb[:, ot, :], po, xb[:, ot, :], ALU.add, [pon, xn] + hr, [xn])
                DMA("sp", xview(xr)[:, :, tok0:tok0 + TT], xb[:], [xn], [dtile(xname, o, tok0) for o in range(NFT)])

        cur, cname = xT_in, "xin"
        for l in range(depth):
            last = (l == depth - 1)
            if "ffn1" in stages:
                ffn(l, cur, cname, xres, "xres", "norm_ffn1T", D["ffn1_wi"], D["ffn1_wo"])
                cur, cname = xres, "xres"
            if "mix" in stages:
                if "proj" in mix_parts:
                    mixer_proj(l, cur, cname)
                if "s5" in mix_parts:
                    s5(l)
                if "nsa" in mix_parts:
                    nsa(l)
                if "out" in mix_parts:
                    mixer_out(l, xres, "xres")
            if "ffn2" in stages:
                dst, dn = (outT, "out") if last else (xres, "xres")
                ffn(l, cur, cname, dst, dn, "norm_ffn2T", D["ffn2_wi"], D["ffn2_wo"])
                cur, cname = xres, "xres"
        final = [(P.dsem[i], 16 * P.dcnt[i]) for i in range(P.NDMA)]
        P.finish(final)
        print("instructions:", P.ninst, "sems:", P.nsem, flush=True)
    return nc


_CACHE = {}


def kernel(**inputs):
    x = np.asarray(inputs["x"], dtype=np.float32)
    if "nc" not in _CACHE:
        _CACHE["nc"] = build_program()
    nc = _CACHE["nc"]
    shared = layout_params(inputs)
    shared.update(host_constants())
    in_maps = []
    ncores = DBG["ncores"]
    for c in range(ncores):
        m = dict(shared)
        m["xT"] = np.ascontiguousarray(x[c // 2].T)
        in_maps.append(m)
    res = run_bass_kernel_spmd(nc, in_maps, core_ids=list(range(ncores)))
    _CACHE["last"] = res
    out = np.stack([np.ascontiguousarray(res.results[(2 * b) % ncores]["outT"].T) for b in range(BATCH)], axis=0)
    return out.astype(np.float32)
```

```python
import numpy as np
from contextlib import ExitStack
import concourse.bass as bass
import concourse.mybir as mybir
from concourse.bass_utils import run_bass_kernel_spmd

F32 = mybir.dt.float32
BF16 = mybir.dt.bfloat16
AF = mybir.ActivationFunctionType
ALU = mybir.AluOpType
AX = mybir.AxisListType

D_MODEL = 1024
SEQ = 4096
BATCH = 4
DEPTH = 4
D_FF = 2816
NFT = D_MODEL // 128
NFF = D_FF // 128
TT = 512
ST = 1024
EPS = 1e-6

ENGS = ("pe", "act", "dve", "pool", "sp")
NO_SAME_ENGINE_WAIT = False


class Prog:
    EPOCH = 20000
    NDMA = 24

    def __init__(self, nc, es):
        self.nc = nc
        self.es = es
        self.streams = {e: [] for e in ENGS}
        self.cnt = {e: 0 for e in ENGS}
        self.cur = {}
        self.nsem = 0
        self.pe_sems = []
        self.own_sems = {e: [] for e in ENGS}
        for e in ENGS:
            self.cur[e] = self._newsem(e)
            self.own_sems[e].append(self.cur[e])
        self.pe_sems.append(self.cur["pe"])
        self.waited = {e: {} for e in ENGS}
        self.buf = {}
        self.xacc = {}
        self.dsem = [self._newsem("dma%d" % i) for i in range(self.NDMA)]
        self.dcnt = [0] * self.NDMA
        self.dnext = 0
        self.ninst = 0

    def _newsem(self, tag):
        self.nsem += 1
        return self.es.enter_context(self.nc.semaphore("s_%s_%d" % (tag, self.nsem)))

    def _wait(self, eng, ev):
        if ev is None:
            return
        sem, val = ev
        if eng == "pe" and any(sem is s for s in self.pe_sems):
            return
        if NO_SAME_ENGINE_WAIT and any(sem is s for s in self.own_sems[eng]):
            return
        w = self.waited[eng]
        if w.get(id(sem), 0) >= val:
            return
        w[id(sem)] = val
        self.streams[eng].append(("wait", sem, val))

    def _deps(self, eng, reads, writes):
        for b in reads:
            st = self.buf.get(b)
            if st is not None:
                self._wait(eng, st["w"])
        for b in writes:
            st = self.buf.get(b)
            if st is not None:
                self._wait(eng, st["w"])
                for ev in st["r"].values():
                    self._wait(eng, ev)

    def _mark(self, key, ev, reads, writes):
        for b in reads:
            st = self.buf.setdefault(b, {"w": None, "r": {}})
            st["r"][key] = ev
        for b in writes:
            self.buf[b] = {"w": ev, "r": {}}

    def _excl(self, eng, names):
        out = []
        for b in names:
            if isinstance(b, str) and (b.startswith("bank") or b.startswith("pbank")):
                st = self.xacc.setdefault(b, {})
                for e2, ev in st.items():
                    if e2 != eng:
                        self._wait(eng, ev)
                out.append(st)
        return out

    def op(self, eng, fn, reads=(), writes=()):
        self._deps(eng, reads, writes)
        xs = self._excl(eng, list(reads) + list(writes))
        if self.cnt[eng] >= self.EPOCH:
            self.cur[eng] = self._newsem(eng)
            self.own_sems[eng].append(self.cur[eng])
            self.cnt[eng] = 0
            if eng == "pe":
                self.pe_sems.append(self.cur[eng])
        self.cnt[eng] += 1
        sem = self.cur[eng]
        ev = (sem, self.cnt[eng])
        self.streams[eng].append(("op", fn, sem))
        self._mark(eng, ev, reads, writes)
        for st in xs:
            st[eng] = ev
        self.ninst += 1
        return ev

    def dma(self, q, out, in_, reads=(), writes=()):
        self._deps(q, reads, writes)
        i = self.dnext
        self.dnext = (self.dnext + 1) % self.NDMA
        sem = self.dsem[i]
        self._wait(q, (sem, 16 * self.dcnt[i]))
        self.dcnt[i] += 1
        ev = (sem, 16 * self.dcnt[i])
        self.streams[q].append(("dma", out, in_, sem))
        self._mark(("dma", i, self.dcnt[i]), ev, reads, writes)
        self.ninst += 1
        return ev

    def barrier(self):
        evs = [(self.cur[e], self.cnt[e]) for e in ENGS if self.cnt[e] > 0]
        evs += [(self.dsem[i], 16 * self.dcnt[i]) for i in range(self.NDMA) if self.dcnt[i] > 0]
        for e in ENGS:
            for ev in evs:
                self._wait(e, ev)
        self.buf = {}
        self.xacc = {}

    def finish(self, final_events):
        for ev in final_events:
            self._wait("sp", ev)
        nc = self.nc
        block = self.es.enter_context(nc.Block())

        def replay(engobj, stream):
            for it in stream:
                if it[0] == "wait":
                    engobj.wait_ge(it[1], it[2])
                elif it[0] == "op":
                    it[1](engobj).then_inc(it[2], 1)
                else:
                    engobj.dma_start(out=it[1], in_=it[2]).then_inc(it[3], 16)

        @block.tensor
        def _(e):
            replay(e, self.streams["pe"])

        @block.scalar
        def _(e):
            replay(e, self.streams["act"])

        @block.vector
        def _(e):
            replay(e, self.streams["dve"])

        @block.gpsimd
        def _(e):
            replay(e, self.streams["pool"])

        @block.sync
        def _(e):
            replay(e, self.streams["sp"])


NCOL = 3864
DBG = {"ncores": 8, "nqb": 32, "s5_nb": 4, "s5_ng": 4, "s5_inv": 1, "ffn_smalldma": 0}
QKV0, QKVW = 512, 1304
GA0, GB0 = 1816, 2840
NEG = -30000.0
OFFC = 248
OFFV = 62
F32N = 15360
BFN = 53248
TWO_PI = float(2 * np.pi)
EVEC = [float(7 - k) for k in range(15)] + [float(k) for k in range(9)] + [float(8 * 2 ** s) for s in range(9)]
NE = len(EVEC)


def host_constants():
    c = {}
    pos = np.arange(SEQ, dtype=np.float32)
    inv_freq = (1.0 / (10000.0 ** (np.arange(32, dtype=np.float32) / 32))).astype(np.float32)
    ang = pos[:, None] * inv_freq[None, :]
    c["cs_tok"] = np.concatenate([np.cos(ang), np.sin(ang)], axis=1).astype(np.float32)
    cend = (np.arange(256) * 16 + 31).astype(np.float32)
    angc = cend[:, None] * inv_freq[None, :]
    c["cs_cmp"] = np.concatenate([np.cos(angc), np.sin(angc)], axis=1).astype(np.float32)
    sel = np.zeros((128, 8, 8, 128), np.float32)
    for g8 in range(8):
        for i in range(8):
            for cin in range(16):
                sel[g8 * 16 + cin, g8, i, i * 16 + cin] = 1.0
    c["sel"] = sel.reshape(128, 64, 128)
    c["selT"] = np.ascontiguousarray(sel.transpose(3, 1, 2, 0)).reshape(128, 64, 128)
    ii = np.arange(128) // 16
    c["tmask"] = (ii[None, :] >= ii[:, None]).astype(np.float32)
    c["identf"] = np.eye(128, dtype=np.float32)
    J = np.zeros((128, 128), np.float32)
    for k in range(128):
        J[k, (k + 64) % 128] = 1.0
    c["jmat"] = J
    sg = np.ones((128, 1), np.float32); sg[64:] = -1.0
    c["sgn"] = sg
    c["evec"] = np.broadcast_to(np.asarray(EVEC, np.float32)[None, :], (128, NE)).copy()
    tq = np.arange(128)
    m = np.arange(512)
    c["maskc"] = np.where(16 * (m[None, :] - OFFC) + 31 <= tq[:, None], 0.0, NEG).astype(np.float32)
    curp = (tq >= 64).astype(np.int64)
    mm = np.arange(128)
    jp = np.broadcast_to(mm[None, :] - OFFV, (128, 128))
    A = (jp < (curp[:, None] - 1)).astype(np.float32)
    Bt = np.zeros((128, 128), np.float32)
    forced = (jp == curp[:, None]) | (jp == curp[:, None] - 1)
    invalid = jp > curp[:, None]
    Bt[forced] = (1e4 + (jp + 70))[forced]
    Bt[invalid] = (-1.0 - (jp + 70))[invalid]
    c["atab"] = A
    c["btab"] = Bt
    kk = np.arange(128)
    c["tri"] = np.where(kk[:, None] <= tq[None, :], 0.0, NEG).astype(np.float32)
    c["anti"] = np.where(kk[:, None] > tq[None, :], 0.0, NEG).astype(np.float32)
    key = np.arange(SEQ)
    c["erows"] = (key[None, :] // 64 == np.arange(64)[:, None]).astype(np.float32)
    return c


CONST_SHAPES = {"cs_tok": [SEQ, 64], "cs_cmp": [256, 64], "sel": [128, 64, 128], "selT": [128, 64, 128],
                "tmask": [128, 128], "identf": [128, 128], "jmat": [128, 128], "sgn": [128, 1],
                "evec": [128, NE], "maskc": [128, 512], "atab": [128, 128], "btab": [128, 128],
                "tri": [128, 128], "anti": [128, 128], "erows": [64, SEQ]}


def layout_params(inp):
    L = DEPTH
    f = lambda a: np.ascontiguousarray(np.asarray(a, np.float32))
    o = {}
    trn = lambda a: f(np.asarray(a, np.float32).reshape(L, NFT, 128).transpose(2, 0, 1))
    o["norm_ffn1T"] = trn(inp["norm_ffn1"]); o["norm_ffn2T"] = trn(inp["norm_ffn2"]); o["norm_mixT"] = trn(inp["norm_mix"])
    for k in ("ffn1_wi", "ffn1_wo", "ffn2_wi", "ffn2_wo", "w_in", "ssm_w_glu", "nsa_w_up", "w_out", "cmp_w1", "cmp_w2"):
        o[k] = f(inp[k])
    dup = lambda a: np.concatenate([a, a], axis=0)
    o["lamre"] = f(dup(np.asarray(inp["ssm_lambda_re"]).transpose(2, 0, 1)))
    o["lamim"] = f(dup(np.asarray(inp["ssm_lambda_im"]).transpose(2, 0, 1)))
    o["logdt"] = f(np.broadcast_to(np.asarray(inp["ssm_log_dt"])[None], (128, L, 32)))
    o["bre"] = f(dup(np.asarray(inp["ssm_b_re"]).transpose(2, 0, 1, 3)))
    o["bim"] = f(dup(np.asarray(inp["ssm_b_im"]).transpose(2, 0, 1, 3)))
    o["cre"] = f(dup(np.asarray(inp["ssm_c_re"]).transpose(3, 0, 1, 2)))
    o["cim"] = f(dup(np.asarray(inp["ssm_c_im"]).transpose(3, 0, 1, 2)))
    dsk = np.asarray(inp["ssm_d"])
    o["dvec"] = f(np.tile(dsk.transpose(2, 0, 1), (8, 1, 1)))
    o["qgain"] = f(np.broadcast_to(np.asarray(inp["q_norm"])[None], (128, L, 64)))
    o["kgain"] = f(np.broadcast_to(np.asarray(inp["k_norm"])[None], (128, L, 3, 64)))
    o["peT"] = f(np.asarray(inp["cmp_pe"]).transpose(3, 0, 1, 2))
    return o


PARAM_SHAPES = {"norm_ffn1T": [128, DEPTH, NFT], "norm_ffn2T": [128, DEPTH, NFT], "norm_mixT": [128, DEPTH, NFT],
                "ffn1_wi": [DEPTH, D_MODEL, 2 * D_FF], "ffn1_wo": [DEPTH, D_FF, D_MODEL],
                "ffn2_wi": [DEPTH, D_MODEL, 2 * D_FF], "ffn2_wo": [DEPTH, D_FF, D_MODEL],
                "w_in": [DEPTH, D_MODEL, NCOL], "ssm_w_glu": [DEPTH, 512, 2048], "nsa_w_up": [DEPTH, 512, 1024],
                "w_out": [DEPTH, D_MODEL, D_MODEL], "cmp_w1": [DEPTH, 2, 2048, 128], "cmp_w2": [DEPTH, 2, 128, 64],
                "lamre": [128, DEPTH, 32], "lamim": [128, DEPTH, 32], "logdt": [128, DEPTH, 32],
                "bre": [128, DEPTH, 32, 16], "bim": [128, DEPTH, 32, 16], "cre": [128, DEPTH, 32, 16], "cim": [128, DEPTH, 32, 16],
                "dvec": [128, DEPTH, 32], "qgain": [128, DEPTH, 64], "kgain": [128, DEPTH, 3, 64], "peT": [64, DEPTH, 2, 32]}


def interleave(gens):
    gens = [g for g in gens if g is not None]
    while gens:
        for g in list(gens):
            try:
                next(g)
            except StopIteration:
                gens.remove(g)


class Arena:
    def __init__(self, ap, n, tag):
        self.ap, self.n, self.tag, self.off = ap, n, tag, 0

    def reset(self):
        self.off = 0

    def take(self, shape):
        n = 1
        for v in shape[1:]:
            n *= v
        assert self.off + n <= self.n, (self.tag, self.off, n, self.n)
        v = self.ap[:, self.off:self.off + n]
        self.off += n
        if len(shape) == 3:
            v = v.rearrange("p (a b) -> p a b", a=shape[1])
        elif len(shape) == 4:
            v = v.rearrange("p (a b c) -> p a b c", a=shape[1], b=shape[2])
        return v


def build_program(depth=DEPTH, stages=("ffn1", "mix", "ffn2"), debug=False, mix_parts=("proj", "s5", "nsa", "out")):
    nc = bass.Bass("TRN2", target_bir_lowering=False)
    es = ExitStack()
    D = {}
    for k, shp in list(PARAM_SHAPES.items()) + list(CONST_SHAPES.items()):
        D[k] = nc.dram_tensor(k, list(shp), F32, kind="ExternalInput").ap()
    xT_in = nc.dram_tensor("xT", [D_MODEL, SEQ], F32, kind="ExternalInput").ap()
    outT = nc.dram_tensor("outT", [D_MODEL, SEQ], F32, kind="ExternalOutput").ap()
    skind = "ExternalOutput" if debug else "Internal"
    xres = nc.dram_tensor("xres", [D_MODEL, SEQ], F32, kind=skind).ap()
    uT_d = nc.dram_tensor("uT_d", [512, SEQ], BF16, kind=skind).ap()
    yT_d = nc.dram_tensor("yT_d", [512, SEQ], BF16, kind=skind).ap()
    ocT_d = nc.dram_tensor("ocT_d", [512, SEQ], BF16, kind=skind).ap()
    qkv_d = nc.dram_tensor("qkv_d", [SEQ, QKVW], F32, kind=skind).ap()

    with es:
        P = Prog(nc, es)
        sbt = lambda name, shape, dt: es.enter_context(nc.sbuf_tensor(name, list(shape), dt))
        FA = Arena(sbt("f32arena", [128, F32N], F32), F32N, "f32")
        BA = Arena(sbt("bf16arena", [128, BFN], BF16), BFN, "bf16")
        itmp = sbt("itmp", [128, 8 * NE], mybir.dt.int32)
        gains = {k: sbt("g_" + k, [128, DEPTH, NFT], F32) for k in ("norm_ffn1T", "norm_ffn2T", "norm_mixT")}
        ones_bf = sbt("ones_bf", [128, 128], BF16)
        ident_bf = sbt("ident_bf", [128, 128], BF16)
        identf = sbt("identf_sb", [128, 128], F32)
        banks = [es.enter_context(nc.psum_tensor("bank%d" % i, [128, 512], F32)) for i in range(6)]
        pbank = es.enter_context(nc.psum_tensor("pbank", [128, 1024], BF16))
        pbankB = es.enter_context(nc.psum_tensor("pbankB", [128, 1024], BF16))
        bn = ["bank%d" % i for i in range(6)]

        def MM(out, lhsT, rhs, st, sp, r, w, sgc=False):
            if sgc:
                P.op("pe", lambda e: e.matmul(out, lhsT, rhs, start=st, stop=sp, skip_group_check=True), r, w)
            else:
                P.op("pe", lambda e: e.matmul(out, lhsT, rhs, start=st, stop=sp), r, w)

        def TR(out, in_, r, w):
            P.op("pe", lambda e: e.transpose(out, in_, ident_bf[:]), list(r) + ["ident_bf"], w)

        def ACT(out, in_, func, r, w, **kw):
            P.op("act", lambda e: e.activation(out=out, in_=in_, func=func, **kw), r, w)

        def VTT(eng, out, a, b, op, r, w):
            P.op(eng, lambda e: e.tensor_tensor(out, a, b, op), r, w)

        def TS(eng, out, a, s1, s2, op0, op1, r, w):
            if op1 is None:
                P.op(eng, lambda e: e.tensor_scalar(out, a, s1, s2, op0=op0), r, w)
            else:
                P.op(eng, lambda e: e.tensor_scalar(out, a, s1, s2, op0=op0, op1=op1), r, w)

        def STT(eng, out, in0, scalar, in1, op0, op1, r, w):
            eng = "dve"
            P.op(eng, lambda e: e.scalar_tensor_tensor(out=out, in0=in0, scalar=scalar, in1=in1, op0=op0, op1=op1), r, w)

        def CP(eng, out, in_, r, w):
            if eng == "act":
                ACT(out, in_, AF.Copy, r, w)
            else:
                P.op(eng, lambda e: e.tensor_copy(out, in_), r, w)

        def MEMSET(eng, out, val, w):
            P.op(eng, lambda e: e.memset(out, val), (), w)

        def RECIP(out, in_, r, w):
            P.op("dve", lambda e: e.reciprocal(out, in_), r, w)

        def DMA(q, out, in_, r, w):
            return P.dma(q, out, in_, r, w)

        xview = lambda t: t.rearrange("(ft p) s -> p ft s", p=128)
        dtile = lambda name, ft, tok0: ("dram", name, ft, tok0)

        for k in gains:
            DMA("sp", gains[k][:], D[k][:, :, :], (), ["g_" + k])
        MEMSET("dve", ones_bf[:], 1.0, ["ones"])
        DMA("sp", identf[:], D["identf"][:, :], (), ["identf"])
        DMA("pool", ident_bf[:], D["identf"][:, :], (), ["ident_bf"])

        def norm_tile(pfx, xb, xn, gain_ap, gname, hdst, hname, sq, rstd, ps_n, psn_name):
            ACT(sq[:], xb[:], AF.Square, [xn], [pfx + "sq"])
            for ft in range(NFT):
                MM(ps_n, ones_bf[:], sq[:, ft, :], ft == 0, ft == NFT - 1, [pfx + "sq", "ones"], [psn_name])
            TS("dve", rstd[:], ps_n, 1.0 / D_MODEL, EPS, ALU.mult, ALU.add, [psn_name], [pfx + "rstd"])
            ACT(rstd[:], rstd[:], AF.Sqrt, [pfx + "rstd"], [pfx + "rstd"])
            RECIP(rstd[:], rstd[:], [pfx + "rstd"], [pfx + "rstd"])
            for ft in range(NFT):
                STT("dve" if ft % 2 == 0 else "pool", hdst[:, ft, :], xb[:, ft, :], gain_ap[:, ft:ft + 1], rstd[:],
                    ALU.mult, ALU.mult, [xn, pfx + "rstd", gname], [hname + "_%d" % ft])

        def ffn(l, src, sname, dst, dname, gkey, wi, wo):
            P.barrier(); FA.reset(); BA.reset()
            gain = gains[gkey]
            xt = [FA.take([128, NFT, TT]) for _ in range(2)]
            rstd = FA.take([128, TT])
            sg = [FA.take([128, TT]) for _ in range(2)]
            sq = BA.take([128, NFT, TT])
            hT = BA.take([128, NFT, ST])
            actT = BA.take([128, NFF, ST])
            wi_sb = [BA.take([128, NFT, 256]) for _ in range(3)]
            wo_sb = [BA.take([128, NFF, 128]) for _ in range(2)]
            ps_n, ps_g, ps_u, ps_o = banks[0][:], [banks[1][:], banks[2][:]], [banks[3][:], banks[4][:]], [banks[5][:], banks[0][:]]
            pon_ = [bn[5], bn[0]]
            nst, ntt = SEQ // ST, ST // TT
            wiv = wi[l].rearrange("(kt p) c -> p kt c", p=128)
            wov = wo[l].rearrange("(ft p) c -> p ft c", p=128)
            def load_norm_st(s):
                for t in range(ntt):
                    tok0 = s * ST + t * TT
                    xb, xn = xt[t % 2], "f_xt%d" % (t % 2)
                    DMA("sp", xb[:], xview(src)[:, :, tok0:tok0 + TT], [dtile(sname, o, tok0) for o in range(NFT)], [xn])
                    norm_tile("f_", xb, xn, gain[:, l, :], "g_" + gkey, hT[:, :, t * TT:(t + 1) * TT], "f_hT%d" % t,
                              sq, rstd, ps_n, bn[0])

            load_norm_st(0)
            for s in range(nst):
                for f in range(NFF):
                    wb, wn = wi_sb[f % 3], "f_wi%d" % (f % 3)
                    if DBG["ffn_smalldma"]:
                        DMA("pool", wb[:, 0:1, 0:128], wiv[:, 0:1, f * 128:(f + 1) * 128], (), [wn + "g"])
                        DMA("pool", wb[:, 0:1, 128:256], wiv[:, 0:1, D_FF + f * 128:D_FF + (f + 1) * 128], (), [wn + "u"])
                    else:
                        DMA("pool", wb[:, :, 0:128], wiv[:, :, f * 128:(f + 1) * 128], (), [wn + "g"])
                        DMA("pool", wb[:, :, 128:256], wiv[:, :, D_FF + f * 128:D_FF + (f + 1) * 128], (), [wn + "u"])
                    for t in range(ntt):
                        pg, pgn, pu, pun = ps_g[t % 2], bn[1 + t % 2], ps_u[t % 2], bn[3 + t % 2]
                        hr = ["f_hT%d_%d" % (t, ft) for ft in range(NFT)]
                        for kt in range(NFT):
                            MM(pg, wb[:, kt, 0:128], hT[:, kt, t * TT:(t + 1) * TT], kt == 0, kt == NFT - 1, hr + [wn + "g"], [pgn])
                        for kt in range(NFT):
                            MM(pu, wb[:, kt, 128:256], hT[:, kt, t * TT:(t + 1) * TT], kt == 0, kt == NFT - 1, hr + [wn + "u"], [pun])
                        sgb, sgn = sg[t % 2], "f_sg%d" % (t % 2)
                        ACT(sgb[:], pg, AF.Silu, [pgn], [sgn])
                        TT_ = "dve"
                        VTT(TT_, actT[:, f, t * TT:(t + 1) * TT], sgb[:], pu, ALU.mult, [sgn, pun], ["f_act%d_%d" % (f, t)])
                if s + 1 < nst:
                    load_norm_st(s + 1)
                for ot in range(NFT):
                    wb, wn = wo_sb[ot % 2], "f_wo%d" % (ot % 2)
                    if DBG["ffn_smalldma"]:
                        DMA("pool", wb[:, 0:1, :], wov[:, 0:1, ot * 128:(ot + 1) * 128], (), [wn])
                    else:
                        DMA("pool", wb[:], wov[:, :, ot * 128:(ot + 1) * 128], (), [wn])
                    for t in range(ntt):
                        tok0 = s * ST + t * TT
                        po, pon = ps_o[t % 2], pon_[t % 2]
                        for f in range(NFF):
                            MM(po, wb[:, f, :], actT[:, f, t * TT:(t + 1) * TT], f == 0, f == NFF - 1,
                               ["f_act%d_%d" % (f, t), wn], [pon])
                        rb, rn = sg[t % 2], "f_sg%d" % (t % 2)
                        DMA("sp", rb[:], src[ot * 128:(ot + 1) * 128, tok0:tok0 + TT], [dtile(sname, ot, tok0)], [rn])
                        STT("dve", rb[:], po, 0.5, rb[:], ALU.mult, ALU.add, [pon, rn], [rn])
                        DMA("sp", dst[ot * 128:(ot + 1) * 128, tok0:tok0 + TT], rb[:], [rn], [dtile(dname, ot, tok0)])

        def mixer_proj(l, src, sname):
            P.barrier(); FA.reset(); BA.reset()
            gain = gains["norm_mixT"]
            xt = [FA.take([128, NFT, TT]) for _ in range(2)]
            rstd = FA.take([128, TT])
            qrow = [FA.take([128, QKVW]) for _ in range(2)]
            sq = BA.take([128, NFT, TT])
            hT2 = [BA.take([128, NFT, TT]) for _ in range(2)]
            wq = BA.take([128, NFT, GA0])
            ub = [BA.take([128, TT]) for _ in range(2)]
            wv = D["w_in"][l].rearrange("(kt p) c -> p kt c", p=128)
            ntile = SEQ // TT

            def load_norm(tt):
                tok0 = tt * TT
                xb, xn = xt[tt % 2], "m_xt%d" % (tt % 2)
                DMA("sp", xb[:], xview(src)[:, :, tok0:tok0 + TT], [dtile(sname, o, tok0) for o in range(NFT)], [xn])
                norm_tile("m_", xb, xn, gain[:, l, :], "g_norm_mixT", hT2[tt % 2], "m_hT%d" % (tt % 2), sq, rstd, banks[0][:], bn[0])

            load_norm(0)
            cblocks = [(0, 128), (128, 256), (256, 384), (384, 512), (512, 1024), (1024, 1536), (1536, GA0)]
            for bi_, (c0_, c1_) in enumerate(cblocks):
                DMA("pool", wq[:, :, c0_:c1_], wv[:, :, c0_:c1_], (), ["m_wq%d" % bi_])
            wqr = ["m_wq%d" % bi_ for bi_ in range(len(cblocks))]
            for tt in range(ntile):
                tok0 = tt * TT
                hT = hT2[tt % 2]
                hr = ["m_hT%d_%d" % (tt % 2, ft) for ft in range(NFT)]
                for c in range(4):
                    pb, pn = banks[1 + c % 2][:], bn[1 + c % 2]
                    for kt in range(NFT):
                        MM(pb, wq[:, kt, c * 128:(c + 1) * 128], hT[:, kt, :], kt == 0, kt == NFT - 1, hr + wqr, [pn])
                    u_, un = ub[c % 2], "m_ub%d" % (c % 2)
                    CP("act", u_[:], pb, [pn], [un])
                    DMA("sp", uT_d[c * 128:(c + 1) * 128, tok0:tok0 + TT], u_[:], [un], [("dram", "uT", c, tt)])
                if tt + 1 < ntile:
                    load_norm(tt + 1)
                for sub in range(4):
                    qr, qn = qrow[sub % 2], "m_qrow%d" % (sub % 2)
                    for bi, (c0, c1) in enumerate(((512, 1024), (1024, 1536), (1536, GA0))):
                        pb, pn = banks[3 + bi][:], bn[3 + bi]
                        for kt in range(NFT):
                            MM(pb[:, 0:c1 - c0], hT[:, kt, sub * 128:(sub + 1) * 128], wq[:, kt, c0:c1], kt == 0, kt == NFT - 1,
                               hr + wqr, [pn])
                        CP("dve" if bi != 1 else "act", qr[:, c0 - 512:c1 - 512], pb[:, 0:c1 - c0], [pn], [qn + "_%d" % bi])
                    r0 = tok0 + sub * 128
                    DMA("sp", qkv_d[r0:r0 + 128, :], qr[:], [qn + "_%d" % bi for bi in range(3)], [("dram", "qkv", r0 // 128)])

        def range_reduce(x, tmp, n, r):
            iv = itmp[:, 0:n]
            TS("dve", tmp, x, 1.0 / TWO_PI, 64.5, ALU.mult, ALU.add, [r], ["s_rtmp"])
            CP("dve", iv, tmp, ["s_rtmp"], ["s_itmp"])
            CP("dve", tmp, iv, ["s_itmp"], ["s_rtmp"])
            TS("dve", tmp, tmp, -64.0, -TWO_PI, ALU.add, ALU.mult, ["s_rtmp"], ["s_rtmp"])
            VTT("dve", x, x, tmp, ALU.add, [r, "s_rtmp"], [r])
            TS("dve", tmp, x, float(np.pi), -TWO_PI, ALU.is_gt, ALU.mult, [r], ["s_rtmp"])
            VTT("dve", x, x, tmp, ALU.add, [r, "s_rtmp"], [r])
            TS("dve", tmp, x, float(-np.pi), TWO_PI, ALU.is_lt, ALU.mult, [r], ["s_rtmp"])
            VTT("dve", x, x, tmp, ALU.add, [r, "s_rtmp"], [r])

        def s5(l):
            P.barrier(); FA.reset(); BA.reset()
            G8 = 8
            lr = FA.take([128, G8]); li = FA.take([128, G8]); ldt = FA.take([128, G8])
            dvc = FA.take([128, G8]); sgn = FA.take([128, 1]); evec = FA.take([128, NE])
            braw = [FA.take([128, G8, 16]) for _ in range(2)]
            craw = [FA.take([128, G8, 16]) for _ in range(2)]
            bbar = [FA.take([128, G8, 16]) for _ in range(2)]
            t8 = [FA.take([128, G8]) for _ in range(6)]
            marg = FA.take([128, G8, NE]); ang = FA.take([128, G8, NE]); ang2 = FA.take([128, G8, NE])
            rtmp = FA.take([128, G8, NE])
            are = FA.take([128, G8, NE]); aim = FA.take([128, G8, NE])
            S = FA.take([128, G8, 15, 16]); R = FA.take([128, G8, 9, 16])
            T1 = FA.take([128, G8, 15, 16]); T2 = FA.take([128, G8, 15, 16])
            xpad = [[FA.take([128, 1024]) for _ in range(2)] for _ in range(2)]
            tmaskf = FA.take([128, 128]); jmat = FA.take([128, 128])
            ttmp = [FA.take([128, 128]) for _ in range(2)]; mtmp = [FA.take([128, 128]) for _ in range(2)]
            mtmp2 = [FA.take([128, 128]) for _ in range(2)]
            ms = [[FA.take([128, 128]) for _ in range(2)] for _ in range(2)]
            sel = BA.take([128, 64, 128]); selT = BA.take([128, 64, 128])
            utile = BA.take([128, SEQ]); ytile = BA.take([128, SEQ])
            usb = [BA.take([128, 512]) for _ in range(2)]; xprev = [BA.take([128, 512]) for _ in range(2)]
            yact = [BA.take([128, 512]) for _ in range(8)]
            Tg = [BA.take([128, 128]) for _ in range(2)]; Gg = [BA.take([128, 128]) for _ in range(2)]; Hg = [BA.take([128, 128]) for _ in range(2)]
            DMA("pool", sel[:], D["sel"][:, :, :], (), ["s_sel"])
            DMA("pool", selT[:], D["selT"][:, :, :], (), ["s_selT"])
            DMA("sp", tmaskf[:], D["tmask"][:, :], (), ["s_tmask"])
            DMA("sp", jmat[:], D["jmat"][:, :], (), ["s_jmat"])
            DMA("sp", sgn[:], D["sgn"][:, :], (), ["s_sgn"])
            DMA("sp", evec[:], D["evec"][:, :], (), ["s_evec"])
            for pp in range(2):
                for kk in range(2):
                    MEMSET("pool", xpad[pp][kk][:, 0:512], 0.0, ["s_xpz"])
            for bt in range(DBG["s5_nb"]):
                g0 = bt * 8
                gs = slice(g0, g0 + 8)
                DMA("sp", lr[:], D["lamre"][:, l, gs], (), ["s_lr"])
                DMA("sp", li[:], D["lamim"][:, l, gs], (), ["s_li"])
                DMA("sp", ldt[:], D["logdt"][:, l, gs], (), ["s_ldt"])
                DMA("sp", dvc[:], D["dvec"][:, l, gs], (), ["s_dvc"])
                DMA("sp", braw[0][:], D["bre"][:, l, gs, :], (), ["s_bre"])
                DMA("sp", braw[1][:], D["bim"][:, l, gs, :], (), ["s_bim"])
                DMA("sp", craw[0][:], D["cre"][:, l, gs, :], (), ["s_cre"])
                DMA("sp", craw[1][:], D["cim"][:, l, gs, :], (), ["s_cim"])
                DMA("sp", utile[:], uT_d[bt * 128:(bt + 1) * 128, :], [("dram", "uT", bt, tt) for tt in range(8)], ["s_utile"])
                dt_, lrdt, lidt, inv, fre, fim = t8
                ACT(dt_[:], ldt[:], AF.Exp, ["s_ldt"], ["s_dt"])
                VTT("dve", lrdt[:], lr[:], dt_[:], ALU.mult, ["s_lr", "s_dt"], ["s_lrdt"])
                VTT("dve", lidt[:], li[:], dt_[:], ALU.mult, ["s_li", "s_dt"], ["s_lidt"])
                bc_e = lambda t: t[:].unsqueeze(2).to_broadcast([128, G8, NE])
                ev_b = evec[:].unsqueeze(1).to_broadcast([128, G8, NE])
                VTT("dve", marg[:], bc_e(lrdt), ev_b, ALU.mult, ["s_lrdt", "s_evec"], ["s_marg"])
                VTT("dve", ang[:], bc_e(lidt), ev_b, ALU.mult, ["s_lidt", "s_evec"], ["s_ang"])
                TS("dve", ang2[:], ang[:], float(np.pi / 2), None, ALU.add, None, ["s_ang"], ["s_ang2"])
                ACT(marg[:], marg[:], AF.Exp, ["s_marg"], ["s_marg"])
                fl = lambda t: t[:].rearrange("p g e -> p (g e)")
                range_reduce(fl(ang), fl(rtmp), G8 * NE, "s_ang")
                range_reduce(fl(ang2), fl(rtmp), G8 * NE, "s_ang2")
                ACT(ang[:], ang[:], AF.Sin, ["s_ang"], ["s_ang"])
                ACT(ang2[:], ang2[:], AF.Sin, ["s_ang2"], ["s_ang2"])
                VTT("dve", are[:], marg[:], ang2[:], ALU.mult, ["s_marg", "s_ang2"], ["s_are"])
                VTT("dve", aim[:], marg[:], ang[:], ALU.mult, ["s_marg", "s_ang"], ["s_aim"])
                a1r, a1i = are[:, :, 6], aim[:, :, 6]
                u0, u1 = T1[:, :, 0, 0], T1[:, :, 0, 1]
                VTT("dve", inv[:], lr[:], lr[:], ALU.mult, ["s_lr"], ["s_inv"])
                VTT("dve", u0, li[:], li[:], ALU.mult, ["s_li", "s_T1h", "s_T2h"], ["s_T1"])
                VTT("dve", inv[:], inv[:], u0, ALU.add, ["s_inv", "s_T1"], ["s_inv"])
                RECIP(inv[:], inv[:], ["s_inv"], ["s_inv"])
                TS("dve", u0, a1r, -1.0, None, ALU.add, None, ["s_are"], ["s_T1"])
                VTT("dve", fre[:], u0, lr[:], ALU.mult, ["s_T1", "s_lr"], ["s_fre"])
                VTT("dve", u1, a1i, li[:], ALU.mult, ["s_aim", "s_li"], ["s_T1b"])
                VTT("dve", fre[:], fre[:], u1, ALU.add, ["s_fre", "s_T1b"], ["s_fre"])
                VTT("dve", fre[:], fre[:], inv[:], ALU.mult, ["s_fre", "s_inv"], ["s_fre"])
                VTT("dve", fim[:], a1i, lr[:], ALU.mult, ["s_aim", "s_lr"], ["s_fim"])
                VTT("dve", u1, u0, li[:], ALU.mult, ["s_T1", "s_li"], ["s_T1b"])
                VTT("dve", fim[:], fim[:], u1, ALU.subtract, ["s_fim", "s_T1b"], ["s_fim"])
                VTT("dve", fim[:], fim[:], inv[:], ALU.mult, ["s_fim", "s_inv"], ["s_fim"])
                bc_c = lambda t: t[:].unsqueeze(2).to_broadcast([128, G8, 16])
                w0, w1_ = T2[:, :, 0, :], T2[:, :, 1, :]
                VTT("dve", w0, bc_c(fre), braw[0][:], ALU.mult, ["s_fre", "s_bre"], ["s_T2"])
                VTT("dve", w1_, bc_c(fim), braw[1][:], ALU.mult, ["s_fim", "s_bim"], ["s_T2b"])
                VTT("dve", bbar[0][:], w0, w1_, ALU.subtract, ["s_T2", "s_T2b"], ["s_bbr"])
                VTT("dve", w0, bc_c(fre), braw[1][:], ALU.mult, ["s_fre", "s_bim"], ["s_T2"])
                VTT("dve", w1_, bc_c(fim), braw[0][:], ALU.mult, ["s_fim", "s_bre"], ["s_T2b"])
                VTT("dve", bbar[1][:], w0, w1_, ALU.add, ["s_T2", "s_T2b"], ["s_bbi"])
                lo, hi = slice(0, 64), slice(64, 128)
                Ab = lambda t, h, k0, k1, n: t[h, :, k0:k1].unsqueeze(3).to_broadcast([64, G8, k1 - k0, 16])
                Zb = lambda t, h, n: t[h, :, :].unsqueeze(2).to_broadcast([64, G8, n, 16])
                XD = ["s_bbr", "s_bbi", "s_fre", "s_fim", "s_T1", "s_T1b", "s_T2", "s_T2b"]
                VTT("dve", T1[lo], Ab(are, lo, 0, 15, 15), Zb(bbar[0], lo, 15), ALU.mult, ["s_are"] + XD, ["s_T1"])
                VTT("pool", T2[lo], Ab(aim, lo, 0, 15, 15), Zb(bbar[1], lo, 15), ALU.mult, ["s_aim"] + XD, ["s_T2"])
                VTT("dve", S[lo], T1[lo], T2[lo], ALU.subtract, ["s_T1", "s_T2"], ["s_Slo"])
                VTT("dve", T1[hi], Ab(are, hi, 0, 15, 15), Zb(bbar[1], hi, 15), ALU.mult, ["s_are"] + XD, ["s_T1h"])
                VTT("pool", T2[hi], Ab(aim, hi, 0, 15, 15), Zb(bbar[0], hi, 15), ALU.mult, ["s_aim"] + XD, ["s_T2h"])
                VTT("dve", S[hi], T1[hi], T2[hi], ALU.add, ["s_T1h", "s_T2h"], ["s_Shi"])
                T1r, T2r = T1[:, :, 0:9, :], T2[:, :, 0:9, :]
                VTT("dve", T1r[lo], Ab(are, lo, 15, 24, 9), Zb(craw[0], lo, 9), ALU.mult, ["s_are", "s_cre", "s_Slo"], ["s_T1"])
                VTT("pool", T2r[lo], Ab(aim, lo, 15, 24, 9), Zb(craw[1], lo, 9), ALU.mult, ["s_aim", "s_cim", "s_Slo"], ["s_T2"])
                VTT("dve", R[lo], T1r[lo], T2r[lo], ALU.subtract, ["s_T1", "s_T2"], ["s_Rlo"])
                VTT("dve", T1r[hi], Ab(are, hi, 15, 24, 9), Zb(craw[1], hi, 9), ALU.mult, ["s_are", "s_cim", "s_Shi"], ["s_T1h"])
                VTT("pool", T2r[hi], Ab(aim, hi, 15, 24, 9), Zb(craw[0], hi, 9), ALU.mult, ["s_aim", "s_cre", "s_Shi"], ["s_T2h"])
                STT("dve", R[hi], T1r[hi], -1.0, T2r[hi], ALU.mult, ALU.subtract, ["s_T1h", "s_T2h"], ["s_Rhi"])
                TS("dve", aim[:, :, 24:33], aim[:, :, 24:33], sgn[:, 0:1], -1.0, ALU.mult, ALU.mult, ["s_aim", "s_sgn"], ["s_aims"])
                Sr, Rr = ["s_Slo", "s_Shi"], ["s_Rlo", "s_Rhi"]
                def s5_group(gg, par):
                    q = "%d" % par
                    usb_, xprev_, Tg_, Gg_, Hg_ = usb[par], xprev[par], Tg[par], Gg[par], Hg[par]
                    xp, ms_, mtmp_, mtmp2_, ttmp_ = xpad[par], ms[par], mtmp[par], mtmp2[par], ttmp[par]
                    Pm = S[:, gg, 7:15, :].rearrange("p i c -> p (i c)")
                    Gs = S[:, gg, 0:8, :].rearrange("p i c -> p (i c)")
                    Qm = R[:, gg, 0:8, :].rearrange("p j c -> p (j c)")
                    Hm = R[:, gg, 1:9, :].rearrange("p j c -> p (j c)")
                    pt, ptn = banks[0][:, 0:128], bn[0]
                    MM(pt, Pm, Qm, True, True, Sr + Rr, [ptn])
                    VTT("dve", ttmp_[:], pt, tmaskf[:], ALU.mult, [ptn, "s_tmask"], ["s_ttmp" + q])
                    yield
                    pu, pun = banks[1][:], bn[1]
                    for i in range(8):
                        MM(pu, sel[:, gg * 8 + i, :], utile[:, i::8], i == 0, i == 7, ["s_sel", "s_utile"], [pun])
                    CP("act", usb_[:], pu, [pun], ["s_usb" + q])
                    yield
                    CP("act", Hg_[:], Hm, Rr, ["s_Hg" + q])
                    STT("dve", Tg_[:], identf[:], dvc[:, gg:gg + 1], ttmp_[:], ALU.mult, ALU.add, ["identf", "s_dvc", "s_ttmp" + q], ["s_Tg" + q])
                    yield
                    pg_, pgn = banks[0][:, 128:256], bn[0]
                    MM(pg_, Gs, identf[:], True, True, Sr + ["identf"], [pgn])
                    CP("act", Gg_[:], pg_, [pgn], ["s_Gg" + q])
                    yield
                    pv, pvn = banks[2][:], bn[2]
                    MM(pv, Gg_[:], usb_[:], True, True, ["s_Gg" + q, "s_usb" + q], [pvn])
                    CP("dve", xp[0][:, 512:1024], pv, [pvn], ["s_xp0_" + q])
                    yield
                    cur = 0
                    pz, pzn = banks[3 + par][:], bn[3 + par]
                    for s_ in range(9):
                        sh = 2 ** s_
                        xc, xcn = xp[cur], "s_xp%d_%s" % (cur, q)
                        xn_, xnn = xp[1 - cur], "s_xp%d_%s" % (1 - cur, q)
                        MM(pz, jmat[:], xc[:, 512 - sh:1024 - sh], True, True, ["s_jmat", xcn, "s_xpz"], [pzn])
                        STT("dve", xn_[:, 512:1024], xc[:, 512 - sh:1024 - sh], are[:, gg, 24 + s_:25 + s_], xc[:, 512:1024],
                            ALU.mult, ALU.add, [xcn, "s_xpz", "s_are"], [xnn])
                        yield
                        STT("dve", xn_[:, 512:1024], pz, aim[:, gg, 24 + s_:25 + s_], xn_[:, 512:1024], ALU.mult, ALU.add,
                            [pzn, xnn, "s_aims", "s_aim"], [xnn])
                        yield
                        cur = 1 - cur
                    CP("act", xprev_[:], xp[cur][:, 511:1023], ["s_xp%d_%s" % (cur, q), "s_xpz"], ["s_xprev" + q])
                    yield
                    py, pyn = banks[5][:], bn[5]
                    MM(py, Tg_[:], usb_[:], True, False, ["s_Tg" + q, "s_usb" + q], [pyn])
                    MM(py, Hg_[:], xprev_[:], False, True, ["s_Hg" + q, "s_xprev" + q], [pyn])
                    ACT(yact[gg][:], py, AF.Gelu_apprx_tanh, [pyn], ["s_yact%d" % gg])
                    yield

                for gp in range(DBG["s5_ng"]):
                    interleave([s5_group(2 * gp, 0), s5_group(2 * gp + 1, 1)])
                if not DBG["s5_inv"]:
                    continue
                for j in range(8):
                    pj, pjn = banks[1 + j % 2][:], bn[1 + j % 2]
                    for gg in range(8):
                        MM(pj, selT[:, gg * 8 + j, :], yact[gg][:], gg == 0, gg == 7, ["s_selT", "s_yact%d" % gg], [pjn])
                    CP("dve" if j % 2 == 0 else "act", ytile[:, j::8], pj, [pjn], ["s_ytile%d" % j])
                DMA("sp", yT_d[bt * 128:(bt + 1) * 128, :], ytile[:], ["s_ytile%d" % j for j in range(8)], [("dram", "yT", bt)])

        def normrope(pfx, src, gain, cs, dst, T, H, tmp, r):
            sqv, xn, ssq, t1, t2 = tmp
            n = T * H
            v4 = lambda t, w: t[:, 0:n * w].rearrange("p (t h d) -> p t h d", t=T, h=H)
            sq4, xn4 = v4(sqv, 64), v4(xn, 64)
            VTT("dve", sq4, src, src, ALU.mult, r, [pfx + "sq"])
            yield
            P.op("dve", lambda e: e.tensor_reduce(out=ssq[:, 0:n], in_=sqv[:, 0:n * 64].rearrange("p (a d) -> p a d", d=64),
                                                  axis=AX.X, op=ALU.add), [pfx + "sq"], [pfx + "ssq"])
            yield
            TS("dve", ssq[:, 0:n], ssq[:, 0:n], 1.0 / 64, EPS, ALU.mult, ALU.add, [pfx + "ssq"], [pfx + "ssq"])
            yield
            ACT(ssq[:, 0:n], ssq[:, 0:n], AF.Sqrt, [pfx + "ssq"], [pfx + "ssq"])
            yield
            RECIP(ssq[:, 0:n], ssq[:, 0:n], [pfx + "ssq"], [pfx + "ssq"])
            yield
            rs4 = ssq[:, 0:n].rearrange("p (t h) -> p t h", t=T).unsqueeze(3).to_broadcast([128, T, H, 64])
            VTT("dve", xn4, src, rs4, ALU.mult, list(r) + [pfx + "ssq"], [pfx + "xn"])
            yield
            g4 = gain.unsqueeze(2).to_broadcast([128, T, H, 64])
            VTT("pool", xn4, xn4, g4, ALU.mult, [pfx + "xn", pfx + "gain"], [pfx + "xn"])
            yield
            c4 = cs[:, 0:32].unsqueeze(1).unsqueeze(1).to_broadcast([128, T, H, 32])
            s4 = cs[:, 32:64].unsqueeze(1).unsqueeze(1).to_broadcast([128, T, H, 32])
            x1, x2 = xn4[:, :, :, 0:32], xn4[:, :, :, 32:64]
            a4, b4 = v4(t1, 32), v4(t2, 32)
            VTT("dve", a4, x1, c4, ALU.mult, [pfx + "xn", pfx + "cs"], [pfx + "t1"])
            yield
            VTT("pool", b4, x2, s4, ALU.mult, [pfx + "xn", pfx + "cs"], [pfx + "t2"])
            yield
            VTT("dve", dst[:, :, :, 0:32], a4, b4, ALU.subtract, [pfx + "t1", pfx + "t2"], [pfx + "dstA"])
            yield
            VTT("dve", a4, x2, c4, ALU.mult, [pfx + "xn", pfx + "cs", pfx + "dstA"], [pfx + "t1"])
            yield
            VTT("pool", b4, x1, s4, ALU.mult, [pfx + "xn", pfx + "cs", pfx + "dstA"], [pfx + "t2"])
            yield
            VTT("dve", dst[:, :, :, 32:64], a4, b4, ALU.add, [pfx + "t1", pfx + "t2"], [pfx + "dstB"])
            yield

        def nsa(l):
            P.barrier(); FA.reset(); BA.reset()
            NB = SEQ // 128
            kselT = [BA.take([128, SEQ]) for _ in range(2)]
            kwinT = [BA.take([128, SEQ]) for _ in range(2)]
            vsel = BA.take([128, 2, NB, 65]); vwin = BA.take([128, 2, NB, 65])
            kcT = [BA.take([128, 256]) for _ in range(2)]
            vc = BA.take([128, 2, 2, 64])
            tri = BA.take([128, 128]); anti = BA.take([128, 128])
            tri4 = BA.take([128, 4, 128]); anti4 = BA.take([128, 4, 128])
            w2sb = BA.take([128, 2, 64])
            bf_mark = BA.off
            qg = FA.take([128, 1, 64]); kg = FA.take([128, 3, 64])
            maskc = FA.take([128, 512]); atab = FA.take([128, 128]); btab = FA.take([128, 128])
            bias = FA.take([128, 2])
            cs = [FA.take([128, 64]) for _ in range(2)]
            tmp = (FA.take([128, 768]), FA.take([128, 768]), FA.take([128, 16]), FA.take([128, 384]), FA.take([128, 384]))
            f_mark = FA.off
            for g in range(2):
                DMA("pool", kselT[g][64:128, :], D["erows"][:, :], (), ["n_erows%d" % g])
            MEMSET("pool", vsel[:, :, :, 64:65], 1.0, ["n_vsel1"])
            MEMSET("pool", vwin[:, :, :, 64:65], 1.0, ["n_vwin1"])
            DMA("pool", tri[:], D["tri"][:, :], (), ["n_tri"])
            DMA("pool", anti[:], D["anti"][:, :], (), ["n_anti"])
            CP("dve", tri4[:], tri[:].unsqueeze(1).to_broadcast([128, 4, 128]), ["n_tri"], ["n_tri4"])
            CP("dve", anti4[:], anti[:].unsqueeze(1).to_broadcast([128, 4, 128]), ["n_anti"], ["n_anti4"])
            DMA("sp", qg[:, 0, :], D["qgain"][:, l, :], (), ["n_qg"])
            DMA("sp", kg[:], D["kgain"][:, l, :, :], (), ["n_kg"])
            TS("dve", qg[:], qg[:], 0.125, None, ALU.mult, None, ["n_qg"], ["n_qg"])
            DMA("sp", maskc[:], D["maskc"][:, :], (), ["n_maskc"])
            DMA("sp", atab[:], D["atab"][:, :], (), ["n_atab"])
            DMA("sp", btab[:], D["btab"][:, :], (), ["n_btab"])
            DMA("pool", w2sb[:], D["cmp_w2"][l].rearrange("t h d -> h t d"), (), ["n_w2"])
            kcmpT = BA.take([128, SEQ]); vcmpT = BA.take([128, SEQ])
            w1sb = [BA.take([128, 32, 128]) for _ in range(2)]
            peT = BA.take([64, 2, 32])
            kb = BA.take([128, 2, 2, 64])
            cb = BA.take([128, 2, 128])
            hidT = BA.take([128, 256])
            kcn = BA.take([128, 1, 2, 64])
            kvrow = [FA.take([128, 768]) for _ in range(2)]
            kcraw = FA.take([128, 2, 2, 64])
            for ty in range(2):
                w1v = D["cmp_w1"][l, ty].rearrange("(l d) h -> d l h", d=64)
                DMA("pool", w1sb[ty][0:64], w1v, (), ["n_w1_%d_lo" % ty])
                DMA("pool", w1sb[ty][64:128], w1v, (), ["n_w1_%d_hi" % ty])
            DMA("pool", peT[0:64], D["peT"][:, l, :, :], (), ["n_peT"])
            for ty in range(2):
                pb, pn = banks[0][:, ty:ty + 1], bn[0]
                for ll in range(32):
                    MM(pb, w1sb[ty][0:64, ll, :], peT[0:64, ty, ll:ll + 1], ll == 0, ll == 31, ["n_w1_%d_lo" % ty, "n_peT"], [pn])
                CP("dve", bias[:, ty:ty + 1], pb, [pn], ["n_bias%d" % ty])
            pq = pbank[:, 0:512]
            tmp2 = (FA.take([128, 768]), FA.take([128, 768]), FA.take([128, 16]), FA.take([128, 384]), FA.take([128, 384]))
            kb2 = [kb, BA.take([128, 2, 2, 64])]
            cb2 = [cb, BA.take([128, 2, 128])]

            def kv_block(tb, par):
                kr, krn = kvrow[par], "n_kvrow%d" % par
                c_, cn = cs3k[par], "n_csk%d" % par
                pfx = "n_k%d_" % par
                kb_, cb_, tmp_ = kb2[par % 2] if par < 2 else kb3, cb2[par % 2] if par < 2 else cb3, (tmp, tmp2, tmp3)[par]
                DMA("sp", kr[:], qkv_d[tb * 128:(tb + 1) * 128, 512:1280], [("dram", "qkv", tb)], [krn])
                DMA("sp", c_[:], D["cs_tok"][tb * 128:(tb + 1) * 128, :], (), [cn])
                yield
                src = kr[:, 256:768].rearrange("p (t x h d) -> p t x h d", t=2, x=2, h=2)[:, :, 0]
                P.buf[pfx + "gain"] = P.buf.get("n_kg", {"w": None, "r": {}})
                P.buf[pfx + "cs"] = P.buf.get(cn, {"w": None, "r": {}})
                vsrc = kr[:, 256:768].rearrange("p (t x h d) -> p t x h d", t=2, x=2, h=2)
                CP("pool", vsel[:, :, tb, 0:64], vsrc[:, 0, 1], [krn], ["n_vsel_%d" % tb])
                CP("pool", vwin[:, :, tb, 0:64], vsrc[:, 1, 1], [krn], ["n_vwin_%d" % tb])
                CP("pool", cb_[:], kr[:, 0:256].rearrange("p (t c) -> p t c", t=2), [krn], ["n_cb%d" % par])
                yield
                for _ in normrope(pfx, src, kg[:, 1:3, :], c_, kb_[:], 2, 2, tmp_, [krn]):
                    yield
                for ty in range(2):
                    po = pbank[:, 512 + ty * 128:512 + (ty + 1) * 128]
                    TR(po, cb_[:, ty, :], ["n_cb%d" % par], ["pbank"])
                    CP("act" if ty == 0 else "dve", (kcmpT if ty == 0 else vcmpT)[:, tb * 128:(tb + 1) * 128], po,
                       ["pbank"], ["n_cT%d_%d" % (ty, tb)])
                yield
                for ty in range(2):
                    for h in range(2):
                        po = pq[0:64, (ty * 2 + h) * 128:(ty * 2 + h + 1) * 128]
                        TR(po, kb_[:, ty, h, :], [pfx + "dstA", pfx + "dstB"], ["pbank"])
                        dstt = (kselT if ty == 0 else kwinT)[h]
                        CP("act" if h == 0 else "dve", dstt[0:64, tb * 128:(tb + 1) * 128], po, ["pbank"],
                           ["n_kT%d_%d_%d" % (ty, h, tb)])
                yield

            tmp3 = (FA.take([128, 768]), FA.take([128, 768]), FA.take([128, 16]), FA.take([128, 384]), FA.take([128, 384]))
            kb3 = BA.take([128, 2, 2, 64]); cb3 = BA.take([128, 2, 128])
            kvrow.append(FA.take([128, 768]))
            cs3k = [FA.take([128, 64]) for _ in range(3)]
            for t3 in range(0, NB, 3):
                interleave([kv_block(tb, tb - t3) for tb in range(t3, min(NB, t3 + 3))])
            MEMSET("dve", hidT[:, 255:256], 0.0, ["n_hid255"])
            for ty in range(2):
                xT_ = kcmpT if ty == 0 else vcmpT
                xr = ["n_cT%d_%d" % (ty, tb) for tb in range(NB)]
                for g in range(2):
                    ph, phn = banks[1][:, 0:255], bn[1]
                    hs = slice(64 * g, 64 * g + 64)
                    for ll in range(32):
                        MM(ph, w1sb[ty][hs, ll, :], xT_[hs, ll:ll + 16 * 254 + 1:16], ll == 0, ll == 31,
                           xr + ["n_w1_%d_%s" % (ty, "lo" if g == 0 else "hi")], [phn])
                    ACT(hidT[:, 0:255], ph, AF.Gelu_apprx_tanh, [phn, "n_bias%d" % ty], ["n_hidT"], bias=bias[:, ty:ty + 1])
                    for c in range(2):
                        po, pon = banks[2][:, c * 64:(c + 1) * 64], bn[2]
                        MM(po, hidT[:, c * 128:(c + 1) * 128], w2sb[:, ty, :], True, True, ["n_hidT", "n_hid255", "n_w2"], [pon])
                        if ty == 1:
                            CP("act", vc[:, g, c, :], po, [pon], ["n_vc%d_%d" % (g, c)])
                        else:
                            CP("act", kcraw[:, c, g, :], po, [pon], ["n_kcraw%d_%d" % (c, g)])
            for c in range(2):
                c_, cn = cs[c], "n_cs%d" % c
                DMA("sp", c_[:], D["cs_cmp"][c * 128:(c + 1) * 128, :], (), [cn])
                P.buf["n_c_gain"] = P.buf.get("n_kg", {"w": None, "r": {}})
                P.buf["n_c_cs"] = P.buf.get(cn, {"w": None, "r": {}})
                for _ in normrope("n_c_", kcraw[:, c:c + 1, :, :], kg[:, 0:1, :], c_, kcn[:], 1, 2, tmp,
                                  ["n_kcraw%d_%d" % (c, g) for g in range(2)]):
                    pass
                for g in range(2):
                    po = pq[0:64, g * 128:(g + 1) * 128]
                    TR(po, kcn[:, 0, g, :], ["n_c_dstA", "n_c_dstB"], ["pbank"])
                    CP("act", kcT[g][0:64, c * 128:(c + 1) * 128], po, ["pbank"], ["n_kcT%d_%d" % (g, c)])
            P.barrier()
            BA.off = bf_mark; FA.off = f_mark
            qaug = [[BA.take([128, 512]) for _ in range(2)] for _ in range(3)]
            PT = [BA.take([128, 512]) for _ in range(3)]
            qn = BA.take([128, 1, 8, 64])
            ident4 = BA.take([128, 4, 128])
            nm = BA.take([128, 2, 128])
            otm = [BA.take([128, 8, 64]) for _ in range(2)]
            ocsb = BA.take([128, 4, 128])
            maskc_bf = BA.take([128, 512])
            qrow = [FA.take([128, 512]) for _ in range(3)]
            cs3 = [FA.take([128, 64]) for _ in range(3)]
            grow = [FA.take([128, 24]) for _ in range(3)]
            gsig = [FA.take([128, 24]) for _ in range(3)]
            pun = [FA.take([128, 2, 256]) for _ in range(4)]
            psumh = FA.take([128, 2, 256])
            den = FA.take([128, 4, 2]); rden = FA.take([128, 4, 2]); cfcs = [FA.take([128, 8]) for _ in range(2)]; thr = FA.take([128, 2])
            CC = FA.take([128, 2, 32]); SS = FA.take([128, 2, 32]); sqv = FA.take([128, 512]); ssq = FA.take([128, 8])
            ta = FA.take([128, 8, 32]); tb = FA.take([128, 8, 32]); tc_ = FA.take([128, 8, 32]); td = FA.take([128, 8, 32])
            dd = FA.take([128, 8, 64])
            imp = FA.take([128, 2, 64]); prio = FA.take([128, 2, 64]); wk = FA.take([128, 2, 64])
            m1 = FA.take([128, 2, 64]); m2 = FA.take([128, 2, 64])
            mx = [FA.take([128, 2, 8]) for _ in range(2)]
            coef = FA.take([128, 2, 3, 4])
            ocs = [FA.take([128, 8, 64]) for _ in range(2)]
            tA = [FA.take([128, 4, 64]) for _ in range(2)]; tB = [FA.take([128, 4, 64]) for _ in range(2)]; tC = [FA.take([128, 4, 64]) for _ in range(2)]
            MEMSET("dve", nm[:, :, 0:64], 0.0, ["n_nm0"])
            CP("dve", maskc_bf[:], maskc[:], ["n_maskc"], ["n_maskcb"])
            CP("dve", ident4[:], ident_bf[:].unsqueeze(1).to_broadcast([128, 4, 128]), ["ident_bf"], ["n_ident4"])
            psc_b, psc_n = banks[0], bn[0]
            poc_b, poc_n = banks[1], bn[1]
            st_b = [(banks[2][:], bn[2]), (banks[3][:], bn[3])]
            pos_b, pos_n = banks[4], bn[4]
            pow_b, pow_n = banks[5], bn[5]
            pqA = pbank
            stepc = [0]

            def stageA_prep(qb):
                pa = qb % 3
                qr, qrn = qrow[pa], "a_qrow%d" % pa
                c_, cn = cs3[pa], "a_cs%d" % pa
                gr, grn, gs_, gsn = grow[pa], "a_grow%d" % pa, gsig[pa], "a_gsig%d" % pa
                DMA("sp", qr[:], qkv_d[qb * 128:(qb + 1) * 128, 0:512], [("dram", "qkv", qb)], [qrn])
                DMA("sp", gr[:], qkv_d[qb * 128:(qb + 1) * 128, 1280:1304], [("dram", "qkv", qb)], [grn])
                DMA("sp", c_[:], D["cs_tok"][qb * 128:(qb + 1) * 128, :], (), [cn])
                yield
                yield
                q3 = qr[:].rearrange("p (h d) -> p h d", h=8)
                x1, x2 = q3[:, :, 0:32], q3[:, :, 32:64]
                g2 = qg[:, 0, :].rearrange("p (a d) -> p a d", a=2)
                VTT("pool", CC[:], c_[:, 0:32].unsqueeze(1).to_broadcast([128, 2, 32]), g2, ALU.mult, [cn, "n_qg"], ["a_CC"])
                VTT("pool", SS[:], c_[:, 32:64].unsqueeze(1).to_broadcast([128, 2, 32]), g2, ALU.mult, [cn, "n_qg"], ["a_SS"])
                VTT("dve", sqv[:], qr[:], qr[:], ALU.mult, [qrn], ["a_sqv"])
                bc8 = lambda t: t.unsqueeze(1).to_broadcast([128, 8, 32])
                VTT("pool", ta[:], x1, bc8(CC[:, 0, :]), ALU.mult, [qrn, "a_CC"], ["a_ta"])
                yield
                P.op("dve", lambda e: e.tensor_reduce(out=ssq[:], in_=sqv[:].rearrange("p (h d) -> p h d", h=8), axis=AX.X, op=ALU.add),
                     ["a_sqv"], ["a_ssq"])
                VTT("pool", tb[:], x2, bc8(SS[:, 1, :]), ALU.mult, [qrn, "a_SS"], ["a_tb"])
                VTT("pool", tc_[:], x2, bc8(CC[:, 1, :]), ALU.mult, [qrn, "a_CC"], ["a_tc"])
                yield
                TS("dve", ssq[:], ssq[:], 1.0 / 64, EPS, ALU.mult, ALU.add, ["a_ssq"], ["a_ssq"])
                VTT("pool", td[:], x1, bc8(SS[:, 0, :]), ALU.mult, [qrn, "a_SS"], ["a_td"])
                VTT("pool", dd[:, :, 0:32], ta[:], tb[:], ALU.subtract, ["a_ta", "a_tb"], ["a_dd1"])
                yield
                ACT(gs_[:], gr[:], AF.Sigmoid, [grn], [gsn])
                yield
                ACT(ssq[:], ssq[:], AF.Sqrt, ["a_ssq"], ["a_ssq"])
                VTT("pool", dd[:, :, 32:64], tc_[:], td[:], ALU.add, ["a_tc", "a_td"], ["a_dd2"])
                yield
                RECIP(ssq[:], ssq[:], ["a_ssq"], ["a_ssq"])
                yield
                VTT("dve", qn[:, 0], dd[:], ssq[:].unsqueeze(2).to_broadcast([128, 8, 64]), ALU.mult, ["a_dd1", "a_dd2", "a_ssq"], ["a_qn"])
                yield
                for hh in range(8):
                    TR(pqA[0:64, hh * 128:(hh + 1) * 128], qn[:, 0, hh, :], ["a_qn"], ["pbank"])
                for g in range(2):
                    CP("dve", qaug[pa][g][0:64, :], pqA[0:64, g * 512:(g + 1) * 512], ["pbank"], ["a_qaug%d_%d_q" % (pa, g)])
                yield

            def stageA_rest(qb):
                pa = qb % 3
                gs_, gsn = gsig[pa], "a_gsig%d" % pa
                nch = 1 if qb <= 15 else 2
                ncol = 128 * nch
                o0 = OFFC - 8 * qb
                poc = poc_b[:, 0:512].rearrange("p (h d) -> p h d", h=8)
                MEMSET("pool", den[:], 0.0, ["a_den"])
                if ncol < 256:
                    MEMSET("pool", psumh[:, :, ncol:256], 0.0, ["a_psumhz"])
                slots = [(g, hp) for hp in range(2) for g in range(2)]
                for (g, hp) in slots:
                    si = 2 * g + hp
                    psc = psc_b[:, 0:2 * ncol].rearrange("p (h n) -> p h n", h=2)
                    for hl in range(2):
                        h = 2 * hp + hl
                        MM(psc[:, hl, :], qaug[pa][g][0:64, h * 128:(h + 1) * 128], kcT[g][0:64, 0:ncol], True, False,
                           ["a_qaug%d_%d_q" % (pa, g)] + ["n_kcT%d_%d" % (g, c) for c in range(2)], [psc_n])
                        MM(psc[:, hl, :], ident_bf[:], maskc_bf[:, o0:o0 + ncol], False, True, ["ident_bf", "n_maskcb"], [psc_n])
                    yield
                    for hl in range(2):
                        ACT(pun[si][:, hl, 0:ncol], psc[:, hl, :], AF.Exp, [psc_n, "a_den"], ["a_pun%d_%d" % (si, hl), "a_den%d_%d" % (si, hl)],
                            accum_out=den[:, si, hl:hl + 1])
                    yield
                dn = ["a_den%d_%d" % (si, hl) for si in range(4) for hl in range(2)]
                if qb == 0:
                    TS("dve", den[:], den[:], 1e-30, None, ALU.max, None, dn + ["a_den"], dn + ["a_den"])
                    yield
                RECIP(rden[:], den[:], dn + ["a_den"], ["a_rden"])
                yield
                g8 = gs_[:].rearrange("p (h b) -> p h b", b=3)
                VTT("pool", cfcs[qb % 2][:], g8[:, :, 0], rden[:].rearrange("p s h -> p (s h)"), ALU.mult, [gsn, "a_rden"], ["a_cfc%d" % (qb % 2)])
                for k in range(4):
                    for g in range(2):
                        si, hl = 2 * g + k // 2, k % 2
                        if k == 0:
                            TS("dve", psumh[:, g, 0:ncol], pun[si][:, hl, 0:ncol], rden[:, si, hl:hl + 1], None, ALU.mult, None,
                               ["a_pun%d_%d" % (si, hl), "a_rden"], ["a_psumh%d" % g])
                        else:
                            STT("dve", psumh[:, g, 0:ncol], pun[si][:, hl, 0:ncol], rden[:, si, hl:hl + 1], psumh[:, g, 0:ncol],
                                ALU.mult, ALU.add, ["a_pun%d_%d" % (si, hl), "a_rden", "a_psumh%d" % g], ["a_psumh%d" % g])
                    yield
                pr = ["a_psumh0", "a_psumh1", "a_psumhz"]
                p4 = psumh[:].rearrange("p g (j r) -> p g j r", r=4)
                VTT("dve", imp[:], p4[:, :, :, 0], p4[:, :, :, 1], ALU.add, pr, ["a_imp"])
                STT("dve", wk[:], p4[:, :, :, 3], 0.5, p4[:, :, :, 2], ALU.mult, ALU.add, pr, ["a_wk"])
                yield
                VTT("dve", imp[:], imp[:], wk[:], ALU.add, ["a_imp", "a_wk"], ["a_imp"])
                yield
                STT("dve", imp[:, :, 1:64], p4[:, :, 0:63, 3], 0.5, imp[:, :, 1:64], ALU.mult, ALU.add, pr + ["a_imp"], ["a_imp"])
                yield
                v0 = OFFV - 2 * qb
                VTT("dve", prio[:], imp[:], atab[:, v0:v0 + 64].unsqueeze(1).to_broadcast([128, 2, 64]), ALU.mult, ["a_imp", "n_atab"], ["a_prio"])
                yield
                VTT("dve", prio[:], prio[:], btab[:, v0:v0 + 64].unsqueeze(1).to_broadcast([128, 2, 64]), ALU.add, ["a_prio", "n_btab"], ["a_prio"])
                yield
                MEMSET("dve", prio[:, :, 0:1], 2e4, ["a_prio"])
                yield
                for g in range(2):
                    P.op("dve", lambda e, g=g: e.max(out=mx[0][:, g, :], in_=prio[:, g, :]), ["a_prio"], ["a_mx0_%d" % g])
                yield
                for g in range(2):
                    P.op("dve", lambda e, g=g: e.match_replace(out=wk[:, g, :], in_to_replace=mx[0][:, g, :], in_values=prio[:, g, :],
                                                               imm_value=-1e9), ["a_prio", "a_mx0_%d" % g, "a_wk"], ["a_wk%d" % g])
                yield
                for g in range(2):
                    P.op("dve", lambda e, g=g: e.max(out=mx[1][:, g, :], in_=wk[:, g, :]), ["a_wk%d" % g], ["a_mx1_%d" % g])
                yield
                TS("dve", thr[:], mx[1][:, :, 7], -0.5, None, ALU.max, None, ["a_mx1_0", "a_mx1_1"], ["a_thr"])
                yield
                for g in range(2):
                    TS("dve", nm[:, g, 64:128], prio[:, g, :], thr[:, g:g + 1], NEG, ALU.is_lt, ALU.mult, ["a_prio", "a_thr"], ["a_nm%d" % g])
                yield
                for g in range(2):
                    pnm = pqA[:, g * 128:(g + 1) * 128]
                    TR(pnm, nm[:, g, :], ["a_nm%d" % g, "n_nm0"], ["pbank"])
                for g in range(2):
                    pnm = pqA[:, g * 128:(g + 1) * 128]
                    CP("dve", qaug[pa][g][64:128, :].rearrange("p (h q) -> p h q", h=4),
                       pnm[64:128, :].unsqueeze(1).to_broadcast([64, 4, 128]), ["pbank"], ["a_qaug%d_%d_m" % (pa, g)])
                yield

            def alloc_step():
                k = stepc[0]
                stepc[0] += 1
                return st_b[k % 2], (PT[k % 3], "a_PT%d" % (k % 3))

            def pathII_steps(qb):
                pa = qb % 3
                nch = 1 if qb <= 15 else 2
                o0 = OFFC - 8 * qb
                poc = poc_b[:, 0:512].rearrange("p (h d) -> p h d", h=8)
                steps = []
                state = {"first": True}
                for c in range(nch):
                    for g in range(2):
                        def mk(c=c, g=g):
                            (pst, pstn), (pt_, ptn_) = alloc_step()

                            def s_():
                                MM(pst, kcT[g][0:64, c * 128:(c + 1) * 128], qaug[pa][g][0:64, :], True, False,
                                   ["a_qaug%d_%d_q" % (pa, g), "n_kcT%d_%d" % (g, c)], [pstn])
                                MM(pst, maskc_bf[:, o0 + c * 128:o0 + (c + 1) * 128], ident4[:].rearrange("p h q -> p (h q)"), False, True,
                                   ["n_maskcb", "n_ident4"], [pstn])

                            def e_():
                                ACT(pt_[:], pst, AF.Exp, [pstn], [ptn_])

                            def p_():
                                for h in range(4):
                                    MM(poc[:, 4 * g + h, :], pt_[:, h * 128:(h + 1) * 128], vc[:, g, c, :], state["first"], c == nch - 1,
                                       [ptn_, "n_vc%d_%d" % (g, c)], [poc_n], sgc=True)
                                    state["first"] = False
                            return {"s": s_, "e": e_, "p": p_}
                        steps.append(mk)
                return steps

            def B_steps(qb):
                pa = qb % 3
                p2 = qb % 2
                gs_, gsn = gsig[pa], "a_gsig%d" % pa
                g4 = gs_[:].rearrange("p (g h b) -> p g h b", g=2, h=4)
                steps = []
                for g in range(2):
                    qa = qaug[pa][g]
                    qa_r = ["a_qaug%d_%d_q" % (pa, g), "a_qaug%d_%d_m" % (pa, g)]
                    pos_ = pos_b[:, 0:260].rearrange("p (h d) -> p h d", h=4)
                    pow_ = pow_b[:, 0:260].rearrange("p (h d) -> p h d", h=4)
                    state = {"fs": True, "fw": True}
                    for c in range(qb + 1):
                        def mk(c=c, g=g, qa=qa, qa_r=qa_r, pos_=pos_, state=state):
                            (pst, pstn), (pt_, ptn_) = alloc_step()
                            diag = (c == qb)

                            def s_():
                                MM(pst, kselT[g][:, c * 128:(c + 1) * 128], qa[:, :], True, not diag,
                                   qa_r + ["n_kT0_%d_%d" % (g, c), "n_erows%d" % g], [pstn])
                                if diag:
                                    MM(pst, ident_bf[:], tri4[:].rearrange("p h q -> p (h q)"), False, True, ["ident_bf", "n_tri4"], [pstn])

                            def e_():
                                ACT(pt_[:], pst, AF.Exp, [pstn], [ptn_])

                            def p_():
                                for h in range(4):
                                    MM(pos_[:, h, :], pt_[:, h * 128:(h + 1) * 128], vsel[:, g, c, :], state["fs"], c == qb,
                                       [ptn_, "n_vsel_%d" % c, "n_vsel1"], [pos_n], sgc=True)
                                    state["fs"] = False
                            return {"s": s_, "e": e_, "p": p_}
                        steps.append(mk)
                    c0 = max(0, qb - 4)
                    for c in range(c0, qb + 1):
                        def mk(c=c, g=g, qa=qa, qa_r=qa_r, pow_=pow_, pos_=pos_, state=state, last=(c == qb)):
                            (pst, pstn), (pt_, ptn_) = alloc_step()
                            special = (c == qb) or (c == qb - 4)

                            def s_():
                                MM(pst, kwinT[g][0:64, c * 128:(c + 1) * 128], qa[0:64, :], True, not special,
                                   [qa_r[0], "n_kT1_%d_%d" % (g, c)], [pstn])
                                if special:
                                    mk_, mkn = (tri4, "n_tri4") if c == qb else (anti4, "n_anti4")
                                    MM(pst, ident_bf[:], mk_[:].rearrange("p h q -> p (h q)"), False, True, ["ident_bf", mkn], [pstn])

                            def e_():
                                ACT(pt_[:], pst, AF.Exp, [pstn], [ptn_])

                            def p_():
                                for h in range(4):
                                    MM(pow_[:, h, :], pt_[:, h * 128:(h + 1) * 128], vwin[:, g, c, :], state["fw"], c == qb,
                                       [ptn_, "n_vwin_%d" % c, "n_vwin1"], [pow_n], sgc=True)
                                    state["fw"] = False

                            def post():
                                cf = coef[:, g]
                                RECIP(cf[:, 1, :], pos_[:, :, 64], [pos_n], ["a_cf1_%d" % g])
                                RECIP(cf[:, 2, :], pow_[:, :, 64], [pow_n], ["a_cf2_%d" % g])
                                yield
                                VTT("dve", cf[:, 1, :], cf[:, 1, :], g4[:, g, :, 1], ALU.mult, ["a_cf1_%d" % g, gsn], ["a_cf1_%d" % g])
                                VTT("dve", cf[:, 2, :], cf[:, 2, :], g4[:, g, :, 2], ALU.mult, ["a_cf2_%d" % g, gsn], ["a_cf2_%d" % g])
                                yield
                                VTT("dve", tA[g][:], pos_[:, :, 0:64], cf[:, 1, :].unsqueeze(2).to_broadcast([128, 4, 64]), ALU.mult,
                                    [pos_n, "a_cf1_%d" % g], ["a_tA%d" % g])
                                VTT("dve", tB[g][:], pow_[:, :, 0:64], cf[:, 2, :].unsqueeze(2).to_broadcast([128, 4, 64]), ALU.mult,
                                    [pow_n, "a_cf2_%d" % g], ["a_tB%d" % g])
                                yield
                                VTT("pool", tC[g][:], tA[g][:], ocs[p2][:, 4 * g:4 * g + 4, :], ALU.add, ["a_tA%d" % g, "a_ocs%d" % p2], ["a_tC%d" % g])
                                yield
                                VTT("pool", otm[p2][:, 4 * g:4 * g + 4, :], tC[g][:], tB[g][:], ALU.add, ["a_tC%d" % g, "a_tB%d" % g],
                                    ["a_otm%d_%d" % (p2, g)])
                                yield
                            d = {"s": s_, "e": e_, "p": p_}
                            if last:
                                d["post"] = post
                            return d
                        steps.append(mk)
                return steps

            def run_steps(mks):
                if not mks:
                    return
                cur_ = mks[0]()
                cur_["s"]()
                for i in range(len(mks)):
                    nxt = None
                    if i + 1 < len(mks):
                        nxt = mks[i + 1]()
                        nxt["s"]()
                    cur_["e"]()
                    cur_["p"]()
                    if "post" in cur_:
                        for _ in cur_["post"]():
                            yield
                    yield
                    cur_ = nxt

            def stageB(qb, nxt_qb):
                pa = qb % 2
                poc = poc_b[:, 0:512].rearrange("p (h d) -> p h d", h=8)
                VTT("dve", ocs[pa][:], poc, cfcs[pa][:].unsqueeze(2).to_broadcast([128, 8, 64]), ALU.mult, [poc_n, "a_cfc%d" % pa], ["a_ocs%d" % pa])
                yield
                mks = B_steps(qb)
                if nxt_qb is not None:
                    mks = mks + pathII_steps(nxt_qb)
                for _ in run_steps(mks):
                    yield
                for ct in range(4):
                    TR(pbankB[:, ct * 128:(ct + 1) * 128], otm[pa][:, 2 * ct:2 * ct + 2, :].rearrange("p h d -> p (h d)"),
                       ["a_otm%d_0" % pa, "a_otm%d_1" % pa], ["pbankB"])
                CP("act", ocsb[:].rearrange("p c q -> p (c q)"), pbankB[:, 0:512], ["pbankB"], ["a_ocsb"])
                DMA("sp", ocT_d.rearrange("(ct p) s -> p ct s", p=128)[:, :, qb * 128:(qb + 1) * 128], ocsb[:], ["a_ocsb"],
                    [("dram", "ocT", qb)])
                yield

            nqb = min(NB, DBG["nqb"])
            if nqb > 0:
                interleave([stageA_prep(0)])
                interleave([stageA_rest(0), stageA_prep(1) if nqb > 1 else None])
                interleave([run_steps(pathII_steps(0))])
            for qb in range(nqb):
                nx = qb + 1 if qb + 1 < nqb else None
                interleave([stageB(qb, nx), stageA_rest(qb + 1) if nx is not None else None,
                            stageA_prep(qb + 2) if qb + 2 < nqb else None])

        def mixer_out(l, xr, xname):
            P.barrier(); FA.reset(); BA.reset()
            gain = gains["norm_mixT"]
            xt = [FA.take([128, NFT, TT]) for _ in range(2)]
            rstd = FA.take([128, TT])
            sga = [FA.take([128, TT]) for _ in range(2)]; sgb = [FA.take([128, TT]) for _ in range(2)]
            sgg = [FA.take([128, TT]) for _ in range(2)]; ya = [FA.take([128, TT]) for _ in range(2)]; yb = [FA.take([128, TT]) for _ in range(2)]
            sq = BA.take([128, NFT, TT]); hT2 = [BA.take([128, NFT, TT]) for _ in range(2)]
            wg = BA.take([128, NFT, 2048]); wout = BA.take([128, NFT, D_MODEL])
            wglu = [BA.take([128, 4, 256]) for _ in range(3)]; wup = [BA.take([128, 4, 128]) for _ in range(3)]
            yt_ = BA.take([128, 4, TT]); oct_ = BA.take([128, 4, TT]); mg = BA.take([128, NFT, TT])
            wv = D["w_in"][l].rearrange("(kt p) c -> p kt c", p=128)
            wov = D["w_out"][l].rearrange("(kt p) c -> p kt c", p=128)
            wgv = D["ssm_w_glu"][l].rearrange("(kt p) c -> p kt c", p=128)
            wuv = D["nsa_w_up"][l].rearrange("(kt p) c -> p kt c", p=128)
            ntile = SEQ // TT

            def load_norm(tt):
                tok0 = tt * TT
                xb, xn = xt[tt % 2], "o_xt%d" % (tt % 2)
                DMA("sp", xb[:], xview(xr)[:, :, tok0:tok0 + TT], [dtile(xname, o, tok0) for o in range(NFT)], [xn])
                norm_tile("o_", xb, xn, gain[:, l, :], "g_norm_mixT", hT2[tt % 2], "o_hT%d" % (tt % 2), sq, rstd, banks[0][:], bn[0])

            load_norm(0)
            for c in range(NFT):
                DMA("pool", wg[:, :, c * 128:(c + 1) * 128], wv[:, :, GA0 + c * 128:GA0 + (c + 1) * 128], (), ["o_wga%d" % c])
                DMA("pool", wg[:, :, 1024 + c * 128:1024 + (c + 1) * 128], wv[:, :, GB0 + c * 128:GB0 + (c + 1) * 128], (), ["o_wgb%d" % c])
            for kt in range(NFT):
                DMA("pool", wout[:, kt, :], wov[:, kt, :], (), ["o_wout%d" % kt])
            wor = ["o_wout%d" % kt for kt in range(NFT)]
            for tt in range(ntile):
                tok0 = tt * TT
                xb, xn = xt[tt % 2], "o_xt%d" % (tt % 2)
                hT = hT2[tt % 2]
                hr = ["o_hT%d_%d" % (tt % 2, ft) for ft in range(NFT)]
                DMA("sp", yt_[:], yT_d.rearrange("(kt p) s -> p kt s", p=128)[:, :, tok0:tok0 + TT],
                    [("dram", "yT", bt) for bt in range(4)], ["o_yt"])
                DMA("sp", oct_[:], ocT_d.rearrange("(kt p) s -> p kt s", p=128)[:, :, tok0:tok0 + TT],
                    [("dram", "ocT", qb) for qb in range(tok0 // 128, tok0 // 128 + 4)], ["o_oct"])
                for c in range(NFT):
                    k = c % 2
                    k3 = (tt * NFT + c) % 3
                    wl, wln = wglu[k3], "o_wglu%d" % k3
                    wu_, wun = wup[k3], "o_wup%d" % k3
                    DMA("pool", wl[:, :, 0:128], wgv[:, :, c * 128:(c + 1) * 128], (), [wln + "v"])
                    DMA("pool", wl[:, :, 128:256], wgv[:, :, 1024 + c * 128:1024 + (c + 1) * 128], (), [wln + "g"])
                    DMA("pool", wu_[:], wuv[:, :, c * 128:(c + 1) * 128], (), [wun])
                    pga, pgb, pv_, pgt, pyb = banks[1][:], banks[2][:], banks[3][:], banks[4][:], banks[5][:]
                    for kt in range(NFT):
                        MM(pga, wg[:, kt, c * 128:(c + 1) * 128], hT[:, kt, :], kt == 0, kt == NFT - 1, hr + ["o_wga%d" % c], [bn[1]])
                    for kt in range(4):
                        MM(pgt, wl[:, kt, 128:256], yt_[:, kt, :], kt == 0, kt == 3, [wln + "g", "o_yt"], [bn[4]])
                    for kt in range(4):
                        MM(pv_, wl[:, kt, 0:128], yt_[:, kt, :], kt == 0, kt == 3, [wln + "v", "o_yt"], [bn[3]])
                    for kt in range(NFT):
                        MM(pgb, wg[:, kt, 1024 + c * 128:1024 + (c + 1) * 128], hT[:, kt, :], kt == 0, kt == NFT - 1, hr + ["o_wgb%d" % c], [bn[2]])
                    for kt in range(4):
                        MM(pyb, wu_[:, kt, :], oct_[:, kt, :], kt == 0, kt == 3, [wun, "o_oct"], [bn[5]])
                    ks = "%d" % k
                    ACT(sga[k][:], pga, AF.Sigmoid, [bn[1]], ["o_sga" + ks])
                    ACT(sgg[k][:], pgt, AF.Sigmoid, [bn[4]], ["o_sgg" + ks])
                    ACT(sgb[k][:], pgb, AF.Sigmoid, [bn[2]], ["o_sgb" + ks])
                    VTT("dve", ya[k][:], pv_, sgg[k][:], ALU.mult, [bn[3], "o_sgg" + ks], ["o_ya" + ks])
                    VTT("dve", yb[k][:], pyb, sgb[k][:], ALU.mult, [bn[5], "o_sgb" + ks], ["o_yb" + ks])
                    VTT("dve", ya[k][:], ya[k][:], sga[k][:], ALU.mult, ["o_ya" + ks, "o_sga" + ks], ["o_ya" + ks])
                    VTT("dve", mg[:, c, :], ya[k][:], yb[k][:], ALU.add, ["o_ya" + ks, "o_yb" + ks], ["o_mg%d" % c])
                    if c == 3 and tt + 1 < ntile:
                        load_norm(tt + 1)
                mgr = ["o_mg%d" % c for c in range(NFT)]
                for ot in range(NFT):
                    po, pon = banks[0][:], bn[0]
                    for c in range(NFT):
                        MM(po, wout[:, c, ot * 128:(ot + 1) * 128], mg[:, c, :], c == 0, c == NFT - 1, mgr + wor, [pon])
                    VTT("dve", xb[:, ot, :], po, xb[:, ot, :], ALU.add, [pon, xn] + hr, [xn])
                DMA("sp", xview(xr)[:, :, tok0:tok0 + TT], xb[:], [xn], [dtile(xname, o, tok0) for o in range(NFT)])

        cur, cname = xT_in, "xin"
        for l in range(depth):
            last = (l == depth - 1)
            if "ffn1" in stages:
                ffn(l, cur, cname, xres, "xres", "norm_ffn1T", D["ffn1_wi"], D["ffn1_wo"])
                cur, cname = xres, "xres"
            if "mix" in stages:
                if "proj" in mix_parts:
                    mixer_proj(l, cur, cname)
                if "s5" in mix_parts:
                    s5(l)
                if "nsa" in mix_parts:
                    nsa(l)
                if "out" in mix_parts:
                    mixer_out(l, xres, "xres")
            if "ffn2" in stages:
                dst, dn = (outT, "out") if last else (xres, "xres")
                ffn(l, cur, cname, dst, dn, "norm_ffn2T", D["ffn2_wi"], D["ffn2_wo"])
                cur, cname = xres, "xres"
        final = [(P.dsem[i], 16 * P.dcnt[i]) for i in range(P.NDMA)]
        P.finish(final)
        print("instructions:", P.ninst, "sems:", P.nsem, flush=True)
    return nc


_CACHE = {}


def kernel(**inputs):
    x = np.asarray(inputs["x"], dtype=np.float32)
    if "nc" not in _CACHE:
        _CACHE["nc"] = build_program()
    nc = _CACHE["nc"]
    shared = layout_params(inputs)
    shared.update(host_constants())
    in_maps = []
    ncores = DBG["ncores"]
    for c in range(ncores):
        m = dict(shared)
        m["xT"] = np.ascontiguousarray(x[c // 2].T)
        in_maps.append(m)
    res = run_bass_kernel_spmd(nc, in_maps, core_ids=list(range(ncores)))
    _CACHE["last"] = res
    out = np.stack([np.ascontiguousarray(res.results[(2 * b) % ncores]["outT"].T) for b in range(BATCH)], axis=0)
    return out.astype(np.float32)
```
